# Optimizing a Trainium2 kernel written in Bass

```python
import jax, jax.numpy as jnp
from jax import lax
import numpy as np

D_MODEL = 1024
BATCH = 8
SEQ = 2048
DEPTH = 2

GRID_W = 64
CTX_LEN = 256
N_MIXERS = 2
N_ATTN_LAYERS = (DEPTH + N_MIXERS - 1) // N_MIXERS
N_LRU_LAYERS = DEPTH // N_MIXERS
HEAD_DIM = 64
N_HEADS = D_MODEL // HEAD_DIM
N_KV_HEADS = 4
GQA_GROUP = N_HEADS // N_KV_HEADS
WINDOW = 128
BLOCK = 128
ROPE_BASE = 10000.0
D_RNN = 1280
LRU_BLOCK_W = 256
N_LRU_BLOCKS = D_RNN // LRU_BLOCK_W
CONV_W = 4
CONV_LEFT = 2
LRU_C = 8.0
D_FF = 4 * D_MODEL
N_MOD = 6
EPS = 1e-6
NEG_INF = -1e30

kernel_name = 'hybrid_swa_rglru_diffusion_block'


def rms_norm(x, g):
    xf = x.astype(jnp.float32)
    y = xf * lax.rsqrt(jnp.mean(xf * xf, axis=-1, keepdims=True) + EPS)
    return (y * g.astype(jnp.float32)).astype(x.dtype)


def modulate(h, shift, scale):
    return h * (1 + scale) + shift


def sqrelu_mlp(h, w1, w2):
    return jnp.square(jax.nn.relu(h @ w1)) @ w2


def axial_rope_tables(n):
    rows = n // GRID_W
    row = jnp.repeat(jnp.arange(rows, dtype=jnp.float32), GRID_W)
    col = jnp.tile(jnp.arange(GRID_W, dtype=jnp.float32), rows)
    half = HEAD_DIM // 2
    inv = ROPE_BASE ** (-jnp.arange(0, half, 2, dtype=jnp.float32) / half)
    ang_r = row[:, None] * inv[None, :]
    ang_c = col[:, None] * inv[None, :]
    ang = jnp.concatenate([ang_r, ang_r, ang_c, ang_c], axis=-1)
    return jnp.cos(ang), jnp.sin(ang)


def rotate_half_axial(x):
    a1, a2, b1, b2 = jnp.split(x, 4, axis=-1)
    return jnp.concatenate([-a2, a1, -b2, b1], axis=-1)


def apply_rope(x, cos, sin):
    xf = x.astype(jnp.float32)
    out = xf * cos[None, :, None, :] + rotate_half_axial(xf) * sin[None, :, None, :]
    return out.astype(x.dtype)


def banded_window_attention(q, k, v, k_ctx, v_ctx, sink):
    B, S = q.shape[0], q.shape[1]
    nb = S // BLOCK
    f32 = jnp.float32
    qb = (q.astype(f32) * (HEAD_DIM ** -0.5)).reshape(B, nb, BLOCK, N_KV_HEADS, GQA_GROUP, HEAD_DIM)

    def band(t):
        tp = jnp.pad(t.astype(f32), ((0, 0), (BLOCK, BLOCK), (0, 0), (0, 0)))
        tp = tp.reshape(B, nb + 2, BLOCK, N_KV_HEADS, HEAD_DIM)
        return jnp.concatenate([tp[:, :-2], tp[:, 1:-1], tp[:, 2:]], axis=2)

    kb, vb = band(k), band(v)
    blk = jnp.arange(nb)[:, None, None]
    qpos = blk * BLOCK + jnp.arange(BLOCK)[None, :, None]
    kpos = blk * BLOCK - BLOCK + jnp.arange(3 * BLOCK)[None, None, :]
    valid = (kpos >= 0) & (kpos < S) & (jnp.abs(kpos - qpos) <= WINDOW)
    kc = k_ctx.astype(f32)
    vc = v_ctx.astype(f32)
    n_ctx = kc.shape[1]
    n_loc = 3 * BLOCK
    sink_col = jnp.broadcast_to(sink.astype(f32).reshape(1, N_KV_HEADS, GQA_GROUP, 1, 1),
                                (B, N_KV_HEADS, GQA_GROUP, BLOCK, 1))

    def one_block(args):
        qi, ki, vi, mi = args
        s_loc = jnp.where(mi, jnp.einsum('bqkgd,bskd->bkgqs', qi, ki), NEG_INF)
        s_ctx = jnp.einsum('bqkgd,bckd->bkgqc', qi, kc)
        p = jax.nn.softmax(jnp.concatenate([s_loc, s_ctx, sink_col], axis=-1), axis=-1)
        return (jnp.einsum('bkgqs,bskd->bqkgd', p[..., :n_loc], vi)
                + jnp.einsum('bkgqc,bckd->bqkgd', p[..., n_loc:n_loc + n_ctx], vc))

    out = lax.map(one_block, (jnp.moveaxis(qb, 1, 0), jnp.moveaxis(kb, 1, 0),
                              jnp.moveaxis(vb, 1, 0), valid))
    out = jnp.moveaxis(out, 0, 1).reshape(B, S, N_HEADS * HEAD_DIM)
    return out.astype(q.dtype)


def context_attention(q_c, k_c, v_c, sink):
    B, C = q_c.shape[0], q_c.shape[1]
    f32 = jnp.float32
    qf = (q_c.astype(f32) * (HEAD_DIM ** -0.5)).reshape(B, C, N_KV_HEADS, GQA_GROUP, HEAD_DIM)
    s = jnp.einsum('bqkgd,bckd->bkgqc', qf, k_c.astype(f32))
    sink_col = jnp.broadcast_to(sink.astype(f32).reshape(1, N_KV_HEADS, GQA_GROUP, 1, 1),
                                (B, N_KV_HEADS, GQA_GROUP, C, 1))
    p = jax.nn.softmax(jnp.concatenate([s, sink_col], axis=-1), axis=-1)
    out = jnp.einsum('bkgqc,bckd->bqkgd', p[..., :C], v_c.astype(f32))
    return out.reshape(B, C, N_HEADS * HEAD_DIM).astype(q_c.dtype)


def attention_mixer(h_x, h_c, w_qkv, w_o, sink, cos, sin, need_ctx):
    nq = N_HEADS * HEAD_DIM
    w_q, w_kv = w_qkv[:, :nq], w_qkv[:, nq:]

    def kv(h):
        kvh = (h @ w_kv).reshape(h.shape[0], h.shape[1], 2, N_KV_HEADS, HEAD_DIM)
        return kvh[:, :, 0], kvh[:, :, 1]

    B, S = h_x.shape[0], h_x.shape[1]
    q_x = apply_rope((h_x @ w_q).reshape(B, S, N_HEADS, HEAD_DIM), cos, sin)
    k_x, v_x = kv(h_x)
    k_x = apply_rope(k_x, cos, sin)
    k_c, v_c = kv(h_c)
    y_x = banded_window_attention(q_x, k_x, v_x, k_c, v_c, sink) @ w_o
    y_c = None
    if need_ctx:
        q_c = (h_c @ w_q).reshape(h_c.shape[0], h_c.shape[1], N_HEADS, HEAD_DIM)
        y_c = context_attention(q_c, k_c, v_c, sink) @ w_o
    return y_x, y_c


def centred_dwconv(u, w, b):
    T = u.shape[1]
    up = jnp.pad(u, ((0, 0), (CONV_LEFT, CONV_W - 1 - CONV_LEFT), (0, 0)))
    out = b
    for tap in range(CONV_W):
        out = out + up[:, tap:tap + T] * w[tap]
    return out


def rglru_coeffs(u, w_a, b_a, w_i, b_i, lam):
    B, T, W = u.shape
    f32 = jnp.float32
    uf = u.astype(f32)
    ub = uf.reshape(B, T, N_LRU_BLOCKS, LRU_BLOCK_W)
    r = jax.nn.sigmoid(jnp.einsum('btnk,nkj->btnj', ub, w_a.astype(f32)).reshape(B, T, W) + b_a.astype(f32))
    i = jax.nn.sigmoid(jnp.einsum('btnk,nkj->btnj', ub, w_i.astype(f32)).reshape(B, T, W) + b_i.astype(f32))
    log_a = -LRU_C * jax.nn.softplus(-lam.astype(f32)) * r
    a = jnp.exp(log_a)
    bx = jnp.sqrt(-jnp.expm1(2 * log_a)) * (i * uf)
    return a, bx


def linear_scan(a, bx, h0, reverse, emit):
    def step(h, ab):
        a_t, b_t = ab
        h = a_t * h + b_t
        return h, (h if emit else None)

    h_last, ys = lax.scan(step, h0, (jnp.swapaxes(a, 0, 1), jnp.swapaxes(bx, 0, 1)), reverse=reverse)
    ys = jnp.swapaxes(ys, 0, 1) if emit else None
    return ys, h_last


def rglru_mixer(h_x, h_c, w_in, conv_w, conv_b, w_a, b_a, w_i, b_i, lam, w_out, need_ctx):
    w_gate, w_rec = w_in[:, :D_RNN], w_in[:, D_RNN:]
    u_x = centred_dwconv(h_x @ w_rec, conv_w, conv_b)
    u_c = centred_dwconv(h_c @ w_rec, conv_w, conv_b)
    B = h_x.shape[0]
    rec_x, rec_c = [], []
    for d in range(2):
        rev = d == 1
        a_c, bx_c = rglru_coeffs(u_c, w_a[d], b_a[d], w_i[d], b_i[d], lam[d])
        ys_c, hc_last = linear_scan(a_c, bx_c, jnp.zeros((B, D_RNN), jnp.float32), rev, need_ctx)
        a_x, bx_x = rglru_coeffs(u_x, w_a[d], b_a[d], w_i[d], b_i[d], lam[d])
        ys_x, _ = linear_scan(a_x, bx_x, hc_last, rev, True)
        rec_x.append(ys_x)
        rec_c.append(ys_c)
    y_x = (jax.nn.gelu(h_x @ w_gate) * (rec_x[0] + rec_x[1]).astype(h_x.dtype)) @ w_out
    y_c = None
    if need_ctx:
        y_c = (jax.nn.gelu(h_c @ w_gate) * (rec_c[0] + rec_c[1]).astype(h_c.dtype)) @ w_out
    return y_x, y_c


def setup_inputs(seed: int = 0) -> dict:
    key = jax.random.key(seed)
    ks = jax.random.split(key, 24)
    f32 = jnp.float32
    nrm = lambda k, shape, s: jax.random.normal(k, shape, f32) * s
    nqkv = (N_HEADS + 2 * N_KV_HEADS) * HEAD_DIM
    u = jax.random.uniform(ks[20], (N_LRU_LAYERS, 2, D_RNN), f32, 0.9, 0.999)
    s = u ** (1.0 / LRU_C)
    lam = jnp.log(s) - jnp.log1p(-s)
    return {
        'x': nrm(ks[0], (BATCH, SEQ, D_MODEL), 1.0),
        'c': nrm(ks[1], (BATCH, D_MODEL), 1.0),
        'ctx': nrm(ks[2], (BATCH, CTX_LEN, D_MODEL), 1.0),
        'c_ctx': nrm(ks[3], (D_MODEL,), 1.0),
        'ada_w': nrm(ks[4], (DEPTH, D_MODEL, N_MOD * D_MODEL), 0.5 * D_MODEL ** -0.5),
        'ada_b': nrm(ks[5], (DEPTH, N_MOD * D_MODEL), 0.01),
        'norm_g': 1.0 + nrm(ks[6], (DEPTH, 4, D_MODEL), 0.05),
        'mlp_w1': nrm(ks[7], (DEPTH, D_MODEL, D_FF), D_MODEL ** -0.5),
        'mlp_w2': nrm(ks[8], (DEPTH, D_FF, D_MODEL), D_FF ** -0.5),
        'attn_w_qkv': nrm(ks[9], (N_ATTN_LAYERS, D_MODEL, nqkv), D_MODEL ** -0.5),
        'attn_w_o': nrm(ks[10], (N_ATTN_LAYERS, N_HEADS * HEAD_DIM, D_MODEL), (N_HEADS * HEAD_DIM) ** -0.5),
        'attn_sink': nrm(ks[11], (N_ATTN_LAYERS, N_HEADS), 0.5),
        'lru_w_in': nrm(ks[12], (N_LRU_LAYERS, D_MODEL, 2 * D_RNN), D_MODEL ** -0.5),
        'lru_conv_w': nrm(ks[13], (N_LRU_LAYERS, CONV_W, D_RNN), CONV_W ** -0.5),
        'lru_conv_b': nrm(ks[14], (N_LRU_LAYERS, D_RNN), 0.01),
        'lru_w_a': nrm(ks[15], (N_LRU_LAYERS, 2, N_LRU_BLOCKS, LRU_BLOCK_W, LRU_BLOCK_W), LRU_BLOCK_W ** -0.5),
        'lru_b_a': nrm(ks[16], (N_LRU_LAYERS, 2, D_RNN), 0.01),
        'lru_w_i': nrm(ks[17], (N_LRU_LAYERS, 2, N_LRU_BLOCKS, LRU_BLOCK_W, LRU_BLOCK_W), LRU_BLOCK_W ** -0.5),
        'lru_b_i': nrm(ks[18], (N_LRU_LAYERS, 2, D_RNN), 0.01),
        'lru_lam': lam,
        'lru_w_out': nrm(ks[19], (N_LRU_LAYERS, D_RNN, D_MODEL), D_RNN ** -0.5),
    }


def reference(x, c, ctx, c_ctx, ada_w, ada_b, norm_g, mlp_w1, mlp_w2, attn_w_qkv, attn_w_o, attn_sink,
              lru_w_in, lru_conv_w, lru_conv_b, lru_w_a, lru_b_a, lru_w_i, lru_b_i, lru_lam, lru_w_out):
    n = x.shape[1]
    cos, sin = axial_rope_tables(n)
    silu_c = jax.nn.silu(c)
    silu_cc = jax.nn.silu(c_ctx)
    for i in range(DEPTH):
        last = i == DEPTH - 1
        mx = jnp.split((silu_c @ ada_w[i] + ada_b[i])[:, None, :], N_MOD, axis=-1)
        mc = jnp.split(silu_cc @ ada_w[i] + ada_b[i], N_MOD, axis=-1)
        g = norm_g[i]
        h_x = modulate(rms_norm(x, g[0]), mx[0], mx[1])
        h_c = modulate(rms_norm(ctx, g[0]), mc[0], mc[1])
        j = i // N_MIXERS
        if i % N_MIXERS == 0:
            y_x, y_c = attention_mixer(h_x, h_c, attn_w_qkv[j], attn_w_o[j], attn_sink[j], cos, sin, not last)
        else:
            y_x, y_c = rglru_mixer(h_x, h_c, lru_w_in[j], lru_conv_w[j], lru_conv_b[j], lru_w_a[j], lru_b_a[j],
                                   lru_w_i[j], lru_b_i[j], lru_lam[j], lru_w_out[j], not last)
        x = x + mx[2] * rms_norm(y_x, g[1])
        x = x + mx[5] * rms_norm(sqrelu_mlp(modulate(rms_norm(x, g[2]), mx[3], mx[4]), mlp_w1[i], mlp_w2[i]), g[3])
        if not last:
            ctx = ctx + mc[2] * rms_norm(y_c, g[1])
            ctx = ctx + mc[5] * rms_norm(sqrelu_mlp(modulate(rms_norm(ctx, g[2]), mc[3], mc[4]),
                                                    mlp_w1[i], mlp_w2[i]), g[3])
    return x
```

```python
import numpy as np
import ml_dtypes
import concourse.bass as bass
import concourse.mybir as mybir
from concourse.bass_utils import run_bass_kernel_spmd

F32 = mybir.dt.float32
BF16 = mybir.dt.bfloat16
AF = mybir.ActivationFunctionType
ALU = mybir.AluOpType

D = 1024
KC = 8
T = 2048
TCX = 256
TT = T + TCX
DFF = 4096
DR = 1280
EPS = 1e-6
GRAN = 256
ARENA_WORDS = 53000


def _prod(s):
    r = 1
    for v in s:
        r *= v
    return r


class Acc:
    __slots__ = ("ap", "keys")

    def __init__(self, ap, keys):
        self.ap = ap
        self.keys = keys


class Buf:
    def __init__(self, arena, name, off, shape, dt):
        self.name = name
        self.off = off
        self.shape = tuple(shape)
        self.dt = dt
        self.esz = 4 if dt == F32 else 2
        nel = _prod(shape)
        nbytes = nel * self.esz
        assert off % 4 == 0 and nbytes % 4 == 0
        w = arena[:, off // 4:(off + nbytes) // 4]
        if dt != F32:
            w = w.bitcast(dt)
        if len(shape) == 2:
            w = w.rearrange("p (a b) -> p a b", a=shape[0])
        elif len(shape) == 3:
            w = w.rearrange("p (a b c) -> p a b c", a=shape[0], b=shape[1])
        elif len(shape) == 4:
            w = w.rearrange("p (a b c d) -> p a b c d", a=shape[0], b=shape[1], c=shape[2])
        self.full = w
        st = []
        s = 1
        for d in reversed(self.shape):
            st.append(s)
            s *= d
        self.strides = tuple(reversed(st))
        self.nbytes = nbytes

    def __getitem__(self, idx):
        if not isinstance(idx, tuple):
            idx = (idx,)
        ap = self.full[idx]
        fidx = list(idx[1:])
        while len(fidx) < len(self.shape):
            fidx.append(slice(None))
        lohi = []
        for d, ix in enumerate(fidx):
            n = self.shape[d]
            if isinstance(ix, int):
                lohi.append((ix, ix))
            else:
                a, b, c = ix.indices(n)
                if c > 0:
                    last = a + ((b - a - 1) // c) * c
                    lohi.append((a, last))
                else:
                    last = a + ((a - b - 1) // (-c)) * c
                    lohi.append((last, a))
        ranges = [(0, 0)]
        outer = lohi[:-1]
        ncomb = _prod([h - l + 1 for l, h in outer]) if outer else 1
        keys = set()
        if ncomb > 256:
            lo = sum(l * s for (l, h), s in zip(lohi, self.strides))
            hi = sum(h * s for (l, h), s in zip(lohi, self.strides))
            b0 = self.off + lo * self.esz
            b1 = self.off + (hi + 1) * self.esz
            keys.update(range(b0 // GRAN, (b1 - 1) // GRAN + 1))
        else:
            def rec(d, base):
                if d == len(lohi) - 1:
                    l, h = lohi[d]
                    b0 = self.off + (base + l) * self.esz
                    b1 = self.off + (base + h + 1) * self.esz
                    keys.update(range(b0 // GRAN, (b1 - 1) // GRAN + 1))
                    return
                l, h = lohi[d]
                for i in range(l, h + 1):
                    rec(d + 1, base + i * self.strides[d])
            rec(0, 0)
        return Acc(ap, keys)


class Prog:
    def __init__(self, nc, stack):
        self.nc = nc
        self.stack = stack
        self.eng = {"pe": nc.tensor, "act": nc.scalar, "dve": nc.vector, "pool": nc.gpsimd, "sp": nc.sync}
        self.sem = {}
        self.cnt = {}
        self.waited = {e: {} for e in self.eng}
        self.lastw = {}
        self.readers = {}
        for e in ("pe", "act", "dve", "pool"):
            self._mksem(e)
        self.arena = stack.enter_context(nc.sbuf_tensor("arena", [128, ARENA_WORDS], F32))
        self.top = 0
        self.tops = []
        self.psum = [stack.enter_context(nc.psum_tensor("ps%d" % i, [128, 512], F32)) for i in range(8)]
        self.rr = 0

    def _mksem(self, name):
        self.sem[name] = self.stack.enter_context(self.nc.semaphore(name.replace(":", "_")))
        self.cnt[name] = 0

    def stream(self, name):
        n = "dma:" + name
        if n not in self.sem:
            self._mksem(n)
        return n

    def alloc(self, name, shape, dt):
        esz = 4 if dt == F32 else 2
        nbytes = _prod(shape) * esz
        nbytes = (nbytes + GRAN - 1) // GRAN * GRAN
        off = self.top
        self.top += nbytes
        assert self.top <= ARENA_WORDS * 4, ("arena overflow", name, self.top)
        self.tops.append((self.top, name))
        return Buf(self.arena, name, off, shape, dt)

    def mark(self):
        return self.top

    def release(self, m):
        self.top = m

    def ps(self, bank, cols=512, rows=128, c0=0, r0=0):
        ap = self.psum[bank][r0:r0 + rows, c0:c0 + cols]
        return Acc(ap, {("ps", bank)})

    def ps_rot(self):
        b = self.rr
        self.rr = (self.rr + 1) % 3
        return b

    def _need(self, eng, reads, writes):
        need = {}

        def add(t, kind):
            if t is None:
                return
            s, v = t
            if s == eng and eng in ("pe", "sp"):
                return
            if need.get(s, 0) < v:
                need[s] = v
        for a in reads:
            for k in a.keys:
                add(self.lastw.get(k), "raw")
        for a in writes:
            for k in a.keys:
                add(self.lastw.get(k), "waw")
                r = self.readers.get(k)
                if r:
                    for s, v in r.items():
                        add((s, v), "war")
        return need

    def _wait(self, eng, need):
        w = self.waited[eng]
        for s, v in need.items():
            if w.get(s, 0) < v:
                self.eng[eng].wait_ge(self.sem[s], v)
                w[s] = v

    def _record(self, t, reads, writes):
        s, v = t
        for a in reads:
            for k in a.keys:
                r = self.readers.setdefault(k, {})
                if r.get(s, 0) < v:
                    r[s] = v
        for a in writes:
            for k in a.keys:
                self.lastw[k] = t
                self.readers[k] = {}

    def op(self, eng, fn, reads=(), writes=(), inc=True):
        psr = [a for a in reads if any(isinstance(k, tuple) and k[0] == "ps" for k in a.keys)]
        if psr:
            writes = list(writes) + psr
        self._wait(eng, self._need(eng, reads, writes))
        ins = fn()
        if inc:
            self.cnt[eng] += 1
            ins.then_inc(self.sem[eng], 1)
            t = (eng, self.cnt[eng])
        else:
            t = (eng, self.cnt[eng] + 1)
        self._record(t, reads, writes)

    def dma(self, queue, stream, out, in_, reads=(), writes=()):
        self._wait(queue, self._need(queue, reads, writes))
        ins = self.eng[queue].dma_start(out=out, in_=in_)
        self.cnt[stream] += 16
        ins.then_inc(self.sem[stream], 16)
        self._record((stream, self.cnt[stream]), reads, writes)

    def dma_group(self, queue, stream, items):
        allr, allw = [], []
        for o, i, r, w in items:
            allr += list(r)
            allw += list(w)
        self._wait(queue, self._need(queue, allr, allw))
        for o, i, r, w in items:
            ins = self.eng[queue].dma_start(out=o, in_=i)
            self.cnt[stream] += 16
            ins.then_inc(self.sem[stream], 16)
        self._record((stream, self.cnt[stream]), allr, allw)

    def mm(self, out, lhsT, rhs, start, stop, **kw):
        self.op("pe", lambda: self.nc.tensor.matmul(out.ap, lhsT.ap, rhs.ap, start=start, stop=stop, **kw),
                reads=[lhsT, rhs], writes=[out], inc=True)

    def act(self, out, in_, func, bias=None, scale=None, extra_reads=()):
        kw = {}
        rd = [in_] + list(extra_reads)
        if bias is not None:
            if isinstance(bias, Acc):
                kw["bias"] = bias.ap
                rd.append(bias)
            else:
                kw["bias"] = bias
        if scale is not None:
            if isinstance(scale, Acc):
                kw["scale"] = scale.ap
                rd.append(scale)
            else:
                kw["scale"] = scale
        self.op("act", lambda: self.nc.scalar.activation(out=out.ap, in_=in_.ap, func=func, **kw),
                reads=rd, writes=[out])

    def tt(self, eng, out, a, b, op):
        self.op(eng, lambda: self.eng[eng].tensor_tensor(out=out.ap, in0=a.ap, in1=b.ap, op=op),
                reads=[a, b], writes=[out])

    def ts(self, eng, out, a, s1, op0, s2=None, op1=None):
        rd = [a]
        v1 = s1.ap if isinstance(s1, Acc) else s1
        v2 = s2.ap if isinstance(s2, Acc) else s2
        if isinstance(s1, Acc):
            rd.append(s1)
        if isinstance(s2, Acc):
            rd.append(s2)
        if op1 is None:
            fn = lambda: self.eng[eng].tensor_scalar(out=out.ap, in0=a.ap, scalar1=v1, scalar2=None, op0=op0)
        else:
            fn = lambda: self.eng[eng].tensor_scalar(out=out.ap, in0=a.ap, scalar1=v1, scalar2=v2, op0=op0, op1=op1)
        self.op(eng, fn, reads=rd, writes=[out])

    def stt(self, out, a, s, b, op0, op1):
        rd = [a, b]
        sv = s.ap if isinstance(s, Acc) else s
        if isinstance(s, Acc):
            rd.append(s)
        self.op("dve", lambda: self.nc.vector.scalar_tensor_tensor(out=out.ap, in0=a.ap, scalar=sv, in1=b.ap, op0=op0, op1=op1),
                reads=rd, writes=[out])

    def copy(self, eng, out, in_):
        if eng == "act":
            self.act(out, in_, AF.Copy)
        else:
            self.op(eng, lambda: self.eng[eng].tensor_copy(out=out.ap, in_=in_.ap), reads=[in_], writes=[out])

    def memset(self, eng, out, val):
        self.op(eng, lambda: self.eng[eng].memset(out.ap, val), reads=[], writes=[out])

    def recip(self, out, in_):
        self.op("dve", lambda: self.nc.vector.reciprocal(out=out.ap, in_=in_.ap), reads=[in_], writes=[out])


def host_consts():
    ident = np.eye(128, dtype=np.float32)
    R = np.zeros((64, 64), np.float32)
    for i in range(16):
        R[i, 16 + i] = -1.0
        R[16 + i, i] = 1.0
        R[32 + i, 48 + i] = -1.0
        R[48 + i, 32 + i] = 1.0
    R2 = np.zeros((128, 128), np.float32)
    R2[:64, :64] = R
    R2[64:, 64:] = R
    RT = np.ascontiguousarray(R2.T)
    j = np.arange(128)[:, None]
    i = np.arange(128)[None, :]
    maskL = (j >= i).astype(np.float32)
    maskU = (j <= i).astype(np.float32)
    rows = T // 64
    row = np.repeat(np.arange(rows, dtype=np.float32), 64)
    col = np.tile(np.arange(64, dtype=np.float32), rows)
    inv = (10000.0 ** (-np.arange(0, 32, 2, dtype=np.float32) / 32)).astype(np.float32)
    ang_r = row[:, None] * inv[None, :]
    ang_c = col[:, None] * inv[None, :]
    ang = np.concatenate([ang_r, ang_r, ang_c, ang_c], axis=-1).astype(np.float32)
    cos = np.cos(ang).astype(np.float32).T
    sin = np.sin(ang).astype(np.float32).T
    cos2 = np.concatenate([cos, cos], axis=0)
    sin2 = np.concatenate([sin, sin], axis=0)
    small = np.concatenate([ident, RT, maskL, maskU], axis=1)
    return np.ascontiguousarray(small), np.ascontiguousarray(cos2), np.ascontiguousarray(sin2)


def build(layers=(0, 1), debug_out=False):
    from contextlib import ExitStack
    nc = bass.Bass("TRN2", target_bir_lowering=False)
    dr = {}

    def din(name, shape):
        dr[name] = nc.dram_tensor(name, list(shape), F32, kind="ExternalInput").ap()
        return dr[name]
    x_d = din("x", [T, D])
    c_d = din("c", [8, 128])
    ctx_d = din("ctx", [TCX, D])
    cctx_d = din("c_ctx", [8, 128])
    adaw_d = din("ada_w", [2, D, 6 * D])
    adab_d = din("ada_b", [96, 128])
    ng_d = din("norm_g", [64, 128])
    w1_d = din("mlp_w1", [2, D, DFF])
    w2_d = din("mlp_w2", [2, DFF, D])
    wqkv_d = din("attn_w_qkv", [D, 1536])
    wo_d = din("attn_w_o", [D, D])
    sink_d = din("attn_sink", [1, 16])
    win_d = din("lru_w_in", [D, 2 * DR])
    convw_d = din("lru_conv_w", [40, 128])
    convb_d = din("lru_conv_b", [10, 128])
    wa_d = din("lru_w_a", [2, 5, 256, 256])
    ba_d = din("lru_b_a", [20, 128])
    wi_d = din("lru_w_i", [2, 5, 256, 256])
    bi_d = din("lru_b_i", [20, 128])
    lam_d = din("lru_lam", [20, 128])
    wout_d = din("lru_w_out", [DR, D])
    csm_d = din("k_small", [128, 512])
    cos_d = din("k_cos", [128, T])
    sin_d = din("k_sin", [128, T])
    out_d = nc.dram_tensor("out", [T, D], F32, kind="ExternalOutput").ap()
    w1s_d = nc.dram_tensor("w1s", [2, D, DFF], BF16).ap()
    w2s_d = nc.dram_tensor("w2s", [2, DFF, D], BF16).ap()
    if debug_out:
        outc_d = nc.dram_tensor("outc", [TCX, D], F32, kind="ExternalOutput").ap()

    with ExitStack() as stack:
        P = Prog(nc, stack)
        s_ld = P.stream("ld")
        s_w = [P.stream("w%d" % i) for i in range(3)]
        s_wh = [P.stream("wh%d" % i) for i in range(3)]
        s_xs = [P.stream("xs%d" % i) for i in range(2)]
        s_st = [P.stream("st%d" % i) for i in range(2)]
        s_w2 = P.stream("wsmall")

        XT = P.alloc("XT", [KC, TT], F32)
        IDENT = P.alloc("IDENT", [128], F32)
        IDB = P.alloc("IDB", [128], BF16)
        ONESB = P.alloc("ONESB", [128], BF16)
        ONES64 = P.alloc("ONES64", [64], BF16)
        EPSC = P.alloc("EPSC", [1], F32)
        COLA = P.alloc("COLA", [80], F32)
        MOD = P.alloc("MOD", [96, 2], F32)
        DER = P.alloc("DER", [2, 2, 4, 8], F32)
        COLC = P.alloc("COLC", [110], F32)
        LCOL = P.alloc("LCOL", [5, 20], F32)
        ES = P.alloc("ES", [8], F32)
        SQ = [P.alloc("SQ%d" % i, [512], BF16) for i in range(2)]
        RSTD = P.alloc("RSTD", [512], F32)
        TMP = [P.alloc("TMP%d" % i, [512], F32) for i in range(2)]
        WS = [P.alloc("WS%d" % i, [8, 1024], BF16) for i in range(3)]
        wrr = [0]

        def wslot():
            i = wrr[0]
            wrr[0] = (i + 1) % 3
            return i

        P.memset("dve", ONESB[:, :], 1.0 / 1024.0)
        P.memset("dve", ONES64[:, :], 1.0)
        P.memset("dve", EPSC[:, :], EPS)

        COLB = P.alloc("COLB", [96], F32)
        SCB = P.alloc("SCB", [16], BF16)
        mk_setup = P.mark()
        CSM = P.alloc("CSM", [512], F32)
        ROWA = P.alloc("ROWA", [128], F32)
        ROWB = P.alloc("ROWB", [128], F32)
        ROWC = P.alloc("ROWC", [128], F32)
        XS = [P.alloc("XS%d" % i, [D], F32) for i in range(2)]

        P.dma("sp", P.stream("ld_a"), CSM.full[:, :], csm_d, writes=[CSM[:, :]])
        P.dma_group("sp", P.stream("ld_b"), [
            (ROWA.full[0:8, :], c_d, [], [ROWA[:, :]]),
            (ROWA.full[8:16, :], cctx_d, [], []),
            (ROWA.full[16:80, :], ng_d, [], []),
            (ROWB.full[0:96, :], adab_d, [], [ROWB[:, :]]),
            (ROWC.full[0:40, :], convw_d, [], [ROWC[:, :]]),
            (ROWC.full[40:50, :], convb_d, [], []),
            (ROWC.full[50:70, :], ba_d, [], []),
            (ROWC.full[70:90, :], bi_d, [], []),
            (ROWC.full[90:110, :], lam_d, [], []),
        ])
        sk = sink_d.rearrange("o (a h) -> o a h", h=2)
        with nc.allow_non_contiguous_dma(reason="tiny sink broadcast"):
            P.dma_group("sp", P.stream("ld_c"), [
                (ES.full[0:64, :], sk[:, :, 0].to_broadcast([64, 8]), [], [ES[:, :]]),
                (ES.full[64:128, :], sk[:, :, 1].to_broadcast([64, 8]), [], []),
            ])
        P.copy("dve", IDENT[:, :], CSM[:, 0:128])
        P.copy("dve", IDB[:, :], CSM[:, 0:128])
        P.act(ES[:, :], ES[:, :], AF.Exp)

        def rows2cols(ROW, nrows, COL):
            pb = P.ps_rot()
            o = P.ps(pb, cols=nrows)
            P.op("pe", lambda: nc.tensor.transpose(o.ap, ROW.full[0:nrows, :], IDENT.full[0:nrows, 0:nrows]),
                 reads=[ROW[:, :], IDENT[:, :]], writes=[o])
            P.copy("dve", COL[:, 0:nrows], o)
        rows2cols(ROWA, 80, COLA)
        rows2cols(ROWB, 96, COLB)
        rows2cols(ROWC, 110, COLC)
        P.act(SCB[:, :], COLA[:, 0:16], AF.Silu)

        MODP = 7
        defer_ada1 = (0 in layers and 1 in layers)
        for i in ([0] if defer_ada1 else [0, 1]):
            for g in range(6):
                sl = wslot()
                W = WS[sl]
                P.dma("pool", s_w[sl], W.full[:, :, :],
                      adaw_d[i, :, g * 1024:(g + 1) * 1024].rearrange("(k p) n -> p k n", p=128),
                      writes=[W[:, :, :]])
                for jj in range(8):
                    j = i * 48 + g * 8 + jj
                    o = P.ps(MODP, cols=2, c0=2 * j)
                    for k in range(KC):
                        P.mm(o, W[:, k, jj * 128:(jj + 1) * 128], SCB[:, k:16:8], start=(k == 0), stop=(k == KC - 1))
        def precast(i):
            for kind, src, dst in (("w1", w1_d[i], w1s_d[i]), ("w2", w2_d[i].rearrange("(a b) n -> a (b n)", b=4), w2s_d[i].rearrange("(a b) n -> a (b n)", b=4))):
                key = Acc(None, {("scr", i, kind)})
                items = []
                for a_ in range(8):
                    items.append((dst[a_ * 128:(a_ + 1) * 128, :], src[a_ * 128:(a_ + 1) * 128, :], [], [key] if a_ == 0 else []))
                P.dma_group("pool", P.stream("pc_%d_%s" % (i, kind)), items)
        if 0 not in layers:
            for i in layers:
                precast(i)
        nmod = 48 if defer_ada1 else 96
        for v in range(2):
            pv = Acc(P.psum[MODP][:, v:2 * nmod:2], {("ps", MODP)})
            P.tt("dve", MOD[:, 0:nmod, v], pv, COLB[:, 0:nmod], ALU.add)

        def der(i):
            for v in range(2):
                b = i * 48
                ng = lambda jn: COLA[:, 16 + (i * 4 + jn) * 8: 16 + (i * 4 + jn) * 8 + 8]
                P.stt(DER[:, i, v, 0, :], MOD[:, b + 8:b + 16, v], 1.0, ng(0), ALU.add, ALU.mult)
                P.tt("dve", DER[:, i, v, 1, :], MOD[:, b + 16:b + 24, v], ng(1), ALU.mult)
                P.stt(DER[:, i, v, 2, :], MOD[:, b + 32:b + 40, v], 1.0, ng(2), ALU.add, ALU.mult)
                P.tt("dve", DER[:, i, v, 3, :], MOD[:, b + 40:b + 48, v], ng(3), ALU.mult)
        der(0)
        if not defer_ada1:
            der(1)

        def A1(i, v, k): return DER[:, i, v, 0, k:k + 1]
        def G1(i, v, k): return DER[:, i, v, 1, k:k + 1]
        def A2(i, v, k): return DER[:, i, v, 2, k:k + 1]
        def G2(i, v, k): return DER[:, i, v, 3, k:k + 1]
        def B1(i, v, k): return MOD[:, i * 48 + k, v:v + 1]
        def B2(i, v, k): return MOD[:, i * 48 + 24 + k, v:v + 1]

        if 1 in layers:
            P.ts("dve", LCOL[:, 0, :], COLC[:, 50:70], 0.5, ALU.mult)
            P.ts("dve", LCOL[:, 1, :], COLC[:, 70:90], 0.5, ALU.mult)
            P.act(LCOL[:, 4, :], COLC[:, 90:110], AF.Exp, scale=-1.0)
            P.ts("dve", LCOL[:, 4, :], LCOL[:, 4, :], 1.0, ALU.add)
            P.act(LCOL[:, 4, :], LCOL[:, 4, :], AF.Ln)
            P.ts("dve", LCOL[:, 2, :], LCOL[:, 4, :], -8.0, ALU.mult)
            P.ts("dve", LCOL[:, 3, :], LCOL[:, 4, :], -4.0, ALU.mult)

        for tb in range(TT // 128):
            sl = tb % 2
            src = x_d[tb * 128:(tb + 1) * 128, :] if tb < 16 else ctx_d[(tb - 16) * 128:(tb - 15) * 128, :]
            P.dma("sp", s_xs[sl], XS[sl].full[:, :], src, writes=[XS[sl][:, :]])
            for h in range(2):
                pb = P.ps_rot()
                for q in range(4):
                    k = h * 4 + q
                    o = P.ps(pb, cols=128, c0=q * 128)
                    P.op("pe", lambda o=o, k=k, sl=sl: nc.tensor.transpose(o.ap, XS[sl].full[:, k * 128:(k + 1) * 128], IDENT.full[:, :]),
                         reads=[XS[sl][:, k * 128:(k + 1) * 128], IDENT[:, :]], writes=[o])
                full = Acc(P.psum[pb][:, :].rearrange("p (q t) -> p q t", q=4), {("ps", pb)})
                P.copy("dve" if h == 0 else "act", XT[:, h * 4:(h + 1) * 4, tb * 128:(tb + 1) * 128], full)
        P.release(mk_setup)

        PSS = 7

        def norm_block(i, t0, n, v, which, H, Hk=None, ssb=None, RSTD=RSTD):
            if Hk is None:
                Hk = lambda k: H[:, k, 0:n]
            Af = A1 if which == 1 else A2
            Bf = B1 if which == 1 else B2
            ss = P.ps(PSS if ssb is None else ssb, cols=n)
            for k in range(KC):
                sq = SQ[k % 2]
                P.act(sq[:, 0:n], XT[:, k, t0:t0 + n], AF.Square)
                P.mm(ss, ONESB[:, :], sq[:, 0:n], start=(k == 0), stop=(k == KC - 1))
            P.act(RSTD[:, 0:n], ss, AF.Ln, bias=EPSC[:, 0:1], scale=1.0)
            P.act(RSTD[:, 0:n], RSTD[:, 0:n], AF.Exp, scale=-0.5)
            for k in range(KC):
                tmp = TMP[k % 2]
                P.stt(tmp[:, 0:n], XT[:, k, t0:t0 + n], Af(i, v, k), RSTD[:, 0:n], ALU.mult, ALU.mult)
                P.ts("dve", Hk(k), tmp[:, 0:n], Bf(i, v, k), ALU.add)

        def resid_update(i, t0, n, v, which, Y, ss):
            Gf = G1 if which == 1 else G2
            P.act(RSTD[:, 0:n], ss, AF.Ln, bias=EPSC[:, 0:1], scale=1.0)
            P.act(RSTD[:, 0:n], RSTD[:, 0:n], AF.Exp, scale=-0.5)
            for k in range(KC):
                tmp = TMP[k % 2]
                P.stt(tmp[:, 0:n], Y[:, k, 0:n], Gf(i, v, k), RSTD[:, 0:n], ALU.mult, ALU.mult)
                P.tt("pool", XT[:, k, t0:t0 + n], XT[:, k, t0:t0 + n], tmp[:, 0:n], ALU.add)

        def proj_out_stats(Wfn, nk, rhsfn, n, Y):
            ss = P.ps(PSS, cols=n)
            pend = None
            for m in range(KC):
                pb = P.ps_rot()
                o = P.ps(pb, cols=n)
                for k in range(nk):
                    P.mm(o, Wfn(k, m), rhsfn(k), start=(k == 0), stop=(k == nk - 1))
                if pend is not None:
                    P.mm(ss, ONESB[:, :], pend[0], start=(pend[1] == 0), stop=False)
                sq = SQ[m % 2]
                P.act(sq[:, 0:n], o, AF.Square)
                P.copy("dve", Y[:, m, 0:n], o)
                pend = (sq[:, 0:n], m)
            P.mm(ss, ONESB[:, :], pend[0], start=False, stop=True)
            return ss

        out_state = {"OS": None, "done": set()}

        def out_tokens(tb):
            OS = out_state["OS"]
            sl = tb % 2
            for h in range(2):
                pb = P.ps_rot()
                for q in range(4):
                    k = h * 4 + q
                    o = P.ps(pb, cols=128, c0=q * 128)
                    src = XT[:, k, tb * 128:(tb + 1) * 128]
                    P.op("pe", lambda o=o, src=src: nc.tensor.transpose(o.ap, src.ap, IDENT.full[:, :]),
                         reads=[src, IDENT[:, :]], writes=[o])
                P.copy("dve" if h == 0 else "act", OS[sl][:, h * 512:(h + 1) * 512], P.ps(pb))
            dst = out_d[tb * 128:(tb + 1) * 128, :] if tb < 16 else outc_d[(tb - 16) * 128:(tb - 15) * 128, :]
            P.dma("sp", s_st[sl], dst, OS[sl].full[:, :], reads=[OS[sl][:, :]])
            out_state["done"].add(tb)

        def mlp(i, blocks, ada_groups=(), emit_out=False):
            mk = P.mark()
            if emit_out:
                out_state["OS"] = [P.alloc("OS%d" % q, [D], F32) for q in range(2)]
            H2 = P.alloc("H2", [KC, 512], BF16)
            G = P.alloc("G", [32, 512], BF16)
            Y = P.alloc("Ym", [KC, 512], F32)
            R1 = [P.alloc("R1_%d" % q, [512], BF16) for q in range(2)]
            RSTDn = P.alloc("RSTDn", [512], F32)
            loads = []
            ada_left = list(ada_groups)
            ada_sched = []
            for bi in range(len(blocks)):
                loads += [("w1", g) for g in range(4)] + [("w2", g) for g in range(4)]
                mine = ada_left[:2]
                ada_left = ada_left[2:]
                ada_sched.append(mine)
                loads += [("ada", g) for g in mine]
            assert not ada_left
            slots = {}

            def issue(L):
                if L >= len(loads) or L in slots:
                    return
                kind, g = loads[L]
                sl = wslot()
                slots[L] = sl
                W = WS[sl]
                key = Acc(None, {("scr", i, kind)})
                if kind == "ada":
                    P.dma("pool", s_w[sl], W.full[:, :, :],
                          adaw_d[1, :, g * 1024:(g + 1) * 1024].rearrange("(k p) n -> p k n", p=128),
                          writes=[W[:, :, :]])
                elif kind == "w1":
                    P.dma("sp", s_wh[sl], W.full[:, :, :],
                          w1s_d[i, :, g * 1024:(g + 1) * 1024].rearrange("(k p) n -> p k n", p=128),
                          reads=[key], writes=[W[:, :, :]])
                else:
                    P.dma("sp", s_wh[sl], W.full.rearrange("p a (b c) -> p (a b) c", c=256),
                          w2s_d[i, :, g * 256:(g + 1) * 256].rearrange("(k p) n -> p k n", p=128),
                          reads=[key], writes=[W[:, :, :]])
            issue(0)
            issue(1)
            L = 0
            norm_block(i, blocks[0][0], blocks[0][1], blocks[0][2], 2, H2, ssb=6, RSTD=RSTDn)
            for bi, (t0, n, v) in enumerate(blocks):
                for g in range(4):
                    issue(L + 2)
                    W = WS[slots[L]]
                    L += 1
                    for jj in range(8):
                        pb = P.ps_rot()
                        o = P.ps(pb, cols=n)
                        for k in range(KC):
                            P.mm(o, W[:, k, jj * 128:(jj + 1) * 128], H2[:, k, 0:n], start=(k == 0), stop=(k == KC - 1))
                        r1 = R1[jj % 2]
                        P.act(r1[:, 0:n], o, AF.Relu)
                        P.tt("dve", G[:, g * 8 + jj, 0:n], r1[:, 0:n], r1[:, 0:n], ALU.mult)
                if bi + 1 < len(blocks):
                    nb_ = blocks[bi + 1]
                    norm_block(i, nb_[0], nb_[1], nb_[2], 2, H2, ssb=6, RSTD=RSTDn)
                ss = P.ps(PSS, cols=n)
                for g in range(4):
                    issue(L + 2)
                    W = WS[slots[L]]
                    L += 1
                    Wv = Acc(W.full.rearrange("p a (b c) -> p (a b) c", c=256), W[:, :, :].keys)
                    for mm_ in range(2):
                        m = g * 2 + mm_
                        pb = P.ps_rot()
                        o = P.ps(pb, cols=n)
                        for k in range(32):
                            lw = Acc(Wv.ap[:, k, mm_ * 128:(mm_ + 1) * 128], Wv.keys)
                            P.mm(o, lw, G[:, k, 0:n], start=(k == 0), stop=(k == 31))
                        sq = SQ[m % 2]
                        P.act(sq[:, 0:n], o, AF.Square)
                        P.copy("dve", Y[:, m, 0:n], o)
                        P.mm(ss, ONESB[:, :], sq[:, 0:n], start=(m == 0), stop=(m == KC - 1))
                resid_update(i, t0, n, v, 2, Y, ss)
                if emit_out and bi > 0:
                    pt0, pn = blocks[bi - 1][0], blocks[bi - 1][1]
                    for tb in range(pt0 // 128, (pt0 + pn) // 128):
                        out_tokens(tb)
                for g in ada_sched[bi]:
                    issue(L + 2)
                    W = WS[slots[L]]
                    L += 1
                    pb = P.ps_rot()
                    for jj in range(8):
                        o = P.ps(pb, cols=2, c0=2 * jj)
                        for k in range(KC):
                            P.mm(o, W[:, k, jj * 128:(jj + 1) * 128], SCB[:, k:16:8], start=(k == 0), stop=(k == KC - 1))
                    for v_ in range(2):
                        pv = Acc(P.psum[pb][:, v_:16:2], {("ps", pb)})
                        P.tt("dve", MOD[:, 48 + g * 8:48 + g * 8 + 8, v_], pv, COLB[:, 48 + g * 8:48 + g * 8 + 8], ALU.add)
            if emit_out:
                pt0, pn = blocks[-1][0], blocks[-1][1]
                for tb in range(pt0 // 128, (pt0 + pn) // 128):
                    out_tokens(tb)
            P.release(mk)

        if 0 in layers:
            i = 0
            mk0 = P.mark()
            RT = P.alloc("RT", [128], BF16)
            MASK = P.alloc("MASK", [2, 4, 128], BF16)
            COS = P.alloc("COS", [T], BF16)
            SIN = P.alloc("SIN", [T], BF16)
            KTb = P.alloc("KT", [2, TT], BF16)
            Vb = P.alloc("V", [TT // 128, 256], BF16)
            WQ, WO, WKV = WS[0], WS[1], WS[2]
            AB = 256
            H = P.alloc("H", [KC, AB], BF16)
            QTb = [P.alloc("QT%d" % q, [2, AB // 128, 4, 128], BF16) for q in range(2)]
            QC = P.alloc("QC", [2, AB // 128, 4, 128], BF16)
            ATb = P.alloc("ATb", [KC, AB], BF16)
            Y = P.alloc("Y", [KC, AB], F32)
            QB = [P.alloc("QB%d" % q, [AB], BF16) for q in range(2)]
            T1 = [P.alloc("T1_%d" % q, [AB], F32) for q in range(2)]
            T2 = [P.alloc("T2_%d" % q, [AB], F32) for q in range(2)]
            PT = [P.alloc("PT%d" % q, [512], BF16) for q in range(10)]
            ptc = [0]
            ZS = P.alloc("ZS", [256], F32)
            mkc = P.mark()
            CSM2 = Buf(P.arena, "CSM2", T1[0].off, [512], F32)
            CST = Buf(P.arena, "CST", Y.off, [T], F32)
            P.dma("sp", P.stream("csm2"), CSM2.full[:, :], csm_d, writes=[CSM2[:, :]])
            P.copy("dve", RT[:, :], CSM2[:, 128:256])
            for q in range(4):
                P.copy("dve", MASK[:, 0, q, :], CSM2[:, 256:384])
                P.copy("dve", MASK[:, 1, q, :], CSM2[:, 384:512])
            P.dma("sp", P.stream("cst"), CST.full[:, :], cos_d, writes=[CST[:, :]])
            P.copy("dve", COS[:, :], CST[:, :])
            P.dma("sp", P.stream("cst"), CST.full[:, :], sin_d, writes=[CST[:, :]])
            P.copy("dve", SIN[:, :], CST[:, :])
            P.release(mkc)
            WQ6 = WQ.full.rearrange("p k (gp g two d) -> p k gp g two d", gp=2, g=4, two=2)
            items = []
            for gp_ in range(2):
                for two_ in range(2):
                    for g_ in range(4):
                        hb = ((2 * gp_ + two_) * 4 + g_) * 64
                        items.append((WQ6[:, :, gp_, g_, two_, :], wqkv_d[:, hb:hb + 64].rearrange("(k p) d -> p k d", p=128),
                                      [], [WQ[:, :, :]] if not items else []))
            P.dma_group("pool", P.stream("wq"), items)
            P.dma("pool", P.stream("wkv"), WKV.full[:, :, 0:512], wqkv_d[:, 1024:1536].rearrange("(k p) n -> p k n", p=128), writes=[WKV[:, :, :]])
            P.dma("pool", P.stream("wo"), WO.full[:, :, :], wo_d.rearrange("(k p) n -> p k n", p=128), writes=[WO[:, :, :]])

            rr2 = [0]

            rope_pend = [None]

            def rope_flush():
                if rope_pend[0] is not None:
                    f = rope_pend[0]
                    rope_pend[0] = None
                    f()

            def rope_evac(o, n, t0, dst, is_x, split=False):
                def vw(a):
                    if not split:
                        return a
                    return Acc(a.ap.rearrange("p (a b) -> p a b", b=128), a.keys)
                if not is_x:
                    P.copy("act", dst, vw(o))
                    return
                q = rr2[0]
                rr2[0] ^= 1
                P.copy("act", QB[q][:, 0:n], o)
                P.tt("dve", T1[q][:, 0:n], o, COS[:, t0:t0 + n], ALU.mult)

                def fin():
                    pb = P.ps_rot()
                    ro = P.ps(pb, cols=n)
                    P.mm(ro, RT[:, :], QB[q][:, 0:n], start=True, stop=True)
                    P.tt("dve", T2[q][:, 0:n], ro, SIN[:, t0:t0 + n], ALU.mult)
                    P.tt("pool", dst, vw(T1[q][:, 0:n]), vw(T2[q][:, 0:n]), ALU.add)
                prevf = rope_pend[0]
                rope_pend[0] = fin
                if prevf is not None:
                    prevf()

            def kvq(t0, n, v, QT, do_norm=True):
                is_x = (v == 0)
                if do_norm:
                    norm_block(i, t0, n, v, 1, H)
                for kc in range(2):
                    pb = P.ps_rot()
                    o = P.ps(pb, cols=n)
                    for k in range(KC):
                        P.mm(o, WKV[:, k, kc * 128:(kc + 1) * 128], H[:, k, 0:n], start=(k == 0), stop=(k == KC - 1))
                    rope_evac(o, n, t0, KTb[:, kc, t0:t0 + n], is_x)
                for s in range(n // 128):
                    pb = P.ps_rot()
                    o = P.ps(pb, cols=256)
                    for k in range(KC):
                        P.mm(o, H[:, k, s * 128:(s + 1) * 128], WKV[:, k, 256:512], start=(k == 0), stop=(k == KC - 1))
                    P.copy("dve", Vb[:, t0 // 128 + s, :], o)
                    rope_flush()
                for j in range(8):
                    gp, g = j // 4, j % 4
                    sidx = (g % 2) * 2 + g // 2
                    pb = P.ps_rot()
                    o = P.ps(pb, cols=n)
                    for k in range(KC):
                        P.mm(o, WQ[:, k, j * 128:(j + 1) * 128], H[:, k, 0:n], start=(k == 0), stop=(k == KC - 1))
                    rope_evac(o, n, t0, QT[:, gp, 0:n // 128, sidx, :], is_x, split=True)
                rope_flush()

            obank = [0]

            def attention(t0, n, v, QT, mid_hook=None):
                its = []
                for qi in range(n // 128):
                    q0 = t0 + qi * 128
                    if v == 0:
                        qb = q0 // 128
                        kchunks = []
                        if qb > 0:
                            kchunks.append((qb - 1, 0))
                        kchunks.append((qb, None))
                        if qb < 15:
                            kchunks.append((qb + 1, 1))
                        kchunks += [(16, None), (17, None)]
                    else:
                        kchunks = [(16, None), (17, None)]
                    for kvh in range(4):
                        its.append((qi, kvh, kchunks))

                def emit_S(qi, kvh, kchunks, pts):
                    half = (kvh % 2) * 64
                    kc = kvh // 2
                    gp = kvh // 2
                    for ci, (kch, msk) in enumerate(kchunks):
                        pb = P.ps_rot()
                        so = P.ps(pb, cols=512)
                        lk = Acc(KTb.full[half:half + 64, kc, kch * 128:(kch + 1) * 128], KTb[:, kc, kch * 128:(kch + 1) * 128].keys)
                        rq = Acc(QT.full[half:half + 64, gp, qi, :, :].rearrange("p s t -> p (s t)"), QT[:, gp, qi, :, :].keys)
                        P.mm(so, lk, rq, start=True, stop=True)
                        pt = PT[ptc[0] % len(PT)]
                        ptc[0] += 1
                        P.act(pt[:, :], so, AF.Exp, scale=0.125)
                        if msk is not None:
                            mv = Acc(MASK.full[:, msk, :, :].rearrange("p g t -> p (g t)"), MASK[:, msk, :, :].keys)
                            P.tt("pool", pt[:, :], pt[:, :], mv, ALU.mult)
                        pts.append((pt, kch))
                        yield

                def emit_PV(qi, kvh, pts):
                    ob = 3 + obank[0]
                    zb = 5 + obank[0]
                    obank[0] ^= 1
                    nch = len(pts)
                    for ci, (pt, kch) in enumerate(pts):
                        vv = Acc(Vb.full[:, kch, kvh * 64:(kvh + 1) * 64], Vb[:, kch, :].keys)
                        last = (ci == nch - 1)
                        for hf in range(2):
                            rp = pt[:, hf * 256:(hf + 1) * 256]
                            oo = Acc(P.psum[ob][hf * 64:(hf + 1) * 64, 0:256], {("ps", ob)})
                            P.op("pe", lambda oo=oo, vv=vv, rp=rp, ci=ci, last=last, hf=hf: nc.tensor.matmul(
                                oo.ap, vv.ap, rp.ap, start=(ci == 0), stop=last, tile_position=(0, hf * 64)),
                                reads=[vv, rp], writes=[oo], inc=True)
                        for hf in range(2):
                            rp = pt[:, hf * 256:(hf + 1) * 256]
                            zz = Acc(P.psum[zb][hf * 64:(hf + 1) * 64, 0:256], {("ps", zb)})
                            P.op("pe", lambda zz=zz, rp=rp, ci=ci, last=last, hf=hf: nc.tensor.matmul(
                                zz.ap, ONES64.full[:, :], rp.ap, start=(ci == 0), stop=last, tile_position=(0, hf * 64)),
                                reads=[ONES64[:, :], rp], writes=[zz], inc=True)
                        yield
                    for gg in range(2):
                        P.ts("dve", ZS[:, gg * 128:(gg + 1) * 128], P.ps(zb, cols=128, c0=gg * 128),
                             ES[:, kvh * 2 + gg:kvh * 2 + gg + 1], ALU.add)
                    P.act(ZS[:, :], ZS[:, :], AF.Ln)
                    P.act(ZS[:, :], ZS[:, :], AF.Exp, scale=-1.0)
                    dst = ATb[:, 2 * kvh:2 * kvh + 2, qi * 128:(qi + 1) * 128]
                    o3 = Acc(P.psum[ob][:, 0:256].rearrange("p (g t) -> p g t", g=2), {("ps", ob)})
                    z3 = Acc(ZS.full.rearrange("p (g t) -> p g t", g=2), ZS[:, :].keys)
                    P.tt("dve", dst, o3, z3, ALU.mult)

                def run(*gens):
                    gens = [g for g in gens if g is not None]
                    while gens:
                        for g in list(gens):
                            try:
                                next(g)
                            except StopIteration:
                                gens.remove(g)
                prev = None
                for idx, (qi, kvh, kchunks) in enumerate(its):
                    pts = []
                    run(emit_S(qi, kvh, kchunks, pts), emit_PV(*prev) if prev is not None else None)
                    prev = (qi, kvh, pts)
                run(emit_PV(*prev))
                if mid_hook is not None:
                    mid_hook()

            def att_block(t0, n, v, QT, mid_hook=None):
                attention(t0, n, v, QT, mid_hook)
                ss = proj_out_stats(lambda k, m: WO[:, k, m * 128:(m + 1) * 128], KC, lambda k: ATb[:, k, 0:n], n, Y)
                resid_update(i, t0, n, v, 1, Y, ss)

            xb = [(b * AB, AB, 0) for b in range(T // AB)]
            cb = (T, TCX, 1)
            kvq(*cb, QC)
            kvq(*xb[0], QTb[0])
            kvq(*xb[1], QTb[1])
            for b in range(len(xb)):
                nxt = xb[b + 2] if b + 2 < len(xb) else None
                hook = (lambda nxt=nxt: norm_block(i, nxt[0], nxt[1], nxt[2], 1, H)) if nxt is not None else None
                att_block(*xb[b], QTb[b % 2], mid_hook=hook)
                if nxt is not None:
                    kvq(*nxt, QTb[b % 2], do_norm=False)
                if b == 0:
                    precast(0)
                if b == 4 and 1 in layers:
                    precast(1)
            att_block(*cb, QC)
            P.release(mk0)
            blocks = [(b * 512, 512, 0) for b in range(4)]
            if 1 in layers or debug_out:
                blocks.append((T, TCX, 1))
            mlp(0, blocks, ada_groups=(list(range(6)) if defer_ada1 else ()))
            if defer_ada1:
                der(1)

        if 1 in layers:
            i = 1
            mk1 = P.mark()
            PROD = P.alloc("PROD", [10, T], BF16)
            mkh = P.mark()
            HT = P.alloc("HT", [KC, T], BF16)
            def xpiece(k, half, shape, dt):
                return Buf(P.arena, "xp", XT.off + (k * TT + T) * 4 + half * 512, shape, dt)
            HTC = [xpiece(k, 0, [TCX], BF16) for k in range(KC)]
            WGP = [xpiece(k, 1, [256], BF16) for k in range(KC)]

            def hsl(k, t0, n):
                return HT[:, k, t0:t0 + n] if t0 < T else HTC[k][:, 0:n]
            VPW = 2320
            XO, CO = 2, 2058
            base = WS[0].off
            AA = Buf(P.arena, "AA", base, [TT], F32)
            WB = [Buf(P.arena, "WBf", base + 9216, [TT], F32), Buf(P.arena, "WBb", base + 2 * 9216, [TT], F32)]
            VP = Buf(P.arena, "VP", base + 3 * 9216, [2, VPW], BF16)
            S1 = Buf(P.arena, "S1", base + 3 * 9216, [TT], F32)
            U = Buf(P.arena, "U", base + 3 * 9216 + 9472, [2, TT], BF16)
            DG = Buf(P.arena, "DG", base + 3 * 9216 + 9472 + 9216, [8, 128], BF16)
            assert base + 3 * 9216 + 9472 + 9216 + 2048 <= WS[2].off + 16384
            WR = Buf(P.arena, "WR", SQ[0].off, [KC, 256], BF16)
            assert SQ[1].off == SQ[0].off + 1024 and RSTD.off == SQ[0].off + 2048
            TR, TI = TMP[0], TMP[1]
            allb = [(T, TCX, 1)] + [(b * 512, 512, 0) for b in range(4)]
            for (t0, n, v) in allb:
                norm_block(i, t0, n, v, 1, None, Hk=(lambda k, t0=t0, n=n: hsl(k, t0, n)))

            def vcol(t0):
                return (XO + t0) if t0 < T else (CO + t0 - T)

            AAC = P.alloc("AAC", [TCX], F32)
            pending = [None]

            def aa_acc(d, t0, n):
                if d == 0:
                    return AA[:, t0:t0 + n]
                if t0 >= T:
                    return AAC[:, 0:n]
                return P.ps(3 + t0 // 512, cols=n)

            def make_scan(d, ch):
                W_ = WB[d]

                def seg(j):
                    if d == 0:
                        if j == 0:
                            P.op("dve", lambda: nc.vector.tensor_tensor_scan(
                                out=W_.full[:, T:TT], data0=AA.full[:, T:TT], data1=W_.full[:, T:TT], initial=0.0,
                                op0=ALU.mult, op1=ALU.add), reads=[AA[:, T:TT], W_[:, T:TT]], writes=[W_[:, T:TT]])
                        else:
                            b_ = j - 1
                            init = W_[:, TT - 1:TT] if b_ == 0 else W_[:, b_ * 512 - 1:b_ * 512]
                            sl_ = slice(b_ * 512, (b_ + 1) * 512)
                            P.op("dve", lambda: nc.vector.tensor_tensor_scan(
                                out=W_.full[:, sl_], data0=AA.full[:, sl_], data1=W_.full[:, sl_], initial=init.ap,
                                op0=ALU.mult, op1=ALU.add), reads=[AA[:, sl_], W_[:, sl_], init], writes=[W_[:, sl_]])
                    else:
                        if j == 0:
                            P.op("dve", lambda: nc.vector.tensor_tensor_scan(
                                out=W_.full[:, TT - 1:T - 1:-1], data0=AAC.full[:, TCX - 1::-1], data1=W_.full[:, TT - 1:T - 1:-1], initial=0.0,
                                op0=ALU.mult, op1=ALU.add), reads=[AAC[:, :], W_[:, T:TT]], writes=[W_[:, T:TT]])
                        else:
                            b_ = 4 - j
                            init = W_[:, T:T + 1] if b_ == 3 else W_[:, (b_ + 1) * 512:(b_ + 1) * 512 + 1]
                            lo, hi = b_ * 512, (b_ + 1) * 512
                            rs = slice(hi - 1, lo - 1 if lo > 0 else None, -1)
                            aap = P.ps(3 + b_)
                            P.op("dve", lambda: nc.vector.tensor_tensor_scan(
                                out=W_.full[:, rs], data0=P.psum[3 + b_][:, 511::-1], data1=W_.full[:, rs], initial=init.ap,
                                op0=ALU.mult, op1=ALU.add), reads=[aap, W_[:, lo:hi], init], writes=[W_[:, lo:hi]])
                            sl_ = slice(lo, hi)
                            P.tt("pool", WB[0][:, sl_], WB[0][:, sl_], WB[1][:, sl_], ALU.add)
                            P.tt("dve", PROD[:, ch, sl_], PROD[:, ch, sl_], WB[0][:, sl_], ALU.mult)
                return seg

            for nb in range(5):
                items = []
                for dd in range(2):
                    for gt, wsrc in enumerate((wa_d, wi_d)):
                        for kc in range(2):
                            pc = WGP[(dd * 2 + gt) * 2 + kc]
                            items.append((pc.full[:, :], wsrc[dd, nb, kc * 128:(kc + 1) * 128, :], [], [pc[:, :]]))
                P.dma_group("pool", P.stream("wg"), items)
                for c2 in range(2):
                    ch = nb * 2 + c2
                    for tap in range(4):
                        P.ts("dve", DG[:, c2 * 4 + tap, :], IDB[:, :], COLC[:, tap * 10 + ch:tap * 10 + ch + 1], ALU.mult)
                P.dma("pool", P.stream("wr"), WR.full[:, :, :], win_d[:, nb * 256:(nb + 1) * 256].rearrange("(k p) n -> p k n", p=128), writes=[WR[:, :, :]])
                for (t0, n, v) in allb[1:]:
                    for c2 in range(2):
                        pb = P.ps_rot()
                        o = P.ps(pb, cols=n)
                        for k in range(KC):
                            P.mm(o, WR[:, k, c2 * 128:(c2 + 1) * 128], hsl(k, t0, n), start=(k == 0), stop=(k == KC - 1))
                        P.act(PROD[:, nb * 2 + c2, t0:t0 + n], o, AF.Gelu_apprx_tanh)
                P.dma("pool", P.stream("wr"), WR.full[:, :, :], win_d[:, DR + nb * 256:DR + (nb + 1) * 256].rearrange("(k p) n -> p k n", p=128), writes=[WR[:, :, :]])
                P.memset("dve", VP[:, :, :], 0.0)
                for (t0, n, v) in allb:
                    for c2 in range(2):
                        pb = P.ps_rot()
                        o = P.ps(pb, cols=n)
                        for k in range(KC):
                            P.mm(o, WR[:, k, c2 * 128:(c2 + 1) * 128], hsl(k, t0, n), start=(k == 0), stop=(k == KC - 1))
                        P.copy("act", VP[:, c2, vcol(t0):vcol(t0) + n], o)
                for (t0, n, v) in allb:
                    for c2 in range(2):
                        pb = P.ps_rot()
                        o = P.ps(pb, cols=n)
                        for tap in range(4):
                            c0 = vcol(t0) + tap - 2
                            P.mm(o, DG[:, c2 * 4 + tap, :], VP[:, c2, c0:c0 + n], start=(tap == 0), stop=(tap == 3))
                        ch = nb * 2 + c2
                        P.act(U[:, c2, t0:t0 + n], o, AF.Identity, bias=COLC[:, 40 + ch:40 + ch + 1], scale=1.0)
                for oc in range(2):
                    ch = nb * 2 + oc
                    for d in range(2):
                        col = d * 10 + ch
                        fwd = allb
                        bwd = [allb[0]] + allb[:0:-1]
                        order = fwd if d == 1 else bwd
                        for j, (t0, n, v) in enumerate(order):
                            if pending[0] is not None:
                                pending[0](j)
                            pb = P.ps_rot()
                            o = P.ps(pb, cols=n)
                            for kc in range(2):
                                P.mm(o, WGP[(d * 2 + 0) * 2 + kc][:, oc * 128:(oc + 1) * 128], U[:, kc, t0:t0 + n], start=(kc == 0), stop=(kc == 1))
                            P.act(TR[:, 0:n], o, AF.Tanh, bias=LCOL[:, 0, col:col + 1], scale=0.5)
                            P.act(aa_acc(d, t0, n), TR[:, 0:n], AF.Exp, bias=LCOL[:, 3, col:col + 1], scale=LCOL[:, 3, col:col + 1])
                            a2p = P.ps(P.ps_rot(), cols=n)
                            P.act(a2p, aa_acc(d, t0, n), AF.Square)
                            pb = P.ps_rot()
                            o = P.ps(pb, cols=n)
                            for kc in range(2):
                                P.mm(o, WGP[(d * 2 + 1) * 2 + kc][:, oc * 128:(oc + 1) * 128], U[:, kc, t0:t0 + n], start=(kc == 0), stop=(kc == 1))
                            P.act(TI[:, 0:n], o, AF.Tanh, bias=LCOL[:, 1, col:col + 1], scale=0.5)
                            P.ts("dve", S1[:, t0:t0 + n], a2p, -1.0, ALU.mult, 1.0, ALU.add)
                            P.stt(WB[d][:, t0:t0 + n], TI[:, 0:n], 1.0, U[:, oc, t0:t0 + n], ALU.add, ALU.mult)
                        pending[0] = None
                        for (t0, n, v) in allb:
                            P.act(S1[:, t0:t0 + n], S1[:, t0:t0 + n], AF.Sqrt)
                        for (t0, n, v) in allb:
                            P.stt(WB[d][:, t0:t0 + n], S1[:, t0:t0 + n], 0.5, WB[d][:, t0:t0 + n], ALU.mult, ALU.mult)
                        pending[0] = make_scan(d, ch)
            for j in range(5):
                pending[0](j)
            pending[0] = None
            P.release(mkh)
            Y1 = P.alloc("Y1", [KC, 512], F32)
            P.dma("pool", s_w[0], WS[0].full[:, :, :], wout_d[0:1024, :].rearrange("(k p) n -> p k n", p=128), writes=[WS[0][:, :, :]])
            P.dma("pool", s_w[1], WS[1].full[:, 0:2, :], wout_d[1024:1280, :].rearrange("(k p) n -> p k n", p=128), writes=[WS[1][:, 0:2, :]])

            def wout(k, m):
                return WS[0][:, k, m * 128:(m + 1) * 128] if k < 8 else WS[1][:, k - 8, m * 128:(m + 1) * 128]
            for b in range(4):
                t0 = b * 512
                ss = proj_out_stats(wout, 10, lambda k: PROD[:, k, t0:t0 + 512], 512, Y1)
                resid_update(i, t0, 512, 0, 1, Y1, ss)
            P.release(mk1)
            mlp(1, [(b * 512, 512, 0) for b in range(4)], emit_out=True)

        mko = P.mark()
        out_state["OS"] = [P.alloc("OSt%d" % q, [D], F32) for q in range(2)]
        nout = TT // 128 if debug_out else T // 128
        for tb in range(nout):
            if tb not in out_state["done"]:
                out_tokens(tb)
        for sl in range(2):
            nc.sync.wait_ge(P.sem[s_st[sl]], P.cnt[s_st[sl]])
        P.release(mko)
    return nc


_CACHE = {}


def _inputs_for_core(b, inp, consts):
    f = lambda a: np.ascontiguousarray(a, dtype=np.float32)
    csm, cos2, sin2 = consts
    return {
        "x": f(inp["x"][b]),
        "c": f(inp["c"][b].reshape(8, 128)),
        "ctx": f(inp["ctx"][b]),
        "c_ctx": f(inp["c_ctx"].reshape(8, 128)),
        "ada_w": f(inp["ada_w"]),
        "ada_b": f(inp["ada_b"].reshape(96, 128)),
        "norm_g": f(inp["norm_g"].reshape(64, 128)),
        "mlp_w1": f(inp["mlp_w1"]),
        "mlp_w2": f(inp["mlp_w2"]),
        "attn_w_qkv": f(inp["attn_w_qkv"][0]),
        "attn_w_o": f(inp["attn_w_o"][0]),
        "attn_sink": f(inp["attn_sink"]),
        "lru_w_in": f(inp["lru_w_in"][0]),
        "lru_conv_w": f(inp["lru_conv_w"][0].reshape(40, 128)),
        "lru_conv_b": f(inp["lru_conv_b"][0].reshape(10, 128)),
        "lru_w_a": f(inp["lru_w_a"][0]),
        "lru_b_a": f(inp["lru_b_a"][0].reshape(20, 128)),
        "lru_w_i": f(inp["lru_w_i"][0]),
        "lru_b_i": f(inp["lru_b_i"][0].reshape(20, 128)),
        "lru_lam": f(inp["lru_lam"][0].reshape(20, 128)),
        "lru_w_out": f(inp["lru_w_out"][0]),
        "k_small": csm, "k_cos": cos2, "k_sin": sin2,
    }


def kernel(**inputs):
    inp = {k: np.asarray(v) for k, v in inputs.items()}
    consts = host_consts()
    if "nc" not in _CACHE:
        _CACHE["nc"] = build((0, 1))
    nc = _CACHE["nc"]
    in_maps = [_inputs_for_core(b, inp, consts) for b in range(8)]
    res = run_bass_kernel_spmd(nc, in_maps, core_ids=list(range(8)))
    out = np.stack([np.asarray(r["out"], dtype=np.float32) for r in res.results], axis=0)
    return out
```

```python
import numpy as np
import ml_dtypes
import concourse.bass as bass
import concourse.mybir as mybir
from concourse.bass_utils import run_bass_kernel_spmd

F32 = mybir.dt.float32
BF16 = mybir.dt.bfloat16
AF = mybir.ActivationFunctionType
ALU = mybir.AluOpType

D = 1024
KC = 8
T = 2048
TCX = 256
TT = T + TCX
DFF = 4096
DR = 1280
EPS = 1e-6
GRAN = 256
ARENA_WORDS = 53000


def _prod(s):
    r = 1
    for v in s:
        r *= v
    return r


class Acc:
    __slots__ = ("ap", "keys")

    def __init__(self, ap, keys):
        self.ap = ap
        self.keys = keys


class Buf:
    def __init__(self, arena, name, off, shape, dt):
        self.name = name
        self.off = off
        self.shape = tuple(shape)
        self.dt = dt
        self.esz = 4 if dt == F32 else 2
        nel = _prod(shape)
        nbytes = nel * self.esz
        assert off % 4 == 0 and nbytes % 4 == 0
        w = arena[:, off // 4:(off + nbytes) // 4]
        if dt != F32:
            w = w.bitcast(dt)
        if len(shape) == 2:
            w = w.rearrange("p (a b) -> p a b", a=shape[0])
        elif len(shape) == 3:
            w = w.rearrange("p (a b c) -> p a b c", a=shape[0], b=shape[1])
        elif len(shape) == 4:
            w = w.rearrange("p (a b c d) -> p a b c d", a=shape[0], b=shape[1], c=shape[2])
        self.full = w
        st = []
        s = 1
        for d in reversed(self.shape):
            st.append(s)
            s *= d
        self.strides = tuple(reversed(st))
        self.nbytes = nbytes

    def __getitem__(self, idx):
        if not isinstance(idx, tuple):
            idx = (idx,)
        ap = self.full[idx]
        fidx = list(idx[1:])
        while len(fidx) < len(self.shape):
            fidx.append(slice(None))
        lohi = []
        for d, ix in enumerate(fidx):
            n = self.shape[d]
            if isinstance(ix, int):
                lohi.append((ix, ix))
            else:
                a, b, c = ix.indices(n)
                if c > 0:
                    last = a + ((b - a - 1) // c) * c
                    lohi.append((a, last))
                else:
                    last = a + ((a - b - 1) // (-c)) * c
                    lohi.append((last, a))
        ranges = [(0, 0)]
        outer = lohi[:-1]
        ncomb = _prod([h - l + 1 for l, h in outer]) if outer else 1
        keys = set()
        if ncomb > 256:
            lo = sum(l * s for (l, h), s in zip(lohi, self.strides))
            hi = sum(h * s for (l, h), s in zip(lohi, self.strides))
            b0 = self.off + lo * self.esz
            b1 = self.off + (hi + 1) * self.esz
            keys.update(range(b0 // GRAN, (b1 - 1) // GRAN + 1))
        else:
            def rec(d, base):
                if d == len(lohi) - 1:
                    l, h = lohi[d]
                    b0 = self.off + (base + l) * self.esz
                    b1 = self.off + (base + h + 1) * self.esz
                    keys.update(range(b0 // GRAN, (b1 - 1) // GRAN + 1))
                    return
                l, h = lohi[d]
                for i in range(l, h + 1):
                    rec(d + 1, base + i * self.strides[d])
            rec(0, 0)
        return Acc(ap, keys)


class Prog:
    def __init__(self, nc, stack):
        self.nc = nc
        self.stack = stack
        self.eng = {"pe": nc.tensor, "act": nc.scalar, "dve": nc.vector, "pool": nc.gpsimd, "sp": nc.sync}
        self.sem = {}
        self.cnt = {}
        self.waited = {e: {} for e in self.eng}
        self.lastw = {}
        self.readers = {}
        for e in ("pe", "act", "dve", "pool"):
            self._mksem(e)
        self.arena = stack.enter_context(nc.sbuf_tensor("arena", [128, ARENA_WORDS], F32))
        self.top = 0
        self.tops = []
        self.psum = [stack.enter_context(nc.psum_tensor("ps%d" % i, [128, 512], F32)) for i in range(8)]
        self.rr = 0

    def _mksem(self, name):
        self.sem[name] = self.stack.enter_context(self.nc.semaphore(name.replace(":", "_")))
        self.cnt[name] = 0

    def stream(self, name):
        n = "dma:" + name
        if n not in self.sem:
            self._mksem(n)
        return n

    def alloc(self, name, shape, dt):
        esz = 4 if dt == F32 else 2
        nbytes = _prod(shape) * esz
        nbytes = (nbytes + GRAN - 1) // GRAN * GRAN
        off = self.top
        self.top += nbytes
        assert self.top <= ARENA_WORDS * 4, ("arena overflow", name, self.top)
        self.tops.append((self.top, name))
        return Buf(self.arena, name, off, shape, dt)

    def mark(self):
        return self.top

    def release(self, m):
        self.top = m

    def ps(self, bank, cols=512, rows=128, c0=0, r0=0):
        ap = self.psum[bank][r0:r0 + rows, c0:c0 + cols]
        return Acc(ap, {("ps", bank)})

    def ps_rot(self):
        b = self.rr
        self.rr = (self.rr + 1) % 3
        return b

    def _need(self, eng, reads, writes):
        need = {}

        def add(t, kind):
            if t is None:
                return
            s, v = t
            if s == eng and eng in ("pe", "sp"):
                return
            if need.get(s, 0) < v:
                need[s] = v
        for a in reads:
            for k in a.keys:
                add(self.lastw.get(k), "raw")
        for a in writes:
            for k in a.keys:
                add(self.lastw.get(k), "waw")
                r = self.readers.get(k)
                if r:
                    for s, v in r.items():
                        add((s, v), "war")
        return need

    def _wait(self, eng, need):
        w = self.waited[eng]
        for s, v in need.items():
            if w.get(s, 0) < v:
                self.eng[eng].wait_ge(self.sem[s], v)
                w[s] = v

    def _record(self, t, reads, writes):
        s, v = t
        for a in reads:
            for k in a.keys:
                r = self.readers.setdefault(k, {})
                if r.get(s, 0) < v:
                    r[s] = v
        for a in writes:
            for k in a.keys:
                self.lastw[k] = t
                self.readers[k] = {}

    def op(self, eng, fn, reads=(), writes=(), inc=True):
        psr = [a for a in reads if any(isinstance(k, tuple) and k[0] == "ps" for k in a.keys)]
        if psr:
            writes = list(writes) + psr
        self._wait(eng, self._need(eng, reads, writes))
        ins = fn()
        if inc:
            self.cnt[eng] += 1
            ins.then_inc(self.sem[eng], 1)
            t = (eng, self.cnt[eng])
        else:
            t = (eng, self.cnt[eng] + 1)
        self._record(t, reads, writes)

    def dma(self, queue, stream, out, in_, reads=(), writes=()):
        self._wait(queue, self._need(queue, reads, writes))
        ins = self.eng[queue].dma_start(out=out, in_=in_)
        self.cnt[stream] += 16
        ins.then_inc(self.sem[stream], 16)
        self._record((stream, self.cnt[stream]), reads, writes)

    def dma_group(self, queue, stream, items):
        allr, allw = [], []
        for o, i, r, w in items:
            allr += list(r)
            allw += list(w)
        self._wait(queue, self._need(queue, allr, allw))
        for o, i, r, w in items:
            ins = self.eng[queue].dma_start(out=o, in_=i)
            self.cnt[stream] += 16
            ins.then_inc(self.sem[stream], 16)
        self._record((stream, self.cnt[stream]), allr, allw)

    def mm(self, out, lhsT, rhs, start, stop, **kw):
        self.op("pe", lambda: self.nc.tensor.matmul(out.ap, lhsT.ap, rhs.ap, start=start, stop=stop, **kw),
                reads=[lhsT, rhs], writes=[out], inc=True)

    def act(self, out, in_, func, bias=None, scale=None, extra_reads=()):
        kw = {}
        rd = [in_] + list(extra_reads)
        if bias is not None:
            if isinstance(bias, Acc):
                kw["bias"] = bias.ap
                rd.append(bias)
            else:
                kw["bias"] = bias
        if scale is not None:
            if isinstance(scale, Acc):
                kw["scale"] = scale.ap
                rd.append(scale)
            else:
                kw["scale"] = scale
        self.op("act", lambda: self.nc.scalar.activation(out=out.ap, in_=in_.ap, func=func, **kw),
                reads=rd, writes=[out])

    def tt(self, eng, out, a, b, op):
        self.op(eng, lambda: self.eng[eng].tensor_tensor(out=out.ap, in0=a.ap, in1=b.ap, op=op),
                reads=[a, b], writes=[out])

    def ts(self, eng, out, a, s1, op0, s2=None, op1=None):
        rd = [a]
        v1 = s1.ap if isinstance(s1, Acc) else s1
        v2 = s2.ap if isinstance(s2, Acc) else s2
        if isinstance(s1, Acc):
            rd.append(s1)
        if isinstance(s2, Acc):
            rd.append(s2)
        if op1 is None:
            fn = lambda: self.eng[eng].tensor_scalar(out=out.ap, in0=a.ap, scalar1=v1, scalar2=None, op0=op0)
        else:
            fn = lambda: self.eng[eng].tensor_scalar(out=out.ap, in0=a.ap, scalar1=v1, scalar2=v2, op0=op0, op1=op1)
        self.op(eng, fn, reads=rd, writes=[out])

    def stt(self, out, a, s, b, op0, op1):
        rd = [a, b]
        sv = s.ap if isinstance(s, Acc) else s
        if isinstance(s, Acc):
            rd.append(s)
        self.op("dve", lambda: self.nc.vector.scalar_tensor_tensor(out=out.ap, in0=a.ap, scalar=sv, in1=b.ap, op0=op0, op1=op1),
                reads=rd, writes=[out])

    def copy(self, eng, out, in_):
        if eng == "act":
            self.act(out, in_, AF.Copy)
        else:
            self.op(eng, lambda: self.eng[eng].tensor_copy(out=out.ap, in_=in_.ap), reads=[in_], writes=[out])

    def memset(self, eng, out, val):
        self.op(eng, lambda: self.eng[eng].memset(out.ap, val), reads=[], writes=[out])

    def recip(self, out, in_):
        self.op("dve", lambda: self.nc.vector.reciprocal(out=out.ap, in_=in_.ap), reads=[in_], writes=[out])


def host_consts():
    ident = np.eye(128, dtype=np.float32)
    R = np.zeros((64, 64), np.float32)
    for i in range(16):
        R[i, 16 + i] = -1.0
        R[16 + i, i] = 1.0
        R[32 + i, 48 + i] = -1.0
        R[48 + i, 32 + i] = 1.0
    R2 = np.zeros((128, 128), np.float32)
    R2[:64, :64] = R
    R2[64:, 64:] = R
    RT = np.ascontiguousarray(R2.T)
    j = np.arange(128)[:, None]
    i = np.arange(128)[None, :]
    maskL = (j >= i).astype(np.float32)
    maskU = (j <= i).astype(np.float32)
    rows = T // 64
    row = np.repeat(np.arange(rows, dtype=np.float32), 64)
    col = np.tile(np.arange(64, dtype=np.float32), rows)
    inv = (10000.0 ** (-np.arange(0, 32, 2, dtype=np.float32) / 32)).astype(np.float32)
    ang_r = row[:, None] * inv[None, :]
    ang_c = col[:, None] * inv[None, :]
    ang = np.concatenate([ang_r, ang_r, ang_c, ang_c], axis=-1).astype(np.float32)
    cos = np.cos(ang).astype(np.float32).T
    sin = np.sin(ang).astype(np.float32).T
    cos2 = np.concatenate([cos, cos], axis=0)
    sin2 = np.concatenate([sin, sin], axis=0)
    small = np.concatenate([ident, RT, maskL, maskU], axis=1)
    return np.ascontiguousarray(small), np.ascontiguousarray(cos2), np.ascontiguousarray(sin2)


def build(layers=(0, 1), debug_out=False):
    from contextlib import ExitStack
    nc = bass.Bass("TRN2", target_bir_lowering=False)
    dr = {}

    def din(name, shape):
        dr[name] = nc.dram_tensor(name, list(shape), F32, kind="ExternalInput").ap()
        return dr[name]
    x_d = din("x", [T, D])
    c_d = din("c", [8, 128])
    ctx_d = din("ctx", [TCX, D])
    cctx_d = din("c_ctx", [8, 128])
    adaw_d = din("ada_w", [2, D, 6 * D])
    adab_d = din("ada_b", [96, 128])
    ng_d = din("norm_g", [64, 128])
    w1_d = din("mlp_w1", [2, D, DFF])
    w2_d = din("mlp_w2", [2, DFF, D])
    wqkv_d = din("attn_w_qkv", [D, 1536])
    wo_d = din("attn_w_o", [D, D])
    sink_d = din("attn_sink", [1, 16])
    win_d = din("lru_w_in", [D, 2 * DR])
    convw_d = din("lru_conv_w", [40, 128])
    convb_d = din("lru_conv_b", [10, 128])
    wa_d = din("lru_w_a", [2, 5, 256, 256])
    ba_d = din("lru_b_a", [20, 128])
    wi_d = din("lru_w_i", [2, 5, 256, 256])
    bi_d = din("lru_b_i", [20, 128])
    lam_d = din("lru_lam", [20, 128])
    wout_d = din("lru_w_out", [DR, D])
    csm_d = din("k_small", [128, 512])
    cos_d = din("k_cos", [128, T])
    sin_d = din("k_sin", [128, T])
    out_d = nc.dram_tensor("out", [T, D], F32, kind="ExternalOutput").ap()
    w1s_d = nc.dram_tensor("w1s", [2, D, DFF], BF16).ap()
    w2s_d = nc.dram_tensor("w2s", [2, DFF, D], BF16).ap()
    if debug_out:
        outc_d = nc.dram_tensor("outc", [TCX, D], F32, kind="ExternalOutput").ap()

    with ExitStack() as stack:
        P = Prog(nc, stack)
        s_ld = P.stream("ld")
        s_w = [P.stream("w%d" % i) for i in range(3)]
        s_wh = [P.stream("wh%d" % i) for i in range(3)]
        s_xs = [P.stream("xs%d" % i) for i in range(2)]
        s_st = [P.stream("st%d" % i) for i in range(2)]
        s_w2 = P.stream("wsmall")

        XT = P.alloc("XT", [KC, TT], F32)
        IDENT = P.alloc("IDENT", [128], F32)
        IDB = P.alloc("IDB", [128], BF16)
        ONESB = P.alloc("ONESB", [128], BF16)
        ONES64 = P.alloc("ONES64", [64], BF16)
        EPSC = P.alloc("EPSC", [1], F32)
        COLA = P.alloc("COLA", [80], F32)
        MOD = P.alloc("MOD", [96, 2], F32)
        DER = P.alloc("DER", [2, 2, 4, 8], F32)
        COLC = P.alloc("COLC", [110], F32)
        LCOL = P.alloc("LCOL", [5, 20], F32)
        ES = P.alloc("ES", [8], F32)
        SQ = [P.alloc("SQ%d" % i, [512], BF16) for i in range(2)]
        RSTD = P.alloc("RSTD", [512], F32)
        TMP = [P.alloc("TMP%d" % i, [512], F32) for i in range(2)]
        WS = [P.alloc("WS%d" % i, [8, 1024], BF16) for i in range(3)]
        wrr = [0]

        def wslot():
            i = wrr[0]
            wrr[0] = (i + 1) % 3
            return i

        P.memset("dve", ONESB[:, :], 1.0 / 1024.0)
        P.memset("dve", ONES64[:, :], 1.0)
        P.memset("dve", EPSC[:, :], EPS)

        COLB = P.alloc("COLB", [96], F32)
        SCB = P.alloc("SCB", [16], BF16)
        mk_setup = P.mark()
        CSM = P.alloc("CSM", [512], F32)
        ROWA = P.alloc("ROWA", [128], F32)
        ROWB = P.alloc("ROWB", [128], F32)
        ROWC = P.alloc("ROWC", [128], F32)
        XS = [P.alloc("XS%d" % i, [D], F32) for i in range(2)]

        P.dma("sp", P.stream("ld_a"), CSM.full[:, :], csm_d, writes=[CSM[:, :]])
        P.dma_group("sp", P.stream("ld_b"), [
            (ROWA.full[0:8, :], c_d, [], [ROWA[:, :]]),
            (ROWA.full[8:16, :], cctx_d, [], []),
            (ROWA.full[16:80, :], ng_d, [], []),
            (ROWB.full[0:96, :], adab_d, [], [ROWB[:, :]]),
            (ROWC.full[0:40, :], convw_d, [], [ROWC[:, :]]),
            (ROWC.full[40:50, :], convb_d, [], []),
            (ROWC.full[50:70, :], ba_d, [], []),
            (ROWC.full[70:90, :], bi_d, [], []),
            (ROWC.full[90:110, :], lam_d, [], []),
        ])
        sk = sink_d.rearrange("o (a h) -> o a h", h=2)
        with nc.allow_non_contiguous_dma(reason="tiny sink broadcast"):
            P.dma_group("sp", P.stream("ld_c"), [
                (ES.full[0:64, :], sk[:, :, 0].to_broadcast([64, 8]), [], [ES[:, :]]),
                (ES.full[64:128, :], sk[:, :, 1].to_broadcast([64, 8]), [], []),
            ])
        P.copy("dve", IDENT[:, :], CSM[:, 0:128])
        P.copy("dve", IDB[:, :], CSM[:, 0:128])
        P.act(ES[:, :], ES[:, :], AF.Exp)

        def rows2cols(ROW, nrows, COL):
            pb = P.ps_rot()
            o = P.ps(pb, cols=nrows)
            P.op("pe", lambda: nc.tensor.transpose(o.ap, ROW.full[0:nrows, :], IDENT.full[0:nrows, 0:nrows]),
                 reads=[ROW[:, :], IDENT[:, :]], writes=[o])
            P.copy("dve", COL[:, 0:nrows], o)
        rows2cols(ROWA, 80, COLA)
        rows2cols(ROWB, 96, COLB)
        rows2cols(ROWC, 110, COLC)
        P.act(SCB[:, :], COLA[:, 0:16], AF.Silu)

        MODP = 7
        defer_ada1 = (0 in layers and 1 in layers)
        for i in ([0] if defer_ada1 else [0, 1]):
            for g in range(6):
                sl = wslot()
                W = WS[sl]
                P.dma("pool", s_w[sl], W.full[:, :, :],
                      adaw_d[i, :, g * 1024:(g + 1) * 1024].rearrange("(k p) n -> p k n", p=128),
                      writes=[W[:, :, :]])
                for jj in range(8):
                    j = i * 48 + g * 8 + jj
                    o = P.ps(MODP, cols=2, c0=2 * j)
                    for k in range(KC):
                        P.mm(o, W[:, k, jj * 128:(jj + 1) * 128], SCB[:, k:16:8], start=(k == 0), stop=(k == KC - 1))
        def precast(i):
            for kind, src, dst in (("w1", w1_d[i], w1s_d[i]), ("w2", w2_d[i].rearrange("(a b) n -> a (b n)", b=4), w2s_d[i].rearrange("(a b) n -> a (b n)", b=4))):
                key = Acc(None, {("scr", i, kind)})
                items = []
                for a_ in range(8):
                    items.append((dst[a_ * 128:(a_ + 1) * 128, :], src[a_ * 128:(a_ + 1) * 128, :], [], [key] if a_ == 0 else []))
                P.dma_group("pool", P.stream("pc_%d_%s" % (i, kind)), items)
        if 0 not in layers:
            for i in layers:
                precast(i)
        nmod = 48 if defer_ada1 else 96
        for v in range(2):
            pv = Acc(P.psum[MODP][:, v:2 * nmod:2], {("ps", MODP)})
            P.tt("dve", MOD[:, 0:nmod, v], pv, COLB[:, 0:nmod], ALU.add)

        def der(i):
            for v in range(2):
                b = i * 48
                ng = lambda jn: COLA[:, 16 + (i * 4 + jn) * 8: 16 + (i * 4 + jn) * 8 + 8]
                P.stt(DER[:, i, v, 0, :], MOD[:, b + 8:b + 16, v], 1.0, ng(0), ALU.add, ALU.mult)
                P.tt("dve", DER[:, i, v, 1, :], MOD[:, b + 16:b + 24, v], ng(1), ALU.mult)
                P.stt(DER[:, i, v, 2, :], MOD[:, b + 32:b + 40, v], 1.0, ng(2), ALU.add, ALU.mult)
                P.tt("dve", DER[:, i, v, 3, :], MOD[:, b + 40:b + 48, v], ng(3), ALU.mult)
        der(0)
        if not defer_ada1:
            der(1)

        def A1(i, v, k): return DER[:, i, v, 0, k:k + 1]
        def G1(i, v, k): return DER[:, i, v, 1, k:k + 1]
        def A2(i, v, k): return DER[:, i, v, 2, k:k + 1]
        def G2(i, v, k): return DER[:, i, v, 3, k:k + 1]
        def B1(i, v, k): return MOD[:, i * 48 + k, v:v + 1]
        def B2(i, v, k): return MOD[:, i * 48 + 24 + k, v:v + 1]

        if 1 in layers:
            P.ts("dve", LCOL[:, 0, :], COLC[:, 50:70], 0.5, ALU.mult)
            P.ts("dve", LCOL[:, 1, :], COLC[:, 70:90], 0.5, ALU.mult)
            P.act(LCOL[:, 4, :], COLC[:, 90:110], AF.Exp, scale=-1.0)
            P.ts("dve", LCOL[:, 4, :], LCOL[:, 4, :], 1.0, ALU.add)
            P.act(LCOL[:, 4, :], LCOL[:, 4, :], AF.Ln)
            P.ts("dve", LCOL[:, 2, :], LCOL[:, 4, :], -8.0, ALU.mult)
            P.ts("dve", LCOL[:, 3, :], LCOL[:, 4, :], -4.0, ALU.mult)

        for tb in range(TT // 128):
            sl = tb % 2
            src = x_d[tb * 128:(tb + 1) * 128, :] if tb < 16 else ctx_d[(tb - 16) * 128:(tb - 15) * 128, :]
            P.dma("sp", s_xs[sl], XS[sl].full[:, :], src, writes=[XS[sl][:, :]])
            for h in range(2):
                pb = P.ps_rot()
                for q in range(4):
                    k = h * 4 + q
                    o = P.ps(pb, cols=128, c0=q * 128)
                    P.op("pe", lambda o=o, k=k, sl=sl: nc.tensor.transpose(o.ap, XS[sl].full[:, k * 128:(k + 1) * 128], IDENT.full[:, :]),
                         reads=[XS[sl][:, k * 128:(k + 1) * 128], IDENT[:, :]], writes=[o])
                full = Acc(P.psum[pb][:, :].rearrange("p (q t) -> p q t", q=4), {("ps", pb)})
                P.copy("dve" if h == 0 else "act", XT[:, h * 4:(h + 1) * 4, tb * 128:(tb + 1) * 128], full)
        P.release(mk_setup)

        PSS = 7

        def norm_block(i, t0, n, v, which, H, Hk=None, ssb=None, RSTD=RSTD):
            if Hk is None:
                Hk = lambda k: H[:, k, 0:n]
            Af = A1 if which == 1 else A2
            Bf = B1 if which == 1 else B2
            ss = P.ps(PSS if ssb is None else ssb, cols=n)
            for k in range(KC):
                sq = SQ[k % 2]
                P.act(sq[:, 0:n], XT[:, k, t0:t0 + n], AF.Square)
                P.mm(ss, ONESB[:, :], sq[:, 0:n], start=(k == 0), stop=(k == KC - 1))
            P.act(RSTD[:, 0:n], ss, AF.Ln, bias=EPSC[:, 0:1], scale=1.0)
            P.act(RSTD[:, 0:n], RSTD[:, 0:n], AF.Exp, scale=-0.5)
            for k in range(KC):
                tmp = TMP[k % 2]
                P.stt(tmp[:, 0:n], XT[:, k, t0:t0 + n], Af(i, v, k), RSTD[:, 0:n], ALU.mult, ALU.mult)
                P.ts("dve", Hk(k), tmp[:, 0:n], Bf(i, v, k), ALU.add)

        def resid_update(i, t0, n, v, which, Y, ss):
            Gf = G1 if which == 1 else G2
            P.act(RSTD[:, 0:n], ss, AF.Ln, bias=EPSC[:, 0:1], scale=1.0)
            P.act(RSTD[:, 0:n], RSTD[:, 0:n], AF.Exp, scale=-0.5)
            for k in range(KC):
                tmp = TMP[k % 2]
                P.stt(tmp[:, 0:n], Y[:, k, 0:n], Gf(i, v, k), RSTD[:, 0:n], ALU.mult, ALU.mult)
                P.tt("pool", XT[:, k, t0:t0 + n], XT[:, k, t0:t0 + n], tmp[:, 0:n], ALU.add)

        def proj_out_stats(Wfn, nk, rhsfn, n, Y):
            ss = P.ps(PSS, cols=n)
            pend = None
            for m in range(KC):
                pb = P.ps_rot()
                o = P.ps(pb, cols=n)
                for k in range(nk):
                    P.mm(o, Wfn(k, m), rhsfn(k), start=(k == 0), stop=(k == nk - 1))
                if pend is not None:
                    P.mm(ss, ONESB[:, :], pend[0], start=(pend[1] == 0), stop=False)
                sq = SQ[m % 2]
                P.act(sq[:, 0:n], o, AF.Square)
                P.copy("dve", Y[:, m, 0:n], o)
                pend = (sq[:, 0:n], m)
            P.mm(ss, ONESB[:, :], pend[0], start=False, stop=True)
            return ss

        out_state = {"OS": None, "done": set()}

        def out_tokens(tb):
            OS = out_state["OS"]
            sl = tb % 2
            for h in range(2):
                pb = P.ps_rot()
                for q in range(4):
                    k = h * 4 + q
                    o = P.ps(pb, cols=128, c0=q * 128)
                    src = XT[:, k, tb * 128:(tb + 1) * 128]
                    P.op("pe", lambda o=o, src=src: nc.tensor.transpose(o.ap, src.ap, IDENT.full[:, :]),
                         reads=[src, IDENT[:, :]], writes=[o])
                P.copy("dve" if h == 0 else "act", OS[sl][:, h * 512:(h + 1) * 512], P.ps(pb))
            dst = out_d[tb * 128:(tb + 1) * 128, :] if tb < 16 else outc_d[(tb - 16) * 128:(tb - 15) * 128, :]
            P.dma("sp", s_st[sl], dst, OS[sl].full[:, :], reads=[OS[sl][:, :]])
            out_state["done"].add(tb)

        def mlp(i, blocks, ada_groups=(), emit_out=False):
            mk = P.mark()
            if emit_out:
                out_state["OS"] = [P.alloc("OS%d" % q, [D], F32) for q in range(2)]
            H2 = P.alloc("H2", [KC, 512], BF16)
            G = P.alloc("G", [32, 512], BF16)
            Y = P.alloc("Ym", [KC, 512], F32)
            R1 = [P.alloc("R1_%d" % q, [512], BF16) for q in range(2)]
            RSTDn = P.alloc("RSTDn", [512], F32)
            loads = []
            ada_left = list(ada_groups)
            ada_sched = []
            for bi in range(len(blocks)):
                loads += [("w1", g) for g in range(4)] + [("w2", g) for g in range(4)]
                mine = ada_left[:2]
                ada_left = ada_left[2:]
                ada_sched.append(mine)
                loads += [("ada", g) for g in mine]
            assert not ada_left
            slots = {}

            def issue(L):
                if L >= len(loads) or L in slots:
                    return
                kind, g = loads[L]
                sl = wslot()
                slots[L] = sl
                W = WS[sl]
                key = Acc(None, {("scr", i, kind)})
                if kind == "ada":
                    P.dma("pool", s_w[sl], W.full[:, :, :],
                          adaw_d[1, :, g * 1024:(g + 1) * 1024].rearrange("(k p) n -> p k n", p=128),
                          writes=[W[:, :, :]])
                elif kind == "w1":
                    P.dma("sp", s_wh[sl], W.full[:, :, :],
                          w1s_d[i, :, g * 1024:(g + 1) * 1024].rearrange("(k p) n -> p k n", p=128),
                          reads=[key], writes=[W[:, :, :]])
                else:
                    P.dma("sp", s_wh[sl], W.full.rearrange("p a (b c) -> p (a b) c", c=256),
                          w2s_d[i, :, g * 256:(g + 1) * 256].rearrange("(k p) n -> p k n", p=128),
                          reads=[key], writes=[W[:, :, :]])
            issue(0)
            issue(1)
            L = 0
            norm_block(i, blocks[0][0], blocks[0][1], blocks[0][2], 2, H2, ssb=6, RSTD=RSTDn)
            for bi, (t0, n, v) in enumerate(blocks):
                for g in range(4):
                    issue(L + 2)
                    W = WS[slots[L]]
                    L += 1
                    for jj in range(8):
                        pb = P.ps_rot()
                        o = P.ps(pb, cols=n)
                        for k in range(KC):
                            P.mm(o, W[:, k, jj * 128:(jj + 1) * 128], H2[:, k, 0:n], start=(k == 0), stop=(k == KC - 1))
                        r1 = R1[jj % 2]
                        P.act(r1[:, 0:n], o, AF.Relu)
                        P.tt("dve", G[:, g * 8 + jj, 0:n], r1[:, 0:n], r1[:, 0:n], ALU.mult)
                if bi + 1 < len(blocks):
                    nb_ = blocks[bi + 1]
                    norm_block(i, nb_[0], nb_[1], nb_[2], 2, H2, ssb=6, RSTD=RSTDn)
                ss = P.ps(PSS, cols=n)
                pend2 = None
                for g in range(4):
                    issue(L + 2)
                    W = WS[slots[L]]
                    L += 1
                    Wv = Acc(W.full.rearrange("p a (b c) -> p (a b) c", c=256), W[:, :, :].keys)
                    for mm_ in range(2):
                        m = g * 2 + mm_
                        pb = P.ps_rot()
                        o = P.ps(pb, cols=n)
                        for k in range(32):
                            lw = Acc(Wv.ap[:, k, mm_ * 128:(mm_ + 1) * 128], Wv.keys)
                            P.mm(o, lw, G[:, k, 0:n], start=(k == 0), stop=(k == 31))
                        if pend2 is not None:
                            P.mm(ss, ONESB[:, :], pend2[0], start=(pend2[1] == 0), stop=False)
                        sq = SQ[m % 2]
                        P.act(sq[:, 0:n], o, AF.Square)
                        P.copy("dve", Y[:, m, 0:n], o)
                        pend2 = (sq[:, 0:n], m)
                P.mm(ss, ONESB[:, :], pend2[0], start=False, stop=True)
                resid_update(i, t0, n, v, 2, Y, ss)
                if emit_out and bi > 0:
                    pt0, pn = blocks[bi - 1][0], blocks[bi - 1][1]
                    for tb in range(pt0 // 128, (pt0 + pn) // 128):
                        out_tokens(tb)
                for g in ada_sched[bi]:
                    issue(L + 2)
                    W = WS[slots[L]]
                    L += 1
                    pb = P.ps_rot()
                    for jj in range(8):
                        o = P.ps(pb, cols=2, c0=2 * jj)
                        for k in range(KC):
                            P.mm(o, W[:, k, jj * 128:(jj + 1) * 128], SCB[:, k:16:8], start=(k == 0), stop=(k == KC - 1))
                    for v_ in range(2):
                        pv = Acc(P.psum[pb][:, v_:16:2], {("ps", pb)})
                        P.tt("dve", MOD[:, 48 + g * 8:48 + g * 8 + 8, v_], pv, COLB[:, 48 + g * 8:48 + g * 8 + 8], ALU.add)
            if emit_out:
                pt0, pn = blocks[-1][0], blocks[-1][1]
                for tb in range(pt0 // 128, (pt0 + pn) // 128):
                    out_tokens(tb)
            P.release(mk)

        if 0 in layers:
            i = 0
            mk0 = P.mark()
            RT = P.alloc("RT", [128], BF16)
            MASK = P.alloc("MASK", [2, 4, 128], BF16)
            COS = P.alloc("COS", [T], BF16)
            SIN = P.alloc("SIN", [T], BF16)
            KTb = P.alloc("KT", [2, TT], BF16)
            Vb = P.alloc("V", [TT // 128, 256], BF16)
            WQ, WO, WKV = WS[0], WS[1], WS[2]
            AB = 256
            H = P.alloc("H", [KC, AB], BF16)
            QTb = [P.alloc("QT%d" % q, [2, AB // 128, 4, 128], BF16) for q in range(2)]
            QC = P.alloc("QC", [2, AB // 128, 4, 128], BF16)
            ATb = P.alloc("ATb", [KC, AB], BF16)
            Y = P.alloc("Y", [KC, AB], F32)
            QB = [P.alloc("QB%d" % q, [AB], BF16) for q in range(2)]
            T1 = [P.alloc("T1_%d" % q, [AB], F32) for q in range(2)]
            T2 = [P.alloc("T2_%d" % q, [AB], F32) for q in range(2)]
            PT = [P.alloc("PT%d" % q, [512], BF16) for q in range(10)]
            ptc = [0]
            ZS = P.alloc("ZS", [256], F32)
            mkc = P.mark()
            CSM2 = Buf(P.arena, "CSM2", T1[0].off, [512], F32)
            CST = Buf(P.arena, "CST", Y.off, [T], F32)
            P.dma("sp", P.stream("csm2"), CSM2.full[:, :], csm_d, writes=[CSM2[:, :]])
            P.copy("dve", RT[:, :], CSM2[:, 128:256])
            for q in range(4):
                P.copy("dve", MASK[:, 0, q, :], CSM2[:, 256:384])
                P.copy("dve", MASK[:, 1, q, :], CSM2[:, 384:512])
            P.dma("sp", P.stream("cst"), CST.full[:, :], cos_d, writes=[CST[:, :]])
            P.copy("dve", COS[:, :], CST[:, :])
            P.dma("sp", P.stream("cst"), CST.full[:, :], sin_d, writes=[CST[:, :]])
            P.copy("dve", SIN[:, :], CST[:, :])
            P.release(mkc)
            WQ6 = WQ.full.rearrange("p k (gp g two d) -> p k gp g two d", gp=2, g=4, two=2)
            items = []
            for gp_ in range(2):
                for two_ in range(2):
                    for g_ in range(4):
                        hb = ((2 * gp_ + two_) * 4 + g_) * 64
                        items.append((WQ6[:, :, gp_, g_, two_, :], wqkv_d[:, hb:hb + 64].rearrange("(k p) d -> p k d", p=128),
                                      [], [WQ[:, :, :]] if not items else []))
            P.dma_group("pool", P.stream("wq"), items)
            P.dma("pool", P.stream("wkv"), WKV.full[:, :, 0:512], wqkv_d[:, 1024:1536].rearrange("(k p) n -> p k n", p=128), writes=[WKV[:, :, :]])
            P.dma("pool", P.stream("wo"), WO.full[:, :, :], wo_d.rearrange("(k p) n -> p k n", p=128), writes=[WO[:, :, :]])

            rr2 = [0]

            rope_pend = [None]

            def rope_flush():
                if rope_pend[0] is not None:
                    f = rope_pend[0]
                    rope_pend[0] = None
                    f()

            def rope_evac(o, n, t0, dst, is_x, split=False):
                def vw(a):
                    if not split:
                        return a
                    return Acc(a.ap.rearrange("p (a b) -> p a b", b=128), a.keys)
                if not is_x:
                    P.copy("act", dst, vw(o))
                    return
                q = rr2[0]
                rr2[0] ^= 1
                P.copy("act", QB[q][:, 0:n], o)
                P.tt("dve", T1[q][:, 0:n], o, COS[:, t0:t0 + n], ALU.mult)

                def fin():
                    pb = P.ps_rot()
                    ro = P.ps(pb, cols=n)
                    P.mm(ro, RT[:, :], QB[q][:, 0:n], start=True, stop=True)
                    P.tt("dve", T2[q][:, 0:n], ro, SIN[:, t0:t0 + n], ALU.mult)
                    P.tt("pool", dst, vw(T1[q][:, 0:n]), vw(T2[q][:, 0:n]), ALU.add)
                prevf = rope_pend[0]
                rope_pend[0] = fin
                if prevf is not None:
                    prevf()

            def kvq(t0, n, v, QT, do_norm=True):
                is_x = (v == 0)
                if do_norm:
                    norm_block(i, t0, n, v, 1, H)
                for kc in range(2):
                    pb = P.ps_rot()
                    o = P.ps(pb, cols=n)
                    for k in range(KC):
                        P.mm(o, WKV[:, k, kc * 128:(kc + 1) * 128], H[:, k, 0:n], start=(k == 0), stop=(k == KC - 1))
                    rope_evac(o, n, t0, KTb[:, kc, t0:t0 + n], is_x)
                for s in range(n // 128):
                    pb = P.ps_rot()
                    o = P.ps(pb, cols=256)
                    for k in range(KC):
                        P.mm(o, H[:, k, s * 128:(s + 1) * 128], WKV[:, k, 256:512], start=(k == 0), stop=(k == KC - 1))
                    P.copy("dve", Vb[:, t0 // 128 + s, :], o)
                    rope_flush()
                for j in range(8):
                    gp, g = j // 4, j % 4
                    sidx = (g % 2) * 2 + g // 2
                    pb = P.ps_rot()
                    o = P.ps(pb, cols=n)
                    for k in range(KC):
                        P.mm(o, WQ[:, k, j * 128:(j + 1) * 128], H[:, k, 0:n], start=(k == 0), stop=(k == KC - 1))
                    rope_evac(o, n, t0, QT[:, gp, 0:n // 128, sidx, :], is_x, split=True)
                rope_flush()

            obank = [0]

            def attention(t0, n, v, QT, mid_hook=None):
                its = []
                for qi in range(n // 128):
                    q0 = t0 + qi * 128
                    if v == 0:
                        qb = q0 // 128
                        kchunks = []
                        if qb > 0:
                            kchunks.append((qb - 1, 0))
                        kchunks.append((qb, None))
                        if qb < 15:
                            kchunks.append((qb + 1, 1))
                        kchunks += [(16, None), (17, None)]
                    else:
                        kchunks = [(16, None), (17, None)]
                    for kvh in range(4):
                        its.append((qi, kvh, kchunks))

                def emit_S(qi, kvh, kchunks, pts):
                    half = (kvh % 2) * 64
                    kc = kvh // 2
                    gp = kvh // 2
                    for ci, (kch, msk) in enumerate(kchunks):
                        pb = P.ps_rot()
                        so = P.ps(pb, cols=512)
                        lk = Acc(KTb.full[half:half + 64, kc, kch * 128:(kch + 1) * 128], KTb[:, kc, kch * 128:(kch + 1) * 128].keys)
                        rq = Acc(QT.full[half:half + 64, gp, qi, :, :].rearrange("p s t -> p (s t)"), QT[:, gp, qi, :, :].keys)
                        P.mm(so, lk, rq, start=True, stop=True)
                        pt = PT[ptc[0] % len(PT)]
                        ptc[0] += 1
                        P.act(pt[:, :], so, AF.Exp, scale=0.125)
                        if msk is not None:
                            mv = Acc(MASK.full[:, msk, :, :].rearrange("p g t -> p (g t)"), MASK[:, msk, :, :].keys)
                            P.tt("pool", pt[:, :], pt[:, :], mv, ALU.mult)
                        pts.append((pt, kch))
                        yield

                def emit_PV(qi, kvh, pts):
                    ob = 3 + obank[0]
                    zb = 5 + obank[0]
                    obank[0] ^= 1
                    nch = len(pts)
                    for ci, (pt, kch) in enumerate(pts):
                        vv = Acc(Vb.full[:, kch, kvh * 64:(kvh + 1) * 64], Vb[:, kch, :].keys)
                        last = (ci == nch - 1)
                        for hf in range(2):
                            rp = pt[:, hf * 256:(hf + 1) * 256]
                            oo = Acc(P.psum[ob][hf * 64:(hf + 1) * 64, 0:256], {("ps", ob)})
                            P.op("pe", lambda oo=oo, vv=vv, rp=rp, ci=ci, last=last, hf=hf: nc.tensor.matmul(
                                oo.ap, vv.ap, rp.ap, start=(ci == 0), stop=last, tile_position=(0, hf * 64)),
                                reads=[vv, rp], writes=[oo], inc=True)
                        for hf in range(2):
                            rp = pt[:, hf * 256:(hf + 1) * 256]
                            zz = Acc(P.psum[zb][hf * 64:(hf + 1) * 64, 0:256], {("ps", zb)})
                            P.op("pe", lambda zz=zz, rp=rp, ci=ci, last=last, hf=hf: nc.tensor.matmul(
                                zz.ap, ONES64.full[:, :], rp.ap, start=(ci == 0), stop=last, tile_position=(0, hf * 64)),
                                reads=[ONES64[:, :], rp], writes=[zz], inc=True)
                        yield
                    for gg in range(2):
                        P.ts("dve", ZS[:, gg * 128:(gg + 1) * 128], P.ps(zb, cols=128, c0=gg * 128),
                             ES[:, kvh * 2 + gg:kvh * 2 + gg + 1], ALU.add)
                    P.act(ZS[:, :], ZS[:, :], AF.Ln)
                    P.act(ZS[:, :], ZS[:, :], AF.Exp, scale=-1.0)
                    dst = ATb[:, 2 * kvh:2 * kvh + 2, qi * 128:(qi + 1) * 128]
                    o3 = Acc(P.psum[ob][:, 0:256].rearrange("p (g t) -> p g t", g=2), {("ps", ob)})
                    z3 = Acc(ZS.full.rearrange("p (g t) -> p g t", g=2), ZS[:, :].keys)
                    P.tt("dve", dst, o3, z3, ALU.mult)

                def run(*gens):
                    gens = [g for g in gens if g is not None]
                    while gens:
                        for g in list(gens):
                            try:
                                next(g)
                            except StopIteration:
                                gens.remove(g)
                prev = None
                for idx, (qi, kvh, kchunks) in enumerate(its):
                    pts = []
                    run(emit_S(qi, kvh, kchunks, pts), emit_PV(*prev) if prev is not None else None)
                    prev = (qi, kvh, pts)
                run(emit_PV(*prev))
                if mid_hook is not None:
                    mid_hook()

            def att_block(t0, n, v, QT, mid_hook=None):
                attention(t0, n, v, QT, mid_hook)
                ss = proj_out_stats(lambda k, m: WO[:, k, m * 128:(m + 1) * 128], KC, lambda k: ATb[:, k, 0:n], n, Y)
                resid_update(i, t0, n, v, 1, Y, ss)

            xb = [(b * AB, AB, 0) for b in range(T // AB)]
            cb = (T, TCX, 1)
            kvq(*cb, QC)
            kvq(*xb[0], QTb[0])
            kvq(*xb[1], QTb[1])
            for b in range(len(xb)):
                nxt = xb[b + 2] if b + 2 < len(xb) else None
                hook = (lambda nxt=nxt: norm_block(i, nxt[0], nxt[1], nxt[2], 1, H)) if nxt is not None else None
                att_block(*xb[b], QTb[b % 2], mid_hook=hook)
                if nxt is not None:
                    kvq(*nxt, QTb[b % 2], do_norm=False)
                if b == 0:
                    precast(0)
                if b == 4 and 1 in layers:
                    precast(1)
            att_block(*cb, QC)
            P.release(mk0)
            blocks = [(b * 512, 512, 0) for b in range(4)]
            if 1 in layers or debug_out:
                blocks.append((T, TCX, 1))
            mlp(0, blocks, ada_groups=(list(range(6)) if defer_ada1 else ()))
            if defer_ada1:
                der(1)

        if 1 in layers:
            i = 1
            mk1 = P.mark()
            PROD = P.alloc("PROD", [10, T], BF16)
            mkh = P.mark()
            HT = P.alloc("HT", [KC, T], BF16)
            def xpiece(k, half, shape, dt):
                return Buf(P.arena, "xp", XT.off + (k * TT + T) * 4 + half * 512, shape, dt)
            HTC = [xpiece(k, 0, [TCX], BF16) for k in range(KC)]
            WGP = [xpiece(k, 1, [256], BF16) for k in range(KC)]

            def hsl(k, t0, n):
                return HT[:, k, t0:t0 + n] if t0 < T else HTC[k][:, 0:n]
            VPW = 2320
            XO, CO = 2, 2058
            base = WS[0].off
            AA = Buf(P.arena, "AA", base, [TT], F32)
            WB = [Buf(P.arena, "WBf", base + 9216, [TT], F32), Buf(P.arena, "WBb", base + 2 * 9216, [TT], F32)]
            VP = Buf(P.arena, "VP", base + 3 * 9216, [2, VPW], BF16)
            S1 = Buf(P.arena, "S1", base + 3 * 9216, [TT], F32)
            U = Buf(P.arena, "U", base + 3 * 9216 + 9472, [2, TT], BF16)
            DG = Buf(P.arena, "DG", base + 3 * 9216 + 9472 + 9216, [8, 128], BF16)
            assert base + 3 * 9216 + 9472 + 9216 + 2048 <= WS[2].off + 16384
            WR = Buf(P.arena, "WR", SQ[0].off, [KC, 256], BF16)
            assert SQ[1].off == SQ[0].off + 1024 and RSTD.off == SQ[0].off + 2048
            TR, TI = TMP[0], TMP[1]
            allb = [(T, TCX, 1)] + [(b * 512, 512, 0) for b in range(4)]
            for (t0, n, v) in allb:
                norm_block(i, t0, n, v, 1, None, Hk=(lambda k, t0=t0, n=n: hsl(k, t0, n)))

            def vcol(t0):
                return (XO + t0) if t0 < T else (CO + t0 - T)

            AAC = P.alloc("AAC", [TCX], F32)
            pending = [None]

            def aa_acc(d, t0, n):
                if d == 0:
                    return AA[:, t0:t0 + n]
                if t0 >= T:
                    return AAC[:, 0:n]
                return P.ps(3 + t0 // 512, cols=n)

            def make_scan(d, ch):
                W_ = WB[d]

                def seg(j):
                    if d == 0:
                        if j == 0:
                            P.op("dve", lambda: nc.vector.tensor_tensor_scan(
                                out=W_.full[:, T:TT], data0=AA.full[:, T:TT], data1=W_.full[:, T:TT], initial=0.0,
                                op0=ALU.mult, op1=ALU.add), reads=[AA[:, T:TT], W_[:, T:TT]], writes=[W_[:, T:TT]])
                        else:
                            b_ = j - 1
                            init = W_[:, TT - 1:TT] if b_ == 0 else W_[:, b_ * 512 - 1:b_ * 512]
                            sl_ = slice(b_ * 512, (b_ + 1) * 512)
                            P.op("dve", lambda: nc.vector.tensor_tensor_scan(
                                out=W_.full[:, sl_], data0=AA.full[:, sl_], data1=W_.full[:, sl_], initial=init.ap,
                                op0=ALU.mult, op1=ALU.add), reads=[AA[:, sl_], W_[:, sl_], init], writes=[W_[:, sl_]])
                    else:
                        if j == 0:
                            P.op("dve", lambda: nc.vector.tensor_tensor_scan(
                                out=W_.full[:, TT - 1:T - 1:-1], data0=AAC.full[:, TCX - 1::-1], data1=W_.full[:, TT - 1:T - 1:-1], initial=0.0,
                                op0=ALU.mult, op1=ALU.add), reads=[AAC[:, :], W_[:, T:TT]], writes=[W_[:, T:TT]])
                        else:
                            b_ = 4 - j
                            init = W_[:, T:T + 1] if b_ == 3 else W_[:, (b_ + 1) * 512:(b_ + 1) * 512 + 1]
                            lo, hi = b_ * 512, (b_ + 1) * 512
                            rs = slice(hi - 1, lo - 1 if lo > 0 else None, -1)
                            aap = P.ps(3 + b_)
                            P.op("dve", lambda: nc.vector.tensor_tensor_scan(
                                out=W_.full[:, rs], data0=P.psum[3 + b_][:, 511::-1], data1=W_.full[:, rs], initial=init.ap,
                                op0=ALU.mult, op1=ALU.add), reads=[aap, W_[:, lo:hi], init], writes=[W_[:, lo:hi]])
                            sl_ = slice(lo, hi)
                            P.tt("dve", WB[0][:, sl_], WB[0][:, sl_], WB[1][:, sl_], ALU.add)
                            P.tt("dve", PROD[:, ch, sl_], PROD[:, ch, sl_], WB[0][:, sl_], ALU.mult)
                return seg

            for nb in range(5):
                items = []
                for dd in range(2):
                    for gt, wsrc in enumerate((wa_d, wi_d)):
                        for kc in range(2):
                            pc = WGP[(dd * 2 + gt) * 2 + kc]
                            items.append((pc.full[:, :], wsrc[dd, nb, kc * 128:(kc + 1) * 128, :], [], [pc[:, :]]))
                P.dma_group("pool", P.stream("wg"), items)
                for c2 in range(2):
                    ch = nb * 2 + c2
                    for tap in range(4):
                        P.ts("dve", DG[:, c2 * 4 + tap, :], IDB[:, :], COLC[:, tap * 10 + ch:tap * 10 + ch + 1], ALU.mult)
                P.dma("pool", P.stream("wr"), WR.full[:, :, :], win_d[:, nb * 256:(nb + 1) * 256].rearrange("(k p) n -> p k n", p=128), writes=[WR[:, :, :]])
                for (t0, n, v) in allb[1:]:
                    for c2 in range(2):
                        pb = P.ps_rot()
                        o = P.ps(pb, cols=n)
                        for k in range(KC):
                            P.mm(o, WR[:, k, c2 * 128:(c2 + 1) * 128], hsl(k, t0, n), start=(k == 0), stop=(k == KC - 1))
                        P.act(PROD[:, nb * 2 + c2, t0:t0 + n], o, AF.Gelu_apprx_tanh)
                P.dma("pool", P.stream("wr"), WR.full[:, :, :], win_d[:, DR + nb * 256:DR + (nb + 1) * 256].rearrange("(k p) n -> p k n", p=128), writes=[WR[:, :, :]])
                for c2_ in range(2):
                    P.memset("dve", VP[:, c2_, 0:XO], 0.0)
                    P.memset("dve", VP[:, c2_, XO + T:CO], 0.0)
                    P.memset("dve", VP[:, c2_, CO + TCX:VPW], 0.0)
                for (t0, n, v) in allb:
                    for c2 in range(2):
                        pb = P.ps_rot()
                        o = P.ps(pb, cols=n)
                        for k in range(KC):
                            P.mm(o, WR[:, k, c2 * 128:(c2 + 1) * 128], hsl(k, t0, n), start=(k == 0), stop=(k == KC - 1))
                        P.copy("act", VP[:, c2, vcol(t0):vcol(t0) + n], o)
                for (t0, n, v) in allb:
                    for c2 in range(2):
                        pb = P.ps_rot()
                        o = P.ps(pb, cols=n)
                        for tap in range(4):
                            c0 = vcol(t0) + tap - 2
                            P.mm(o, DG[:, c2 * 4 + tap, :], VP[:, c2, c0:c0 + n], start=(tap == 0), stop=(tap == 3))
                        ch = nb * 2 + c2
                        P.act(U[:, c2, t0:t0 + n], o, AF.Identity, bias=COLC[:, 40 + ch:40 + ch + 1], scale=1.0)
                for oc in range(2):
                    ch = nb * 2 + oc
                    for d in range(2):
                        col = d * 10 + ch
                        fwd = allb
                        bwd = [allb[0]] + allb[:0:-1]
                        order = fwd if d == 1 else bwd
                        for j, (t0, n, v) in enumerate(order):
                            if pending[0] is not None:
                                pending[0](j)
                            pb = P.ps_rot()
                            o = P.ps(pb, cols=n)
                            for kc in range(2):
                                P.mm(o, WGP[(d * 2 + 0) * 2 + kc][:, oc * 128:(oc + 1) * 128], U[:, kc, t0:t0 + n], start=(kc == 0), stop=(kc == 1))
                            P.act(TR[:, 0:n], o, AF.Tanh, bias=LCOL[:, 0, col:col + 1], scale=0.5)
                            P.act(aa_acc(d, t0, n), TR[:, 0:n], AF.Exp, bias=LCOL[:, 3, col:col + 1], scale=LCOL[:, 3, col:col + 1])
                            a2p = P.ps(P.ps_rot(), cols=n)
                            P.act(a2p, aa_acc(d, t0, n), AF.Square)
                            pb = P.ps_rot()
                            o = P.ps(pb, cols=n)
                            for kc in range(2):
                                P.mm(o, WGP[(d * 2 + 1) * 2 + kc][:, oc * 128:(oc + 1) * 128], U[:, kc, t0:t0 + n], start=(kc == 0), stop=(kc == 1))
                            P.act(TI[:, 0:n], o, AF.Tanh, bias=LCOL[:, 1, col:col + 1], scale=0.5)
                            P.ts("dve", S1[:, t0:t0 + n], a2p, -1.0, ALU.mult, 1.0, ALU.add)
                            P.stt(WB[d][:, t0:t0 + n], TI[:, 0:n], 1.0, U[:, oc, t0:t0 + n], ALU.add, ALU.mult)
                        pending[0] = None
                        for (t0, n, v) in allb:
                            P.act(S1[:, t0:t0 + n], S1[:, t0:t0 + n], AF.Sqrt)
                        for (t0, n, v) in allb:
                            P.stt(WB[d][:, t0:t0 + n], S1[:, t0:t0 + n], 0.5, WB[d][:, t0:t0 + n], ALU.mult, ALU.mult)
                        pending[0] = make_scan(d, ch)
            for j in range(5):
                pending[0](j)
            pending[0] = None
            P.release(mkh)
            Y1 = P.alloc("Y1", [KC, 512], F32)
            P.dma("pool", s_w[0], WS[0].full[:, :, :], wout_d[0:1024, :].rearrange("(k p) n -> p k n", p=128), writes=[WS[0][:, :, :]])
            P.dma("pool", s_w[1], WS[1].full[:, 0:2, :], wout_d[1024:1280, :].rearrange("(k p) n -> p k n", p=128), writes=[WS[1][:, 0:2, :]])

            def wout(k, m):
                return WS[0][:, k, m * 128:(m + 1) * 128] if k < 8 else WS[1][:, k - 8, m * 128:(m + 1) * 128]
            for b in range(4):
                t0 = b * 512
                ss = proj_out_stats(wout, 10, lambda k: PROD[:, k, t0:t0 + 512], 512, Y1)
                resid_update(i, t0, 512, 0, 1, Y1, ss)
            P.release(mk1)
            mlp(1, [(b * 512, 512, 0) for b in range(4)], emit_out=True)

        mko = P.mark()
        out_state["OS"] = [P.alloc("OSt%d" % q, [D], F32) for q in range(2)]
        nout = TT // 128 if debug_out else T // 128
        for tb in range(nout):
            if tb not in out_state["done"]:
                out_tokens(tb)
        for sl in range(2):
            nc.sync.wait_ge(P.sem[s_st[sl]], P.cnt[s_st[sl]])
        P.release(mko)
    return nc


_CACHE = {}


def _inputs_for_core(b, inp, consts):
    f = lambda a: np.ascontiguousarray(a, dtype=np.float32)
    csm, cos2, sin2 = consts
    return {
        "x": f(inp["x"][b]),
        "c": f(inp["c"][b].reshape(8, 128)),
        "ctx": f(inp["ctx"][b]),
        "c_ctx": f(inp["c_ctx"].reshape(8, 128)),
        "ada_w": f(inp["ada_w"]),
        "ada_b": f(inp["ada_b"].reshape(96, 128)),
        "norm_g": f(inp["norm_g"].reshape(64, 128)),
        "mlp_w1": f(inp["mlp_w1"]),
        "mlp_w2": f(inp["mlp_w2"]),
        "attn_w_qkv": f(inp["attn_w_qkv"][0]),
        "attn_w_o": f(inp["attn_w_o"][0]),
        "attn_sink": f(inp["attn_sink"]),
        "lru_w_in": f(inp["lru_w_in"][0]),
        "lru_conv_w": f(inp["lru_conv_w"][0].reshape(40, 128)),
        "lru_conv_b": f(inp["lru_conv_b"][0].reshape(10, 128)),
        "lru_w_a": f(inp["lru_w_a"][0]),
        "lru_b_a": f(inp["lru_b_a"][0].reshape(20, 128)),
        "lru_w_i": f(inp["lru_w_i"][0]),
        "lru_b_i": f(inp["lru_b_i"][0].reshape(20, 128)),
        "lru_lam": f(inp["lru_lam"][0].reshape(20, 128)),
        "lru_w_out": f(inp["lru_w_out"][0]),
        "k_small": csm, "k_cos": cos2, "k_sin": sin2,
    }


def kernel(**inputs):
    inp = {k: np.asarray(v) for k, v in inputs.items()}
    consts = host_consts()
    if "nc" not in _CACHE:
        _CACHE["nc"] = build((0, 1))
    nc = _CACHE["nc"]
    in_maps = [_inputs_for_core(b, inp, consts) for b in range(8)]
    res = run_bass_kernel_spmd(nc, in_maps, core_ids=list(range(8)))
    out = np.stack([np.asarray(r["out"], dtype=np.float32) for r in res.results], axis=0)
    return out
```

```python
import numpy as np
import ml_dtypes
import concourse.bass as bass
import concourse.mybir as mybir
from concourse.bass_utils import run_bass_kernel_spmd

F32 = mybir.dt.float32
BF16 = mybir.dt.bfloat16
AF = mybir.ActivationFunctionType
ALU = mybir.AluOpType

D = 1024
KC = 8
T = 2048
TCX = 256
TT = T + TCX
DFF = 4096
DR = 1280
EPS = 1e-6
GRAN = 256
ARENA_WORDS = 53000


def _prod(s):
    r = 1
    for v in s:
        r *= v
    return r


class Acc:
    __slots__ = ("ap", "keys")

    def __init__(self, ap, keys):
        self.ap = ap
        self.keys = keys


class Buf:
    def __init__(self, arena, name, off, shape, dt):
        self.name = name
        self.off = off
        self.shape = tuple(shape)
        self.dt = dt
        self.esz = 4 if dt == F32 else 2
        nel = _prod(shape)
        nbytes = nel * self.esz
        assert off % 4 == 0 and nbytes % 4 == 0
        w = arena[:, off // 4:(off + nbytes) // 4]
        if dt != F32:
            w = w.bitcast(dt)
        if len(shape) == 2:
            w = w.rearrange("p (a b) -> p a b", a=shape[0])
        elif len(shape) == 3:
            w = w.rearrange("p (a b c) -> p a b c", a=shape[0], b=shape[1])
        elif len(shape) == 4:
            w = w.rearrange("p (a b c d) -> p a b c d", a=shape[0], b=shape[1], c=shape[2])
        self.full = w
        st = []
        s = 1
        for d in reversed(self.shape):
            st.append(s)
            s *= d
        self.strides = tuple(reversed(st))
        self.nbytes = nbytes

    def __getitem__(self, idx):
        if not isinstance(idx, tuple):
            idx = (idx,)
        ap = self.full[idx]
        fidx = list(idx[1:])
        while len(fidx) < len(self.shape):
            fidx.append(slice(None))
        lohi = []
        for d, ix in enumerate(fidx):
            n = self.shape[d]
            if isinstance(ix, int):
                lohi.append((ix, ix))
            else:
                a, b, c = ix.indices(n)
                if c > 0:
                    last = a + ((b - a - 1) // c) * c
                    lohi.append((a, last))
                else:
                    last = a + ((a - b - 1) // (-c)) * c
                    lohi.append((last, a))
        ranges = [(0, 0)]
        outer = lohi[:-1]
        ncomb = _prod([h - l + 1 for l, h in outer]) if outer else 1
        keys = set()
        if ncomb > 256:
            lo = sum(l * s for (l, h), s in zip(lohi, self.strides))
            hi = sum(h * s for (l, h), s in zip(lohi, self.strides))
            b0 = self.off + lo * self.esz
            b1 = self.off + (hi + 1) * self.esz
            keys.update(range(b0 // GRAN, (b1 - 1) // GRAN + 1))
        else:
            def rec(d, base):
                if d == len(lohi) - 1:
                    l, h = lohi[d]
                    b0 = self.off + (base + l) * self.esz
                    b1 = self.off + (base + h + 1) * self.esz
                    keys.update(range(b0 // GRAN, (b1 - 1) // GRAN + 1))
                    return
                l, h = lohi[d]
                for i in range(l, h + 1):
                    rec(d + 1, base + i * self.strides[d])
            rec(0, 0)
        return Acc(ap, keys)


class Prog:
    def __init__(self, nc, stack):
        self.nc = nc
        self.stack = stack
        self.eng = {"pe": nc.tensor, "act": nc.scalar, "dve": nc.vector, "pool": nc.gpsimd, "sp": nc.sync}
        self.sem = {}
        self.cnt = {}
        self.waited = {e: {} for e in self.eng}
        self.lastw = {}
        self.readers = {}
        for e in ("pe", "act", "dve", "pool"):
            self._mksem(e)
        self.arena = stack.enter_context(nc.sbuf_tensor("arena", [128, ARENA_WORDS], F32))
        self.top = 0
        self.tops = []
        self.psum = [stack.enter_context(nc.psum_tensor("ps%d" % i, [128, 512], F32)) for i in range(8)]
        self.rr = 0

    def _mksem(self, name):
        self.sem[name] = self.stack.enter_context(self.nc.semaphore(name.replace(":", "_")))
        self.cnt[name] = 0

    def stream(self, name):
        n = "dma:" + name
        if n not in self.sem:
            self._mksem(n)
        return n

    def alloc(self, name, shape, dt):
        esz = 4 if dt == F32 else 2
        nbytes = _prod(shape) * esz
        nbytes = (nbytes + GRAN - 1) // GRAN * GRAN
        off = self.top
        self.top += nbytes
        assert self.top <= ARENA_WORDS * 4, ("arena overflow", name, self.top)
        self.tops.append((self.top, name))
        return Buf(self.arena, name, off, shape, dt)

    def mark(self):
        return self.top

    def release(self, m):
        self.top = m

    def ps(self, bank, cols=512, rows=128, c0=0, r0=0):
        ap = self.psum[bank][r0:r0 + rows, c0:c0 + cols]
        return Acc(ap, {("ps", bank)})

    def ps_rot(self):
        b = self.rr
        self.rr = (self.rr + 1) % 3
        return b

    def _need(self, eng, reads, writes):
        need = {}

        def add(t, kind):
            if t is None:
                return
            s, v = t
            if s == eng and eng in ("pe", "sp"):
                return
            if need.get(s, 0) < v:
                need[s] = v
        for a in reads:
            for k in a.keys:
                add(self.lastw.get(k), "raw")
        for a in writes:
            for k in a.keys:
                add(self.lastw.get(k), "waw")
                r = self.readers.get(k)
                if r:
                    for s, v in r.items():
                        add((s, v), "war")
        return need

    def _wait(self, eng, need):
        w = self.waited[eng]
        for s, v in need.items():
            if w.get(s, 0) < v:
                self.eng[eng].wait_ge(self.sem[s], v)
                w[s] = v

    def _record(self, t, reads, writes):
        s, v = t
        for a in reads:
            for k in a.keys:
                r = self.readers.setdefault(k, {})
                if r.get(s, 0) < v:
                    r[s] = v
        for a in writes:
            for k in a.keys:
                self.lastw[k] = t
                self.readers[k] = {}

    def op(self, eng, fn, reads=(), writes=(), inc=True):
        psr = [a for a in reads if any(isinstance(k, tuple) and k[0] == "ps" for k in a.keys)]
        if psr:
            writes = list(writes) + psr
        self._wait(eng, self._need(eng, reads, writes))
        ins = fn()
        if inc:
            self.cnt[eng] += 1
            ins.then_inc(self.sem[eng], 1)
            t = (eng, self.cnt[eng])
        else:
            t = (eng, self.cnt[eng] + 1)
        self._record(t, reads, writes)

    def dma(self, queue, stream, out, in_, reads=(), writes=()):
        self._wait(queue, self._need(queue, reads, writes))
        ins = self.eng[queue].dma_start(out=out, in_=in_)
        self.cnt[stream] += 16
        ins.then_inc(self.sem[stream], 16)
        self._record((stream, self.cnt[stream]), reads, writes)

    def dma_group(self, queue, stream, items):
        allr, allw = [], []
        for o, i, r, w in items:
            allr += list(r)
            allw += list(w)
        self._wait(queue, self._need(queue, allr, allw))
        for o, i, r, w in items:
            ins = self.eng[queue].dma_start(out=o, in_=i)
            self.cnt[stream] += 16
            ins.then_inc(self.sem[stream], 16)
        self._record((stream, self.cnt[stream]), allr, allw)

    def mm(self, out, lhsT, rhs, start, stop, **kw):
        self.op("pe", lambda: self.nc.tensor.matmul(out.ap, lhsT.ap, rhs.ap, start=start, stop=stop, **kw),
                reads=[lhsT, rhs], writes=[out], inc=True)

    def act(self, out, in_, func, bias=None, scale=None, extra_reads=()):
        kw = {}
        rd = [in_] + list(extra_reads)
        if bias is not None:
            if isinstance(bias, Acc):
                kw["bias"] = bias.ap
                rd.append(bias)
            else:
                kw["bias"] = bias
        if scale is not None:
            if isinstance(scale, Acc):
                kw["scale"] = scale.ap
                rd.append(scale)
            else:
                kw["scale"] = scale
        self.op("act", lambda: self.nc.scalar.activation(out=out.ap, in_=in_.ap, func=func, **kw),
                reads=rd, writes=[out])

    def tt(self, eng, out, a, b, op):
        self.op(eng, lambda: self.eng[eng].tensor_tensor(out=out.ap, in0=a.ap, in1=b.ap, op=op),
                reads=[a, b], writes=[out])

    def ts(self, eng, out, a, s1, op0, s2=None, op1=None):
        rd = [a]
        v1 = s1.ap if isinstance(s1, Acc) else s1
        v2 = s2.ap if isinstance(s2, Acc) else s2
        if isinstance(s1, Acc):
            rd.append(s1)
        if isinstance(s2, Acc):
            rd.append(s2)
        if op1 is None:
            fn = lambda: self.eng[eng].tensor_scalar(out=out.ap, in0=a.ap, scalar1=v1, scalar2=None, op0=op0)
        else:
            fn = lambda: self.eng[eng].tensor_scalar(out=out.ap, in0=a.ap, scalar1=v1, scalar2=v2, op0=op0, op1=op1)
        self.op(eng, fn, reads=rd, writes=[out])

    def stt(self, out, a, s, b, op0, op1):
        rd = [a, b]
        sv = s.ap if isinstance(s, Acc) else s
        if isinstance(s, Acc):
            rd.append(s)
        self.op("dve", lambda: self.nc.vector.scalar_tensor_tensor(out=out.ap, in0=a.ap, scalar=sv, in1=b.ap, op0=op0, op1=op1),
                reads=rd, writes=[out])

    def copy(self, eng, out, in_):
        if eng == "act":
            self.act(out, in_, AF.Copy)
        else:
            self.op(eng, lambda: self.eng[eng].tensor_copy(out=out.ap, in_=in_.ap), reads=[in_], writes=[out])

    def memset(self, eng, out, val):
        self.op(eng, lambda: self.eng[eng].memset(out.ap, val), reads=[], writes=[out])

    def recip(self, out, in_):
        self.op("dve", lambda: self.nc.vector.reciprocal(out=out.ap, in_=in_.ap), reads=[in_], writes=[out])


def host_consts():
    ident = np.eye(128, dtype=np.float32)
    R = np.zeros((64, 64), np.float32)
    for i in range(16):
        R[i, 16 + i] = -1.0
        R[16 + i, i] = 1.0
        R[32 + i, 48 + i] = -1.0
        R[48 + i, 32 + i] = 1.0
    R2 = np.zeros((128, 128), np.float32)
    R2[:64, :64] = R
    R2[64:, 64:] = R
    RT = np.ascontiguousarray(R2.T)
    j = np.arange(128)[:, None]
    i = np.arange(128)[None, :]
    maskL = (j >= i).astype(np.float32)
    maskU = (j <= i).astype(np.float32)
    rows = T // 64
    row = np.repeat(np.arange(rows, dtype=np.float32), 64)
    col = np.tile(np.arange(64, dtype=np.float32), rows)
    inv = (10000.0 ** (-np.arange(0, 32, 2, dtype=np.float32) / 32)).astype(np.float32)
    ang_r = row[:, None] * inv[None, :]
    ang_c = col[:, None] * inv[None, :]
    ang = np.concatenate([ang_r, ang_r, ang_c, ang_c], axis=-1).astype(np.float32)
    cos = np.cos(ang).astype(np.float32).T
    sin = np.sin(ang).astype(np.float32).T
    cos2 = np.concatenate([cos, cos], axis=0)
    sin2 = np.concatenate([sin, sin], axis=0)
    small = np.concatenate([ident, RT, maskL, maskU], axis=1)
    return np.ascontiguousarray(small), np.ascontiguousarray(cos2), np.ascontiguousarray(sin2)


def build(layers=(0, 1), debug_out=False):
    from contextlib import ExitStack
    nc = bass.Bass("TRN2", target_bir_lowering=False)
    dr = {}

    def din(name, shape):
        dr[name] = nc.dram_tensor(name, list(shape), F32, kind="ExternalInput").ap()
        return dr[name]
    x_d = din("x", [T, D])
    c_d = din("c", [8, 128])
    ctx_d = din("ctx", [TCX, D])
    cctx_d = din("c_ctx", [8, 128])
    adaw_d = din("ada_w", [2, D, 6 * D])
    adab_d = din("ada_b", [96, 128])
    ng_d = din("norm_g", [64, 128])
    w1_d = din("mlp_w1", [2, D, DFF])
    w2_d = din("mlp_w2", [2, DFF, D])
    wqkv_d = din("attn_w_qkv", [D, 1536])
    wo_d = din("attn_w_o", [D, D])
    sink_d = din("attn_sink", [1, 16])
    win_d = din("lru_w_in", [D, 2 * DR])
    convw_d = din("lru_conv_w", [40, 128])
    convb_d = din("lru_conv_b", [10, 128])
    wa_d = din("lru_w_a", [2, 5, 256, 256])
    ba_d = din("lru_b_a", [20, 128])
    wi_d = din("lru_w_i", [2, 5, 256, 256])
    bi_d = din("lru_b_i", [20, 128])
    lam_d = din("lru_lam", [20, 128])
    wout_d = din("lru_w_out", [DR, D])
    csm_d = din("k_small", [128, 512])
    cos_d = din("k_cos", [128, T])
    sin_d = din("k_sin", [128, T])
    out_d = nc.dram_tensor("out", [T, D], F32, kind="ExternalOutput").ap()
    w1s_d = nc.dram_tensor("w1s", [2, D, DFF], BF16).ap()
    w2s_d = nc.dram_tensor("w2s", [2, DFF, D], BF16).ap()
    if debug_out:
        outc_d = nc.dram_tensor("outc", [TCX, D], F32, kind="ExternalOutput").ap()

    with ExitStack() as stack:
        P = Prog(nc, stack)
        s_ld = P.stream("ld")
        s_w = [P.stream("w%d" % i) for i in range(3)]
        s_wh = [P.stream("wh%d" % i) for i in range(3)]
        s_xs = [P.stream("xs%d" % i) for i in range(2)]
        s_st = [P.stream("st%d" % i) for i in range(2)]
        s_w2 = P.stream("wsmall")

        XT = P.alloc("XT", [KC, TT], F32)
        IDENT = P.alloc("IDENT", [128], F32)
        IDB = P.alloc("IDB", [128], BF16)
        ONESB = P.alloc("ONESB", [128], BF16)
        ONES64 = P.alloc("ONES64", [64], BF16)
        EPSC = P.alloc("EPSC", [1], F32)
        COLA = P.alloc("COLA", [80], F32)
        MOD = P.alloc("MOD", [96, 2], F32)
        DER = P.alloc("DER", [2, 2, 4, 8], F32)
        COLC = P.alloc("COLC", [110], F32)
        LCOL = P.alloc("LCOL", [5, 20], F32)
        ES = P.alloc("ES", [8], F32)
        SQ = [P.alloc("SQ%d" % i, [512], BF16) for i in range(2)]
        RSTD = P.alloc("RSTD", [512], F32)
        TMP = [P.alloc("TMP%d" % i, [512], F32) for i in range(2)]
        WS = [P.alloc("WS%d" % i, [8, 1024], BF16) for i in range(3)]
        wrr = [0]

        def wslot():
            i = wrr[0]
            wrr[0] = (i + 1) % 3
            return i

        P.memset("dve", ONESB[:, :], 1.0 / 1024.0)
        P.memset("dve", ONES64[:, :], 1.0)
        P.memset("dve", EPSC[:, :], EPS)

        COLB = P.alloc("COLB", [96], F32)
        SCB = P.alloc("SCB", [16], BF16)
        mk_setup = P.mark()
        CSM = P.alloc("CSM", [512], F32)
        ROWA = P.alloc("ROWA", [128], F32)
        ROWB = P.alloc("ROWB", [128], F32)
        ROWC = P.alloc("ROWC", [128], F32)
        XS = [P.alloc("XS%d" % i, [D], F32) for i in range(2)]

        P.dma("sp", P.stream("ld_a"), CSM.full[:, :], csm_d, writes=[CSM[:, :]])
        P.dma_group("sp", P.stream("ld_b"), [
            (ROWA.full[0:8, :], c_d, [], [ROWA[:, :]]),
            (ROWA.full[8:16, :], cctx_d, [], []),
            (ROWA.full[16:80, :], ng_d, [], []),
            (ROWB.full[0:96, :], adab_d, [], [ROWB[:, :]]),
            (ROWC.full[0:40, :], convw_d, [], [ROWC[:, :]]),
            (ROWC.full[40:50, :], convb_d, [], []),
            (ROWC.full[50:70, :], ba_d, [], []),
            (ROWC.full[70:90, :], bi_d, [], []),
            (ROWC.full[90:110, :], lam_d, [], []),
        ])
        sk = sink_d.rearrange("o (a h) -> o a h", h=2)
        with nc.allow_non_contiguous_dma(reason="tiny sink broadcast"):
            P.dma_group("sp", P.stream("ld_c"), [
                (ES.full[0:64, :], sk[:, :, 0].to_broadcast([64, 8]), [], [ES[:, :]]),
                (ES.full[64:128, :], sk[:, :, 1].to_broadcast([64, 8]), [], []),
            ])
        P.copy("dve", IDENT[:, :], CSM[:, 0:128])
        P.copy("dve", IDB[:, :], CSM[:, 0:128])
        P.act(ES[:, :], ES[:, :], AF.Exp)

        def rows2cols(ROW, nrows, COL):
            pb = P.ps_rot()
            o = P.ps(pb, cols=nrows)
            P.op("pe", lambda: nc.tensor.transpose(o.ap, ROW.full[0:nrows, :], IDENT.full[0:nrows, 0:nrows]),
                 reads=[ROW[:, :], IDENT[:, :]], writes=[o])
            P.copy("dve", COL[:, 0:nrows], o)
        rows2cols(ROWA, 80, COLA)
        rows2cols(ROWB, 96, COLB)
        rows2cols(ROWC, 110, COLC)
        P.act(SCB[:, :], COLA[:, 0:16], AF.Silu)

        MODP = 7
        defer_ada1 = (0 in layers and 1 in layers)
        for i in ([0] if defer_ada1 else [0, 1]):
            for g in range(6):
                sl = wslot()
                W = WS[sl]
                P.dma("pool", s_w[sl], W.full[:, :, :],
                      adaw_d[i, :, g * 1024:(g + 1) * 1024].rearrange("(k p) n -> p k n", p=128),
                      writes=[W[:, :, :]])
                for jj in range(8):
                    j = i * 48 + g * 8 + jj
                    o = P.ps(MODP, cols=2, c0=2 * j)
                    for k in range(KC):
                        P.mm(o, W[:, k, jj * 128:(jj + 1) * 128], SCB[:, k:16:8], start=(k == 0), stop=(k == KC - 1))
        def precast(i):
            for kind, src, dst in (("w1", w1_d[i], w1s_d[i]), ("w2", w2_d[i].rearrange("(a b) n -> a (b n)", b=4), w2s_d[i].rearrange("(a b) n -> a (b n)", b=4))):
                key = Acc(None, {("scr", i, kind)})
                items = []
                for a_ in range(8):
                    items.append((dst[a_ * 128:(a_ + 1) * 128, :], src[a_ * 128:(a_ + 1) * 128, :], [], [key] if a_ == 0 else []))
                P.dma_group("pool", P.stream("pc_%d_%s" % (i, kind)), items)
        if 0 not in layers:
            for i in layers:
                precast(i)
        nmod = 48 if defer_ada1 else 96
        for v in range(2):
            pv = Acc(P.psum[MODP][:, v:2 * nmod:2], {("ps", MODP)})
            P.tt("dve", MOD[:, 0:nmod, v], pv, COLB[:, 0:nmod], ALU.add)

        def der(i):
            for v in range(2):
                b = i * 48
                ng = lambda jn: COLA[:, 16 + (i * 4 + jn) * 8: 16 + (i * 4 + jn) * 8 + 8]
                P.stt(DER[:, i, v, 0, :], MOD[:, b + 8:b + 16, v], 1.0, ng(0), ALU.add, ALU.mult)
                P.tt("dve", DER[:, i, v, 1, :], MOD[:, b + 16:b + 24, v], ng(1), ALU.mult)
                P.stt(DER[:, i, v, 2, :], MOD[:, b + 32:b + 40, v], 1.0, ng(2), ALU.add, ALU.mult)
                P.tt("dve", DER[:, i, v, 3, :], MOD[:, b + 40:b + 48, v], ng(3), ALU.mult)
        der(0)
        if not defer_ada1:
            der(1)

        def A1(i, v, k): return DER[:, i, v, 0, k:k + 1]
        def G1(i, v, k): return DER[:, i, v, 1, k:k + 1]
        def A2(i, v, k): return DER[:, i, v, 2, k:k + 1]
        def G2(i, v, k): return DER[:, i, v, 3, k:k + 1]
        def B1(i, v, k): return MOD[:, i * 48 + k, v:v + 1]
        def B2(i, v, k): return MOD[:, i * 48 + 24 + k, v:v + 1]

        if 1 in layers:
            P.ts("dve", LCOL[:, 0, :], COLC[:, 50:70], 0.5, ALU.mult)
            P.ts("dve", LCOL[:, 1, :], COLC[:, 70:90], 0.5, ALU.mult)
            P.act(LCOL[:, 4, :], COLC[:, 90:110], AF.Exp, scale=-1.0)
            P.ts("dve", LCOL[:, 4, :], LCOL[:, 4, :], 1.0, ALU.add)
            P.act(LCOL[:, 4, :], LCOL[:, 4, :], AF.Ln)
            P.ts("dve", LCOL[:, 2, :], LCOL[:, 4, :], -8.0, ALU.mult)
            P.ts("dve", LCOL[:, 3, :], LCOL[:, 4, :], -4.0, ALU.mult)

        for tb in range(TT // 128):
            sl = tb % 2
            src = x_d[tb * 128:(tb + 1) * 128, :] if tb < 16 else ctx_d[(tb - 16) * 128:(tb - 15) * 128, :]
            P.dma("sp", s_xs[sl], XS[sl].full[:, :], src, writes=[XS[sl][:, :]])
            for h in range(2):
                pb = P.ps_rot()
                for q in range(4):
                    k = h * 4 + q
                    o = P.ps(pb, cols=128, c0=q * 128)
                    P.op("pe", lambda o=o, k=k, sl=sl: nc.tensor.transpose(o.ap, XS[sl].full[:, k * 128:(k + 1) * 128], IDENT.full[:, :]),
                         reads=[XS[sl][:, k * 128:(k + 1) * 128], IDENT[:, :]], writes=[o])
                full = Acc(P.psum[pb][:, :].rearrange("p (q t) -> p q t", q=4), {("ps", pb)})
                P.copy("dve" if h == 0 else "act", XT[:, h * 4:(h + 1) * 4, tb * 128:(tb + 1) * 128], full)
        P.release(mk_setup)

        PSS = 7

        def norm_block(i, t0, n, v, which, H, Hk=None, ssb=None, RSTD=RSTD):
            if Hk is None:
                Hk = lambda k: H[:, k, 0:n]
            Af = A1 if which == 1 else A2
            Bf = B1 if which == 1 else B2
            ss = P.ps(PSS if ssb is None else ssb, cols=n)
            for k in range(KC):
                sq = SQ[k % 2]
                P.act(sq[:, 0:n], XT[:, k, t0:t0 + n], AF.Square)
                P.mm(ss, ONESB[:, :], sq[:, 0:n], start=(k == 0), stop=(k == KC - 1))
            P.act(RSTD[:, 0:n], ss, AF.Ln, bias=EPSC[:, 0:1], scale=1.0)
            P.act(RSTD[:, 0:n], RSTD[:, 0:n], AF.Exp, scale=-0.5)
            for k in range(KC):
                tmp = TMP[k % 2]
                P.stt(tmp[:, 0:n], XT[:, k, t0:t0 + n], Af(i, v, k), RSTD[:, 0:n], ALU.mult, ALU.mult)
                P.ts("dve", Hk(k), tmp[:, 0:n], Bf(i, v, k), ALU.add)

        def resid_update(i, t0, n, v, which, Y, ss):
            Gf = G1 if which == 1 else G2
            P.act(RSTD[:, 0:n], ss, AF.Ln, bias=EPSC[:, 0:1], scale=1.0)
            P.act(RSTD[:, 0:n], RSTD[:, 0:n], AF.Exp, scale=-0.5)
            for k in range(KC):
                tmp = TMP[k % 2]
                P.tt("dve", tmp[:, 0:n], Y[:, k, 0:n], RSTD[:, 0:n], ALU.mult)
                P.stt(XT[:, k, t0:t0 + n], tmp[:, 0:n], Gf(i, v, k), XT[:, k, t0:t0 + n], ALU.mult, ALU.add)

        def proj_out_stats(Wfn, nk, rhsfn, n, Y):
            ss = P.ps(PSS, cols=n)
            pend = None
            for m in range(KC):
                pb = P.ps_rot()
                o = P.ps(pb, cols=n)
                for k in range(nk):
                    P.mm(o, Wfn(k, m), rhsfn(k), start=(k == 0), stop=(k == nk - 1))
                if pend is not None:
                    P.mm(ss, ONESB[:, :], pend[0], start=(pend[1] == 0), stop=False)
                sq = SQ[m % 2]
                P.act(sq[:, 0:n], o, AF.Square)
                P.copy("dve", Y[:, m, 0:n], o)
                pend = (sq[:, 0:n], m)
            P.mm(ss, ONESB[:, :], pend[0], start=False, stop=True)
            return ss

        out_state = {"OS": None, "done": set()}

        def out_tokens(tb):
            OS = out_state["OS"]
            sl = tb % 2
            for h in range(2):
                pb = P.ps_rot()
                for q in range(4):
                    k = h * 4 + q
                    o = P.ps(pb, cols=128, c0=q * 128)
                    src = XT[:, k, tb * 128:(tb + 1) * 128]
                    P.op("pe", lambda o=o, src=src: nc.tensor.transpose(o.ap, src.ap, IDENT.full[:, :]),
                         reads=[src, IDENT[:, :]], writes=[o])
                P.copy("dve" if h == 0 else "act", OS[sl][:, h * 512:(h + 1) * 512], P.ps(pb))
            dst = out_d[tb * 128:(tb + 1) * 128, :] if tb < 16 else outc_d[(tb - 16) * 128:(tb - 15) * 128, :]
            P.dma("sp", s_st[sl], dst, OS[sl].full[:, :], reads=[OS[sl][:, :]])
            out_state["done"].add(tb)

        def mlp(i, blocks, ada_groups=(), emit_out=False):
            mk = P.mark()
            if emit_out:
                out_state["OS"] = [P.alloc("OS%d" % q, [D], F32) for q in range(2)]
            H2 = P.alloc("H2", [KC, 512], BF16)
            G = P.alloc("G", [32, 512], BF16)
            Y = P.alloc("Ym", [KC, 512], F32)
            R1 = [P.alloc("R1_%d" % q, [512], BF16) for q in range(2)]
            RSTDn = P.alloc("RSTDn", [512], F32)
            loads = []
            ada_left = list(ada_groups)
            ada_sched = []
            for bi in range(len(blocks)):
                loads += [("w1", g) for g in range(4)] + [("w2", g) for g in range(4)]
                mine = ada_left[:2]
                ada_left = ada_left[2:]
                ada_sched.append(mine)
                loads += [("ada", g) for g in mine]
            assert not ada_left
            slots = {}

            def issue(L):
                if L >= len(loads) or L in slots:
                    return
                kind, g = loads[L]
                sl = wslot()
                slots[L] = sl
                W = WS[sl]
                key = Acc(None, {("scr", i, kind)})
                if kind == "ada":
                    P.dma("pool", s_w[sl], W.full[:, :, :],
                          adaw_d[1, :, g * 1024:(g + 1) * 1024].rearrange("(k p) n -> p k n", p=128),
                          writes=[W[:, :, :]])
                elif kind == "w1":
                    P.dma("sp", s_wh[sl], W.full[:, :, :],
                          w1s_d[i, :, g * 1024:(g + 1) * 1024].rearrange("(k p) n -> p k n", p=128),
                          reads=[key], writes=[W[:, :, :]])
                else:
                    P.dma("sp", s_wh[sl], W.full.rearrange("p a (b c) -> p (a b) c", c=256),
                          w2s_d[i, :, g * 256:(g + 1) * 256].rearrange("(k p) n -> p k n", p=128),
                          reads=[key], writes=[W[:, :, :]])
            issue(0)
            issue(1)
            L = 0
            norm_block(i, blocks[0][0], blocks[0][1], blocks[0][2], 2, H2, ssb=6, RSTD=RSTDn)
            for bi, (t0, n, v) in enumerate(blocks):
                for g in range(4):
                    issue(L + 2)
                    W = WS[slots[L]]
                    L += 1
                    for jj in range(8):
                        pb = P.ps_rot()
                        o = P.ps(pb, cols=n)
                        for k in range(KC):
                            P.mm(o, W[:, k, jj * 128:(jj + 1) * 128], H2[:, k, 0:n], start=(k == 0), stop=(k == KC - 1))
                        r1 = R1[jj % 2]
                        P.act(r1[:, 0:n], o, AF.Relu)
                        P.tt("dve", G[:, g * 8 + jj, 0:n], r1[:, 0:n], r1[:, 0:n], ALU.mult)
                if bi + 1 < len(blocks):
                    nb_ = blocks[bi + 1]
                    norm_block(i, nb_[0], nb_[1], nb_[2], 2, H2, ssb=6, RSTD=RSTDn)
                ss = P.ps(PSS, cols=n)
                pend2 = None
                for g in range(4):
                    issue(L + 2)
                    W = WS[slots[L]]
                    L += 1
                    Wv = Acc(W.full.rearrange("p a (b c) -> p (a b) c", c=256), W[:, :, :].keys)
                    for mm_ in range(2):
                        m = g * 2 + mm_
                        pb = P.ps_rot()
                        o = P.ps(pb, cols=n)
                        for k in range(32):
                            lw = Acc(Wv.ap[:, k, mm_ * 128:(mm_ + 1) * 128], Wv.keys)
                            P.mm(o, lw, G[:, k, 0:n], start=(k == 0), stop=(k == 31))
                        if pend2 is not None:
                            P.mm(ss, ONESB[:, :], pend2[0], start=(pend2[1] == 0), stop=False)
                        sq = SQ[m % 2]
                        P.act(sq[:, 0:n], o, AF.Square)
                        P.copy("dve", Y[:, m, 0:n], o)
                        pend2 = (sq[:, 0:n], m)
                P.mm(ss, ONESB[:, :], pend2[0], start=False, stop=True)
                resid_update(i, t0, n, v, 2, Y, ss)
                if emit_out and bi > 0:
                    pt0, pn = blocks[bi - 1][0], blocks[bi - 1][1]
                    for tb in range(pt0 // 128, (pt0 + pn) // 128):
                        out_tokens(tb)
                for g in ada_sched[bi]:
                    issue(L + 2)
                    W = WS[slots[L]]
                    L += 1
                    pb = P.ps_rot()
                    for jj in range(8):
                        o = P.ps(pb, cols=2, c0=2 * jj)
                        for k in range(KC):
                            P.mm(o, W[:, k, jj * 128:(jj + 1) * 128], SCB[:, k:16:8], start=(k == 0), stop=(k == KC - 1))
                    for v_ in range(2):
                        pv = Acc(P.psum[pb][:, v_:16:2], {("ps", pb)})
                        P.tt("dve", MOD[:, 48 + g * 8:48 + g * 8 + 8, v_], pv, COLB[:, 48 + g * 8:48 + g * 8 + 8], ALU.add)
            if emit_out:
                pt0, pn = blocks[-1][0], blocks[-1][1]
                for tb in range(pt0 // 128, (pt0 + pn) // 128):
                    out_tokens(tb)
            P.release(mk)

        if 0 in layers:
            i = 0
            mk0 = P.mark()
            RT = P.alloc("RT", [128], BF16)
            MASK = P.alloc("MASK", [2, 4, 128], BF16)
            COS = P.alloc("COS", [T], BF16)
            SIN = P.alloc("SIN", [T], BF16)
            KTb = P.alloc("KT", [2, TT], BF16)
            Vb = P.alloc("V", [TT // 128, 256], BF16)
            WQ, WO, WKV = WS[0], WS[1], WS[2]
            AB = 256
            H = P.alloc("H", [KC, AB], BF16)
            QTb = [P.alloc("QT%d" % q, [2, AB // 128, 4, 128], BF16) for q in range(2)]
            QC = P.alloc("QC", [2, AB // 128, 4, 128], BF16)
            ATb = P.alloc("ATb", [KC, AB], BF16)
            Y = P.alloc("Y", [KC, AB], F32)
            QB = [P.alloc("QB%d" % q, [AB], BF16) for q in range(2)]
            T1 = [P.alloc("T1_%d" % q, [AB], F32) for q in range(2)]
            T2 = [P.alloc("T2_%d" % q, [AB], F32) for q in range(2)]
            PT = [P.alloc("PT%d" % q, [512], BF16) for q in range(10)]
            ptc = [0]
            ZS = P.alloc("ZS", [256], F32)
            mkc = P.mark()
            CSM2 = Buf(P.arena, "CSM2", T1[0].off, [512], F32)
            CST = Buf(P.arena, "CST", Y.off, [T], F32)
            P.dma("sp", P.stream("csm2"), CSM2.full[:, :], csm_d, writes=[CSM2[:, :]])
            P.copy("dve", RT[:, :], CSM2[:, 128:256])
            for q in range(4):
                P.copy("dve", MASK[:, 0, q, :], CSM2[:, 256:384])
                P.copy("dve", MASK[:, 1, q, :], CSM2[:, 384:512])
            P.dma("sp", P.stream("cst"), CST.full[:, :], cos_d, writes=[CST[:, :]])
            P.copy("dve", COS[:, :], CST[:, :])
            P.dma("sp", P.stream("cst"), CST.full[:, :], sin_d, writes=[CST[:, :]])
            P.copy("dve", SIN[:, :], CST[:, :])
            P.release(mkc)
            WQ6 = WQ.full.rearrange("p k (gp g two d) -> p k gp g two d", gp=2, g=4, two=2)
            items = []
            for gp_ in range(2):
                for two_ in range(2):
                    for g_ in range(4):
                        hb = ((2 * gp_ + two_) * 4 + g_) * 64
                        items.append((WQ6[:, :, gp_, g_, two_, :], wqkv_d[:, hb:hb + 64].rearrange("(k p) d -> p k d", p=128),
                                      [], [WQ[:, :, :]] if not items else []))
            P.dma_group("pool", P.stream("wq"), items)
            P.dma("pool", P.stream("wkv"), WKV.full[:, :, 0:512], wqkv_d[:, 1024:1536].rearrange("(k p) n -> p k n", p=128), writes=[WKV[:, :, :]])
            P.dma("pool", P.stream("wo"), WO.full[:, :, :], wo_d.rearrange("(k p) n -> p k n", p=128), writes=[WO[:, :, :]])

            rr2 = [0]

            rope_pend = [None]

            def rope_flush():
                if rope_pend[0] is not None:
                    f = rope_pend[0]
                    rope_pend[0] = None
                    f()

            def rope_evac(o, n, t0, dst, is_x, split=False):
                def vw(a):
                    if not split:
                        return a
                    return Acc(a.ap.rearrange("p (a b) -> p a b", b=128), a.keys)
                if not is_x:
                    P.copy("act", dst, vw(o))
                    return
                q = rr2[0]
                rr2[0] ^= 1
                P.copy("act", QB[q][:, 0:n], o)
                P.tt("dve", T1[q][:, 0:n], o, COS[:, t0:t0 + n], ALU.mult)

                def fin():
                    pb = P.ps_rot()
                    ro = P.ps(pb, cols=n)
                    P.mm(ro, RT[:, :], QB[q][:, 0:n], start=True, stop=True)
                    P.tt("dve", T2[q][:, 0:n], ro, SIN[:, t0:t0 + n], ALU.mult)
                    P.tt("pool", dst, vw(T1[q][:, 0:n]), vw(T2[q][:, 0:n]), ALU.add)
                prevf = rope_pend[0]
                rope_pend[0] = fin
                if prevf is not None:
                    prevf()

            def kvq(t0, n, v, QT, do_norm=True):
                is_x = (v == 0)
                if do_norm:
                    norm_block(i, t0, n, v, 1, H)
                for kc in range(2):
                    pb = P.ps_rot()
                    o = P.ps(pb, cols=n)
                    for k in range(KC):
                        P.mm(o, WKV[:, k, kc * 128:(kc + 1) * 128], H[:, k, 0:n], start=(k == 0), stop=(k == KC - 1))
                    rope_evac(o, n, t0, KTb[:, kc, t0:t0 + n], is_x)
                for s in range(n // 128):
                    pb = P.ps_rot()
                    o = P.ps(pb, cols=256)
                    for k in range(KC):
                        P.mm(o, H[:, k, s * 128:(s + 1) * 128], WKV[:, k, 256:512], start=(k == 0), stop=(k == KC - 1))
                    P.copy("dve", Vb[:, t0 // 128 + s, :], o)
                    rope_flush()
                for j in range(8):
                    gp, g = j // 4, j % 4
                    sidx = (g % 2) * 2 + g // 2
                    pb = P.ps_rot()
                    o = P.ps(pb, cols=n)
                    for k in range(KC):
                        P.mm(o, WQ[:, k, j * 128:(j + 1) * 128], H[:, k, 0:n], start=(k == 0), stop=(k == KC - 1))
                    rope_evac(o, n, t0, QT[:, gp, 0:n // 128, sidx, :], is_x, split=True)
                rope_flush()

            obank = [0]

            def attention(t0, n, v, QT, mid_hook=None):
                its = []
                for qi in range(n // 128):
                    q0 = t0 + qi * 128
                    if v == 0:
                        qb = q0 // 128
                        kchunks = []
                        if qb > 0:
                            kchunks.append((qb - 1, 0))
                        kchunks.append((qb, None))
                        if qb < 15:
                            kchunks.append((qb + 1, 1))
                        kchunks += [(16, None), (17, None)]
                    else:
                        kchunks = [(16, None), (17, None)]
                    for kvh in range(4):
                        its.append((qi, kvh, kchunks))

                def emit_S(qi, kvh, kchunks, pts):
                    half = (kvh % 2) * 64
                    kc = kvh // 2
                    gp = kvh // 2
                    for ci, (kch, msk) in enumerate(kchunks):
                        pb = P.ps_rot()
                        so = P.ps(pb, cols=512)
                        lk = Acc(KTb.full[half:half + 64, kc, kch * 128:(kch + 1) * 128], KTb[:, kc, kch * 128:(kch + 1) * 128].keys)
                        rq = Acc(QT.full[half:half + 64, gp, qi, :, :].rearrange("p s t -> p (s t)"), QT[:, gp, qi, :, :].keys)
                        P.mm(so, lk, rq, start=True, stop=True)
                        pt = PT[ptc[0] % len(PT)]
                        ptc[0] += 1
                        P.act(pt[:, :], so, AF.Exp, scale=0.125)
                        if msk is not None:
                            mv = Acc(MASK.full[:, msk, :, :].rearrange("p g t -> p (g t)"), MASK[:, msk, :, :].keys)
                            P.tt("pool", pt[:, :], pt[:, :], mv, ALU.mult)
                        pts.append((pt, kch))
                        yield

                def emit_PV(qi, kvh, pts):
                    ob = 3 + obank[0]
                    zb = 5 + obank[0]
                    obank[0] ^= 1
                    nch = len(pts)
                    for ci, (pt, kch) in enumerate(pts):
                        vv = Acc(Vb.full[:, kch, kvh * 64:(kvh + 1) * 64], Vb[:, kch, :].keys)
                        last = (ci == nch - 1)
                        for hf in range(2):
                            rp = pt[:, hf * 256:(hf + 1) * 256]
                            oo = Acc(P.psum[ob][hf * 64:(hf + 1) * 64, 0:256], {("ps", ob)})
                            P.op("pe", lambda oo=oo, vv=vv, rp=rp, ci=ci, last=last, hf=hf: nc.tensor.matmul(
                                oo.ap, vv.ap, rp.ap, start=(ci == 0), stop=last, tile_position=(0, hf * 64)),
                                reads=[vv, rp], writes=[oo], inc=True)
                        for hf in range(2):
                            rp = pt[:, hf * 256:(hf + 1) * 256]
                            zz = Acc(P.psum[zb][hf * 64:(hf + 1) * 64, 0:256], {("ps", zb)})
                            P.op("pe", lambda zz=zz, rp=rp, ci=ci, last=last, hf=hf: nc.tensor.matmul(
                                zz.ap, ONES64.full[:, :], rp.ap, start=(ci == 0), stop=last, tile_position=(0, hf * 64)),
                                reads=[ONES64[:, :], rp], writes=[zz], inc=True)
                        yield
                    for gg in range(2):
                        P.ts("dve", ZS[:, gg * 128:(gg + 1) * 128], P.ps(zb, cols=128, c0=gg * 128),
                             ES[:, kvh * 2 + gg:kvh * 2 + gg + 1], ALU.add)
                    P.act(ZS[:, :], ZS[:, :], AF.Ln)
                    P.act(ZS[:, :], ZS[:, :], AF.Exp, scale=-1.0)
                    dst = ATb[:, 2 * kvh:2 * kvh + 2, qi * 128:(qi + 1) * 128]
                    o3 = Acc(P.psum[ob][:, 0:256].rearrange("p (g t) -> p g t", g=2), {("ps", ob)})
                    z3 = Acc(ZS.full.rearrange("p (g t) -> p g t", g=2), ZS[:, :].keys)
                    P.tt("dve", dst, o3, z3, ALU.mult)

                def run(*gens):
                    gens = [g for g in gens if g is not None]
                    while gens:
                        for g in list(gens):
                            try:
                                next(g)
                            except StopIteration:
                                gens.remove(g)
                prev = None
                for idx, (qi, kvh, kchunks) in enumerate(its):
                    pts = []
                    run(emit_S(qi, kvh, kchunks, pts), emit_PV(*prev) if prev is not None else None)
                    prev = (qi, kvh, pts)
                run(emit_PV(*prev))
                if mid_hook is not None:
                    mid_hook()

            def att_block(t0, n, v, QT, mid_hook=None):
                attention(t0, n, v, QT, mid_hook)
                ss = proj_out_stats(lambda k, m: WO[:, k, m * 128:(m + 1) * 128], KC, lambda k: ATb[:, k, 0:n], n, Y)
                resid_update(i, t0, n, v, 1, Y, ss)

            xb = [(b * AB, AB, 0) for b in range(T // AB)]
            cb = (T, TCX, 1)
            kvq(*cb, QC)
            kvq(*xb[0], QTb[0])
            kvq(*xb[1], QTb[1])
            for b in range(len(xb)):
                nxt = xb[b + 2] if b + 2 < len(xb) else None
                hook = (lambda nxt=nxt: norm_block(i, nxt[0], nxt[1], nxt[2], 1, H)) if nxt is not None else None
                att_block(*xb[b], QTb[b % 2], mid_hook=hook)
                if nxt is not None:
                    kvq(*nxt, QTb[b % 2], do_norm=False)
                if b == 0:
                    precast(0)
                if b == 4 and 1 in layers:
                    precast(1)
            att_block(*cb, QC)
            P.release(mk0)
            blocks = [(b * 512, 512, 0) for b in range(4)]
            if 1 in layers or debug_out:
                blocks.append((T, TCX, 1))
            mlp(0, blocks, ada_groups=(list(range(6)) if defer_ada1 else ()))
            if defer_ada1:
                der(1)

        if 1 in layers:
            i = 1
            mk1 = P.mark()
            PROD = P.alloc("PROD", [10, T], BF16)
            mkh = P.mark()
            HT = P.alloc("HT", [KC, T], BF16)
            def xpiece(k, half, shape, dt):
                return Buf(P.arena, "xp", XT.off + (k * TT + T) * 4 + half * 512, shape, dt)
            HTC = [xpiece(k, 0, [TCX], BF16) for k in range(KC)]
            WGP = [xpiece(k, 1, [256], BF16) for k in range(KC)]

            def hsl(k, t0, n):
                return HT[:, k, t0:t0 + n] if t0 < T else HTC[k][:, 0:n]
            VPW = 2320
            XO, CO = 2, 2058
            base = WS[0].off
            AA = Buf(P.arena, "AA", base, [TT], F32)
            WB = [Buf(P.arena, "WBf", base + 9216, [TT], F32), Buf(P.arena, "WBb", base + 2 * 9216, [TT], F32)]
            VP = Buf(P.arena, "VP", base + 3 * 9216, [2, VPW], BF16)
            S1 = Buf(P.arena, "S1", base + 3 * 9216, [TT], F32)
            U = Buf(P.arena, "U", base + 3 * 9216 + 9472, [2, TT], BF16)
            DG = Buf(P.arena, "DG", base + 3 * 9216 + 9472 + 9216, [8, 128], BF16)
            assert base + 3 * 9216 + 9472 + 9216 + 2048 <= WS[2].off + 16384
            WR = Buf(P.arena, "WR", SQ[0].off, [KC, 256], BF16)
            assert SQ[1].off == SQ[0].off + 1024 and RSTD.off == SQ[0].off + 2048
            TR, TI = TMP[0], TMP[1]
            allb = [(T, TCX, 1)] + [(b * 512, 512, 0) for b in range(4)]
            for (t0, n, v) in allb:
                norm_block(i, t0, n, v, 1, None, Hk=(lambda k, t0=t0, n=n: hsl(k, t0, n)))

            def vcol(t0):
                return (XO + t0) if t0 < T else (CO + t0 - T)

            AAC = P.alloc("AAC", [TCX], F32)
            pending = [None]

            def aa_acc(d, t0, n):
                if d == 0:
                    return AA[:, t0:t0 + n]
                if t0 >= T:
                    return AAC[:, 0:n]
                return P.ps(3 + t0 // 512, cols=n)

            def make_scan(d, ch):
                W_ = WB[d]

                def seg(j):
                    if d == 0:
                        if j == 0:
                            P.op("dve", lambda: nc.vector.tensor_tensor_scan(
                                out=W_.full[:, T:TT], data0=AA.full[:, T:TT], data1=W_.full[:, T:TT], initial=0.0,
                                op0=ALU.mult, op1=ALU.add), reads=[AA[:, T:TT], W_[:, T:TT]], writes=[W_[:, T:TT]])
                        else:
                            b_ = j - 1
                            init = W_[:, TT - 1:TT] if b_ == 0 else W_[:, b_ * 512 - 1:b_ * 512]
                            sl_ = slice(b_ * 512, (b_ + 1) * 512)
                            P.op("dve", lambda: nc.vector.tensor_tensor_scan(
                                out=W_.full[:, sl_], data0=AA.full[:, sl_], data1=W_.full[:, sl_], initial=init.ap,
                                op0=ALU.mult, op1=ALU.add), reads=[AA[:, sl_], W_[:, sl_], init], writes=[W_[:, sl_]])
                    else:
                        if j == 0:
                            P.op("dve", lambda: nc.vector.tensor_tensor_scan(
                                out=W_.full[:, TT - 1:T - 1:-1], data0=AAC.full[:, TCX - 1::-1], data1=W_.full[:, TT - 1:T - 1:-1], initial=0.0,
                                op0=ALU.mult, op1=ALU.add), reads=[AAC[:, :], W_[:, T:TT]], writes=[W_[:, T:TT]])
                        else:
                            b_ = 4 - j
                            init = W_[:, T:T + 1] if b_ == 3 else W_[:, (b_ + 1) * 512:(b_ + 1) * 512 + 1]
                            lo, hi = b_ * 512, (b_ + 1) * 512
                            rs = slice(hi - 1, lo - 1 if lo > 0 else None, -1)
                            aap = P.ps(3 + b_)
                            P.op("dve", lambda: nc.vector.tensor_tensor_scan(
                                out=W_.full[:, rs], data0=P.psum[3 + b_][:, 511::-1], data1=W_.full[:, rs], initial=init.ap,
                                op0=ALU.mult, op1=ALU.add), reads=[aap, W_[:, lo:hi], init], writes=[W_[:, lo:hi]])
                            sl_ = slice(lo, hi)
                            P.tt("dve", WB[0][:, sl_], WB[0][:, sl_], WB[1][:, sl_], ALU.add)
                            P.tt("dve", PROD[:, ch, sl_], PROD[:, ch, sl_], WB[0][:, sl_], ALU.mult)
                return seg

            for nb in range(5):
                items = []
                for dd in range(2):
                    for gt, wsrc in enumerate((wa_d, wi_d)):
                        for kc in range(2):
                            pc = WGP[(dd * 2 + gt) * 2 + kc]
                            items.append((pc.full[:, :], wsrc[dd, nb, kc * 128:(kc + 1) * 128, :], [], [pc[:, :]]))
                P.dma_group("pool", P.stream("wg"), items)
                for c2 in range(2):
                    ch = nb * 2 + c2
                    for tap in range(4):
                        P.ts("dve", DG[:, c2 * 4 + tap, :], IDB[:, :], COLC[:, tap * 10 + ch:tap * 10 + ch + 1], ALU.mult)
                P.dma("pool", P.stream("wr"), WR.full[:, :, :], win_d[:, nb * 256:(nb + 1) * 256].rearrange("(k p) n -> p k n", p=128), writes=[WR[:, :, :]])
                for (t0, n, v) in allb[1:]:
                    for c2 in range(2):
                        pb = P.ps_rot()
                        o = P.ps(pb, cols=n)
                        for k in range(KC):
                            P.mm(o, WR[:, k, c2 * 128:(c2 + 1) * 128], hsl(k, t0, n), start=(k == 0), stop=(k == KC - 1))
                        P.act(PROD[:, nb * 2 + c2, t0:t0 + n], o, AF.Gelu_apprx_tanh)
                P.dma("pool", P.stream("wr"), WR.full[:, :, :], win_d[:, DR + nb * 256:DR + (nb + 1) * 256].rearrange("(k p) n -> p k n", p=128), writes=[WR[:, :, :]])
                for c2_ in range(2):
                    P.memset("dve", VP[:, c2_, 0:XO], 0.0)
                    P.memset("dve", VP[:, c2_, XO + T:CO], 0.0)
                    P.memset("dve", VP[:, c2_, CO + TCX:VPW], 0.0)
                for (t0, n, v) in allb:
                    for c2 in range(2):
                        pb = P.ps_rot()
                        o = P.ps(pb, cols=n)
                        for k in range(KC):
                            P.mm(o, WR[:, k, c2 * 128:(c2 + 1) * 128], hsl(k, t0, n), start=(k == 0), stop=(k == KC - 1))
                        P.copy("act", VP[:, c2, vcol(t0):vcol(t0) + n], o)
                for (t0, n, v) in allb:
                    for c2 in range(2):
                        pb = P.ps_rot()
                        o = P.ps(pb, cols=n)
                        for tap in range(4):
                            c0 = vcol(t0) + tap - 2
                            P.mm(o, DG[:, c2 * 4 + tap, :], VP[:, c2, c0:c0 + n], start=(tap == 0), stop=(tap == 3))
                        ch = nb * 2 + c2
                        P.act(U[:, c2, t0:t0 + n], o, AF.Identity, bias=COLC[:, 40 + ch:40 + ch + 1], scale=1.0)
                for oc in range(2):
                    ch = nb * 2 + oc
                    for d in range(2):
                        col = d * 10 + ch
                        fwd = allb
                        bwd = [allb[0]] + allb[:0:-1]
                        order = fwd if d == 1 else bwd
                        for j, (t0, n, v) in enumerate(order):
                            if pending[0] is not None:
                                pending[0](j)
                            pb = P.ps_rot()
                            o = P.ps(pb, cols=n)
                            for kc in range(2):
                                P.mm(o, WGP[(d * 2 + 0) * 2 + kc][:, oc * 128:(oc + 1) * 128], U[:, kc, t0:t0 + n], start=(kc == 0), stop=(kc == 1))
                            P.act(TR[:, 0:n], o, AF.Tanh, bias=LCOL[:, 0, col:col + 1], scale=0.5)
                            P.act(aa_acc(d, t0, n), TR[:, 0:n], AF.Exp, bias=LCOL[:, 3, col:col + 1], scale=LCOL[:, 3, col:col + 1])
                            a2p = P.ps(P.ps_rot(), cols=n)
                            P.act(a2p, aa_acc(d, t0, n), AF.Square)
                            pb = P.ps_rot()
                            o = P.ps(pb, cols=n)
                            for kc in range(2):
                                P.mm(o, WGP[(d * 2 + 1) * 2 + kc][:, oc * 128:(oc + 1) * 128], U[:, kc, t0:t0 + n], start=(kc == 0), stop=(kc == 1))
                            P.act(TI[:, 0:n], o, AF.Tanh, bias=LCOL[:, 1, col:col + 1], scale=0.5)
                            P.ts("dve", S1[:, t0:t0 + n], a2p, -1.0, ALU.mult, 1.0, ALU.add)
                            P.stt(WB[d][:, t0:t0 + n], TI[:, 0:n], 1.0, U[:, oc, t0:t0 + n], ALU.add, ALU.mult)
                        pending[0] = None
                        for (t0, n, v) in allb:
                            P.act(S1[:, t0:t0 + n], S1[:, t0:t0 + n], AF.Sqrt)
                        for (t0, n, v) in allb:
                            P.stt(WB[d][:, t0:t0 + n], S1[:, t0:t0 + n], 0.5, WB[d][:, t0:t0 + n], ALU.mult, ALU.mult)
                        pending[0] = make_scan(d, ch)
            for j in range(5):
                pending[0](j)
            pending[0] = None
            P.release(mkh)
            Y1 = P.alloc("Y1", [KC, 512], F32)
            P.dma("pool", s_w[0], WS[0].full[:, :, :], wout_d[0:1024, :].rearrange("(k p) n -> p k n", p=128), writes=[WS[0][:, :, :]])
            P.dma("pool", s_w[1], WS[1].full[:, 0:2, :], wout_d[1024:1280, :].rearrange("(k p) n -> p k n", p=128), writes=[WS[1][:, 0:2, :]])

            def wout(k, m):
                return WS[0][:, k, m * 128:(m + 1) * 128] if k < 8 else WS[1][:, k - 8, m * 128:(m + 1) * 128]
            for b in range(4):
                t0 = b * 512
                ss = proj_out_stats(wout, 10, lambda k: PROD[:, k, t0:t0 + 512], 512, Y1)
                resid_update(i, t0, 512, 0, 1, Y1, ss)
            P.release(mk1)
            mlp(1, [(b * 512, 512, 0) for b in range(4)], emit_out=True)

        mko = P.mark()
        out_state["OS"] = [P.alloc("OSt%d" % q, [D], F32) for q in range(2)]
        nout = TT // 128 if debug_out else T // 128
        for tb in range(nout):
            if tb not in out_state["done"]:
                out_tokens(tb)
        for sl in range(2):
            nc.sync.wait_ge(P.sem[s_st[sl]], P.cnt[s_st[sl]])
        P.release(mko)
    return nc


_CACHE = {}


def _inputs_for_core(b, inp, consts):
    f = lambda a: np.ascontiguousarray(a, dtype=np.float32)
    csm, cos2, sin2 = consts
    return {
        "x": f(inp["x"][b]),
        "c": f(inp["c"][b].reshape(8, 128)),
        "ctx": f(inp["ctx"][b]),
        "c_ctx": f(inp["c_ctx"].reshape(8, 128)),
        "ada_w": f(inp["ada_w"]),
        "ada_b": f(inp["ada_b"].reshape(96, 128)),
        "norm_g": f(inp["norm_g"].reshape(64, 128)),
        "mlp_w1": f(inp["mlp_w1"]),
        "mlp_w2": f(inp["mlp_w2"]),
        "attn_w_qkv": f(inp["attn_w_qkv"][0]),
        "attn_w_o": f(inp["attn_w_o"][0]),
        "attn_sink": f(inp["attn_sink"]),
        "lru_w_in": f(inp["lru_w_in"][0]),
        "lru_conv_w": f(inp["lru_conv_w"][0].reshape(40, 128)),
        "lru_conv_b": f(inp["lru_conv_b"][0].reshape(10, 128)),
        "lru_w_a": f(inp["lru_w_a"][0]),
        "lru_b_a": f(inp["lru_b_a"][0].reshape(20, 128)),
        "lru_w_i": f(inp["lru_w_i"][0]),
        "lru_b_i": f(inp["lru_b_i"][0].reshape(20, 128)),
        "lru_lam": f(inp["lru_lam"][0].reshape(20, 128)),
        "lru_w_out": f(inp["lru_w_out"][0]),
        "k_small": csm, "k_cos": cos2, "k_sin": sin2,
    }


def kernel(**inputs):
    inp = {k: np.asarray(v) for k, v in inputs.items()}
    consts = host_consts()
    if "nc" not in _CACHE:
        _CACHE["nc"] = build((0, 1))
    nc = _CACHE["nc"]
    in_maps = [_inputs_for_core(b, inp, consts) for b in range(8)]
    res = run_bass_kernel_spmd(nc, in_maps, core_ids=list(range(8)))
    out = np.stack([np.asarray(r["out"], dtype=np.float32) for r in res.results], axis=0)
    return out
```

```python
import numpy as np
import ml_dtypes
import concourse.bass as bass
import concourse.mybir as mybir
from concourse.bass_utils import run_bass_kernel_spmd

F32 = mybir.dt.float32
BF16 = mybir.dt.bfloat16
AF = mybir.ActivationFunctionType
ALU = mybir.AluOpType

D = 1024
KC = 8
T = 2048
TCX = 256
TT = T + TCX
DFF = 4096
DR = 1280
EPS = 1e-6
GRAN = 256
ARENA_WORDS = 53000


def _prod(s):
    r = 1
    for v in s:
        r *= v
    return r


class Acc:
    __slots__ = ("ap", "keys")

    def __init__(self, ap, keys):
        self.ap = ap
        self.keys = keys


class Buf:
    def __init__(self, arena, name, off, shape, dt):
        self.name = name
        self.off = off
        self.shape = tuple(shape)
        self.dt = dt
        self.esz = 4 if dt == F32 else 2
        nel = _prod(shape)
        nbytes = nel * self.esz
        assert off % 4 == 0 and nbytes % 4 == 0
        w = arena[:, off // 4:(off + nbytes) // 4]
        if dt != F32:
            w = w.bitcast(dt)
        if len(shape) == 2:
            w = w.rearrange("p (a b) -> p a b", a=shape[0])
        elif len(shape) == 3:
            w = w.rearrange("p (a b c) -> p a b c", a=shape[0], b=shape[1])
        elif len(shape) == 4:
            w = w.rearrange("p (a b c d) -> p a b c d", a=shape[0], b=shape[1], c=shape[2])
        self.full = w
        st = []
        s = 1
        for d in reversed(self.shape):
            st.append(s)
            s *= d
        self.strides = tuple(reversed(st))
        self.nbytes = nbytes

    def __getitem__(self, idx):
        if not isinstance(idx, tuple):
            idx = (idx,)
        ap = self.full[idx]
        fidx = list(idx[1:])
        while len(fidx) < len(self.shape):
            fidx.append(slice(None))
        lohi = []
        for d, ix in enumerate(fidx):
            n = self.shape[d]
            if isinstance(ix, int):
                lohi.append((ix, ix))
            else:
                a, b, c = ix.indices(n)
                if c > 0:
                    last = a + ((b - a - 1) // c) * c
                    lohi.append((a, last))
                else:
                    last = a + ((a - b - 1) // (-c)) * c
                    lohi.append((last, a))
        ranges = [(0, 0)]
        outer = lohi[:-1]
        ncomb = _prod([h - l + 1 for l, h in outer]) if outer else 1
        keys = set()
        if ncomb > 256:
            lo = sum(l * s for (l, h), s in zip(lohi, self.strides))
            hi = sum(h * s for (l, h), s in zip(lohi, self.strides))
            b0 = self.off + lo * self.esz
            b1 = self.off + (hi + 1) * self.esz
            keys.update(range(b0 // GRAN, (b1 - 1) // GRAN + 1))
        else:
            def rec(d, base):
                if d == len(lohi) - 1:
                    l, h = lohi[d]
                    b0 = self.off + (base + l) * self.esz
                    b1 = self.off + (base + h + 1) * self.esz
                    keys.update(range(b0 // GRAN, (b1 - 1) // GRAN + 1))
                    return
                l, h = lohi[d]
                for i in range(l, h + 1):
                    rec(d + 1, base + i * self.strides[d])
            rec(0, 0)
        return Acc(ap, keys)


class Prog:
    def __init__(self, nc, stack):
        self.nc = nc
        self.stack = stack
        self.eng = {"pe": nc.tensor, "act": nc.scalar, "dve": nc.vector, "pool": nc.gpsimd, "sp": nc.sync}
        self.sem = {}
        self.cnt = {}
        self.waited = {e: {} for e in self.eng}
        self.lastw = {}
        self.readers = {}
        for e in ("pe", "act", "dve", "pool"):
            self._mksem(e)
        self.arena = stack.enter_context(nc.sbuf_tensor("arena", [128, ARENA_WORDS], F32))
        self.top = 0
        self.tops = []
        self.psum = [stack.enter_context(nc.psum_tensor("ps%d" % i, [128, 512], F32)) for i in range(8)]
        self.rr = 0

    def _mksem(self, name):
        self.sem[name] = self.stack.enter_context(self.nc.semaphore(name.replace(":", "_")))
        self.cnt[name] = 0

    def stream(self, name):
        n = "dma:" + name
        if n not in self.sem:
            self._mksem(n)
        return n

    def alloc(self, name, shape, dt):
        esz = 4 if dt == F32 else 2
        nbytes = _prod(shape) * esz
        nbytes = (nbytes + GRAN - 1) // GRAN * GRAN
        off = self.top
        self.top += nbytes
        assert self.top <= ARENA_WORDS * 4, ("arena overflow", name, self.top)
        self.tops.append((self.top, name))
        return Buf(self.arena, name, off, shape, dt)

    def mark(self):
        return self.top

    def release(self, m):
        self.top = m

    def ps(self, bank, cols=512, rows=128, c0=0, r0=0):
        ap = self.psum[bank][r0:r0 + rows, c0:c0 + cols]
        return Acc(ap, {("ps", bank)})

    def ps_rot(self):
        b = self.rr
        self.rr = (self.rr + 1) % 3
        return b

    def _need(self, eng, reads, writes):
        need = {}

        def add(t, kind):
            if t is None:
                return
            s, v = t
            if s == eng and eng in ("pe", "sp"):
                return
            if need.get(s, 0) < v:
                need[s] = v
        for a in reads:
            for k in a.keys:
                add(self.lastw.get(k), "raw")
        for a in writes:
            for k in a.keys:
                add(self.lastw.get(k), "waw")
                r = self.readers.get(k)
                if r:
                    for s, v in r.items():
                        add((s, v), "war")
        return need

    def _wait(self, eng, need):
        w = self.waited[eng]
        for s, v in need.items():
            if w.get(s, 0) < v:
                self.eng[eng].wait_ge(self.sem[s], v)
                w[s] = v

    def _record(self, t, reads, writes):
        s, v = t
        for a in reads:
            for k in a.keys:
                r = self.readers.setdefault(k, {})
                if r.get(s, 0) < v:
                    r[s] = v
        for a in writes:
            for k in a.keys:
                self.lastw[k] = t
                self.readers[k] = {}

    def op(self, eng, fn, reads=(), writes=(), inc=True):
        psr = [a for a in reads if any(isinstance(k, tuple) and k[0] == "ps" for k in a.keys)]
        if psr:
            writes = list(writes) + psr
        self._wait(eng, self._need(eng, reads, writes))
        ins = fn()
        if inc:
            self.cnt[eng] += 1
            ins.then_inc(self.sem[eng], 1)
            t = (eng, self.cnt[eng])
        else:
            t = (eng, self.cnt[eng] + 1)
        self._record(t, reads, writes)

    def dma(self, queue, stream, out, in_, reads=(), writes=()):
        self._wait(queue, self._need(queue, reads, writes))
        ins = self.eng[queue].dma_start(out=out, in_=in_)
        self.cnt[stream] += 16
        ins.then_inc(self.sem[stream], 16)
        self._record((stream, self.cnt[stream]), reads, writes)

    def dma_group(self, queue, stream, items):
        allr, allw = [], []
        for o, i, r, w in items:
            allr += list(r)
            allw += list(w)
        self._wait(queue, self._need(queue, allr, allw))
        for o, i, r, w in items:
            ins = self.eng[queue].dma_start(out=o, in_=i)
            self.cnt[stream] += 16
            ins.then_inc(self.sem[stream], 16)
        self._record((stream, self.cnt[stream]), allr, allw)

    def mm(self, out, lhsT, rhs, start, stop, **kw):
        self.op("pe", lambda: self.nc.tensor.matmul(out.ap, lhsT.ap, rhs.ap, start=start, stop=stop, **kw),
                reads=[lhsT, rhs], writes=[out], inc=True)

    def act(self, out, in_, func, bias=None, scale=None, extra_reads=()):
        kw = {}
        rd = [in_] + list(extra_reads)
        if bias is not None:
            if isinstance(bias, Acc):
                kw["bias"] = bias.ap
                rd.append(bias)
            else:
                kw["bias"] = bias
        if scale is not None:
            if isinstance(scale, Acc):
                kw["scale"] = scale.ap
                rd.append(scale)
            else:
                kw["scale"] = scale
        self.op("act", lambda: self.nc.scalar.activation(out=out.ap, in_=in_.ap, func=func, **kw),
                reads=rd, writes=[out])

    def tt(self, eng, out, a, b, op):
        self.op(eng, lambda: self.eng[eng].tensor_tensor(out=out.ap, in0=a.ap, in1=b.ap, op=op),
                reads=[a, b], writes=[out])

    def ts(self, eng, out, a, s1, op0, s2=None, op1=None):
        rd = [a]
        v1 = s1.ap if isinstance(s1, Acc) else s1
        v2 = s2.ap if isinstance(s2, Acc) else s2
        if isinstance(s1, Acc):
            rd.append(s1)
        if isinstance(s2, Acc):
            rd.append(s2)
        if op1 is None:
            fn = lambda: self.eng[eng].tensor_scalar(out=out.ap, in0=a.ap, scalar1=v1, scalar2=None, op0=op0)
        else:
            fn = lambda: self.eng[eng].tensor_scalar(out=out.ap, in0=a.ap, scalar1=v1, scalar2=v2, op0=op0, op1=op1)
        self.op(eng, fn, reads=rd, writes=[out])

    def stt(self, out, a, s, b, op0, op1):
        rd = [a, b]
        sv = s.ap if isinstance(s, Acc) else s
        if isinstance(s, Acc):
            rd.append(s)
        self.op("dve", lambda: self.nc.vector.scalar_tensor_tensor(out=out.ap, in0=a.ap, scalar=sv, in1=b.ap, op0=op0, op1=op1),
                reads=rd, writes=[out])

    def copy(self, eng, out, in_):
        if eng == "act":
            self.act(out, in_, AF.Copy)
        else:
            self.op(eng, lambda: self.eng[eng].tensor_copy(out=out.ap, in_=in_.ap), reads=[in_], writes=[out])

    def memset(self, eng, out, val):
        self.op(eng, lambda: self.eng[eng].memset(out.ap, val), reads=[], writes=[out])

    def recip(self, out, in_):
        self.op("dve", lambda: self.nc.vector.reciprocal(out=out.ap, in_=in_.ap), reads=[in_], writes=[out])


def host_consts():
    ident = np.eye(128, dtype=np.float32)
    R = np.zeros((64, 64), np.float32)
    for i in range(16):
        R[i, 16 + i] = -1.0
        R[16 + i, i] = 1.0
        R[32 + i, 48 + i] = -1.0
        R[48 + i, 32 + i] = 1.0
    R2 = np.zeros((128, 128), np.float32)
    R2[:64, :64] = R
    R2[64:, 64:] = R
    RT = np.ascontiguousarray(R2.T)
    j = np.arange(128)[:, None]
    i = np.arange(128)[None, :]
    maskL = (j >= i).astype(np.float32)
    maskU = (j <= i).astype(np.float32)
    rows = T // 64
    row = np.repeat(np.arange(rows, dtype=np.float32), 64)
    col = np.tile(np.arange(64, dtype=np.float32), rows)
    inv = (10000.0 ** (-np.arange(0, 32, 2, dtype=np.float32) / 32)).astype(np.float32)
    ang_r = row[:, None] * inv[None, :]
    ang_c = col[:, None] * inv[None, :]
    ang = np.concatenate([ang_r, ang_r, ang_c, ang_c], axis=-1).astype(np.float32)
    cos = np.cos(ang).astype(np.float32).T
    sin = np.sin(ang).astype(np.float32).T
    cos2 = np.concatenate([cos, cos], axis=0)
    sin2 = np.concatenate([sin, sin], axis=0)
    small = np.concatenate([ident, RT, maskL, maskU], axis=1)
    return np.ascontiguousarray(small), np.ascontiguousarray(cos2), np.ascontiguousarray(sin2)


def build(layers=(0, 1), debug_out=False):
    from contextlib import ExitStack
    nc = bass.Bass("TRN2", target_bir_lowering=False)
    dr = {}

    def din(name, shape):
        dr[name] = nc.dram_tensor(name, list(shape), F32, kind="ExternalInput").ap()
        return dr[name]
    x_d = din("x", [T, D])
    c_d = din("c", [8, 128])
    ctx_d = din("ctx", [TCX, D])
    cctx_d = din("c_ctx", [8, 128])
    adaw_d = din("ada_w", [2, D, 6 * D])
    adab_d = din("ada_b", [96, 128])
    ng_d = din("norm_g", [64, 128])
    w1_d = din("mlp_w1", [2, D, DFF])
    w2_d = din("mlp_w2", [2, DFF, D])
    wqkv_d = din("attn_w_qkv", [D, 1536])
    wo_d = din("attn_w_o", [D, D])
    sink_d = din("attn_sink", [1, 16])
    win_d = din("lru_w_in", [D, 2 * DR])
    convw_d = din("lru_conv_w", [40, 128])
    convb_d = din("lru_conv_b", [10, 128])
    wa_d = din("lru_w_a", [2, 5, 256, 256])
    ba_d = din("lru_b_a", [20, 128])
    wi_d = din("lru_w_i", [2, 5, 256, 256])
    bi_d = din("lru_b_i", [20, 128])
    lam_d = din("lru_lam", [20, 128])
    wout_d = din("lru_w_out", [DR, D])
    csm_d = din("k_small", [128, 512])
    cos_d = din("k_cos", [128, T])
    sin_d = din("k_sin", [128, T])
    out_d = nc.dram_tensor("out", [T, D], F32, kind="ExternalOutput").ap()
    w1s_d = nc.dram_tensor("w1s", [2, D, DFF], BF16).ap()
    w2s_d = nc.dram_tensor("w2s", [2, DFF, D], BF16).ap()
    if debug_out:
        outc_d = nc.dram_tensor("outc", [TCX, D], F32, kind="ExternalOutput").ap()

    with ExitStack() as stack:
        P = Prog(nc, stack)
        s_ld = P.stream("ld")
        s_w = [P.stream("w%d" % i) for i in range(3)]
        s_wh = [P.stream("wh%d" % i) for i in range(3)]
        s_xs = [P.stream("xs%d" % i) for i in range(2)]
        s_st = [P.stream("st%d" % i) for i in range(2)]
        s_w2 = P.stream("wsmall")

        XT = P.alloc("XT", [KC, TT], F32)
        IDENT = P.alloc("IDENT", [128], F32)
        IDB = P.alloc("IDB", [128], BF16)
        ONESB = P.alloc("ONESB", [128], BF16)
        ONES64 = P.alloc("ONES64", [64], BF16)
        EPSC = P.alloc("EPSC", [1], F32)
        COLA = P.alloc("COLA", [80], F32)
        MOD = P.alloc("MOD", [96, 2], F32)
        DER = P.alloc("DER", [2, 2, 4, 8], F32)
        COLC = P.alloc("COLC", [110], F32)
        LCOL = P.alloc("LCOL", [5, 20], F32)
        ES = P.alloc("ES", [8], F32)
        SQ = [P.alloc("SQ%d" % i, [512], BF16) for i in range(2)]
        RSTD = P.alloc("RSTD", [512], F32)
        TMP = [P.alloc("TMP%d" % i, [512], F32) for i in range(2)]
        WS = [P.alloc("WS%d" % i, [8, 1024], BF16) for i in range(3)]
        wrr = [0]

        def wslot():
            i = wrr[0]
            wrr[0] = (i + 1) % 3
            return i

        P.memset("dve", ONESB[:, :], 1.0 / 1024.0)
        P.memset("dve", ONES64[:, :], 1.0)
        P.memset("dve", EPSC[:, :], EPS)

        COLB = P.alloc("COLB", [96], F32)
        SCB = P.alloc("SCB", [16], BF16)
        mk_setup = P.mark()
        CSM = P.alloc("CSM", [512], F32)
        ROWA = P.alloc("ROWA", [128], F32)
        ROWB = P.alloc("ROWB", [128], F32)
        ROWC = P.alloc("ROWC", [128], F32)
        XS = [P.alloc("XS%d" % i, [D], F32) for i in range(2)]

        P.dma("sp", P.stream("ld_a"), CSM.full[:, :], csm_d, writes=[CSM[:, :]])
        P.dma_group("sp", P.stream("ld_b"), [
            (ROWA.full[0:8, :], c_d, [], [ROWA[:, :]]),
            (ROWA.full[8:16, :], cctx_d, [], []),
            (ROWA.full[16:80, :], ng_d, [], []),
            (ROWB.full[0:96, :], adab_d, [], [ROWB[:, :]]),
            (ROWC.full[0:40, :], convw_d, [], [ROWC[:, :]]),
            (ROWC.full[40:50, :], convb_d, [], []),
            (ROWC.full[50:70, :], ba_d, [], []),
            (ROWC.full[70:90, :], bi_d, [], []),
            (ROWC.full[90:110, :], lam_d, [], []),
        ])
        sk = sink_d.rearrange("o (a h) -> o a h", h=2)
        with nc.allow_non_contiguous_dma(reason="tiny sink broadcast"):
            P.dma_group("sp", P.stream("ld_c"), [
                (ES.full[0:64, :], sk[:, :, 0].to_broadcast([64, 8]), [], [ES[:, :]]),
                (ES.full[64:128, :], sk[:, :, 1].to_broadcast([64, 8]), [], []),
            ])
        P.copy("dve", IDENT[:, :], CSM[:, 0:128])
        P.copy("dve", IDB[:, :], CSM[:, 0:128])
        P.act(ES[:, :], ES[:, :], AF.Exp)

        def rows2cols(ROW, nrows, COL):
            pb = P.ps_rot()
            o = P.ps(pb, cols=nrows)
            P.op("pe", lambda: nc.tensor.transpose(o.ap, ROW.full[0:nrows, :], IDENT.full[0:nrows, 0:nrows]),
                 reads=[ROW[:, :], IDENT[:, :]], writes=[o])
            P.copy("dve", COL[:, 0:nrows], o)
        rows2cols(ROWA, 80, COLA)
        rows2cols(ROWB, 96, COLB)
        rows2cols(ROWC, 110, COLC)
        P.act(SCB[:, :], COLA[:, 0:16], AF.Silu)

        MODP = 7
        defer_ada1 = (0 in layers and 1 in layers)
        for i in ([0] if defer_ada1 else [0, 1]):
            for g in range(6):
                sl = wslot()
                W = WS[sl]
                P.dma("pool", s_w[sl], W.full[:, :, :],
                      adaw_d[i, :, g * 1024:(g + 1) * 1024].rearrange("(k p) n -> p k n", p=128),
                      writes=[W[:, :, :]])
                for jj in range(8):
                    j = i * 48 + g * 8 + jj
                    o = P.ps(MODP, cols=2, c0=2 * j)
                    for k in range(KC):
                        P.mm(o, W[:, k, jj * 128:(jj + 1) * 128], SCB[:, k:16:8], start=(k == 0), stop=(k == KC - 1))
        def precast(i):
            for kind, src, dst in (("w1", w1_d[i], w1s_d[i]), ("w2", w2_d[i].rearrange("(a b) n -> a (b n)", b=4), w2s_d[i].rearrange("(a b) n -> a (b n)", b=4))):
                key = Acc(None, {("scr", i, kind)})
                items = []
                for a_ in range(8):
                    items.append((dst[a_ * 128:(a_ + 1) * 128, :], src[a_ * 128:(a_ + 1) * 128, :], [], [key] if a_ == 0 else []))
                P.dma_group("pool", P.stream("pc_%d_%s" % (i, kind)), items)
        if 0 not in layers:
            for i in layers:
                precast(i)
        nmod = 48 if defer_ada1 else 96
        for v in range(2):
            pv = Acc(P.psum[MODP][:, v:2 * nmod:2], {("ps", MODP)})
            P.tt("dve", MOD[:, 0:nmod, v], pv, COLB[:, 0:nmod], ALU.add)

        def der(i):
            for v in range(2):
                b = i * 48
                ng = lambda jn: COLA[:, 16 + (i * 4 + jn) * 8: 16 + (i * 4 + jn) * 8 + 8]
                P.stt(DER[:, i, v, 0, :], MOD[:, b + 8:b + 16, v], 1.0, ng(0), ALU.add, ALU.mult)
                P.tt("dve", DER[:, i, v, 1, :], MOD[:, b + 16:b + 24, v], ng(1), ALU.mult)
                P.stt(DER[:, i, v, 2, :], MOD[:, b + 32:b + 40, v], 1.0, ng(2), ALU.add, ALU.mult)
                P.tt("dve", DER[:, i, v, 3, :], MOD[:, b + 40:b + 48, v], ng(3), ALU.mult)
        der(0)
        if not defer_ada1:
            der(1)

        def A1(i, v, k): return DER[:, i, v, 0, k:k + 1]
        def G1(i, v, k): return DER[:, i, v, 1, k:k + 1]
        def A2(i, v, k): return DER[:, i, v, 2, k:k + 1]
        def G2(i, v, k): return DER[:, i, v, 3, k:k + 1]
        def B1(i, v, k): return MOD[:, i * 48 + k, v:v + 1]
        def B2(i, v, k): return MOD[:, i * 48 + 24 + k, v:v + 1]

        if 1 in layers:
            P.ts("dve", LCOL[:, 0, :], COLC[:, 50:70], 0.5, ALU.mult)
            P.ts("dve", LCOL[:, 1, :], COLC[:, 70:90], 0.5, ALU.mult)
            P.act(LCOL[:, 4, :], COLC[:, 90:110], AF.Exp, scale=-1.0)
            P.ts("dve", LCOL[:, 4, :], LCOL[:, 4, :], 1.0, ALU.add)
            P.act(LCOL[:, 4, :], LCOL[:, 4, :], AF.Ln)
            P.ts("dve", LCOL[:, 2, :], LCOL[:, 4, :], -8.0, ALU.mult)
            P.ts("dve", LCOL[:, 3, :], LCOL[:, 4, :], -4.0, ALU.mult)

        for tb in range(TT // 128):
            sl = tb % 2
            src = x_d[tb * 128:(tb + 1) * 128, :] if tb < 16 else ctx_d[(tb - 16) * 128:(tb - 15) * 128, :]
            P.dma("sp", s_xs[sl], XS[sl].full[:, :], src, writes=[XS[sl][:, :]])
            for h in range(2):
                pb = P.ps_rot()
                for q in range(4):
                    k = h * 4 + q
                    o = P.ps(pb, cols=128, c0=q * 128)
                    P.op("pe", lambda o=o, k=k, sl=sl: nc.tensor.transpose(o.ap, XS[sl].full[:, k * 128:(k + 1) * 128], IDENT.full[:, :]),
                         reads=[XS[sl][:, k * 128:(k + 1) * 128], IDENT[:, :]], writes=[o])
                full = Acc(P.psum[pb][:, :].rearrange("p (q t) -> p q t", q=4), {("ps", pb)})
                P.copy("dve" if h == 0 else "act", XT[:, h * 4:(h + 1) * 4, tb * 128:(tb + 1) * 128], full)
        P.release(mk_setup)

        PSS = 7

        def norm_block(i, t0, n, v, which, H, Hk=None, ssb=None, RSTD=RSTD):
            if Hk is None:
                Hk = lambda k: H[:, k, 0:n]
            Af = A1 if which == 1 else A2
            Bf = B1 if which == 1 else B2
            ss = P.ps(PSS if ssb is None else ssb, cols=n)
            for k in range(KC):
                sq = SQ[k % 2]
                P.act(sq[:, 0:n], XT[:, k, t0:t0 + n], AF.Square)
                P.mm(ss, ONESB[:, :], sq[:, 0:n], start=(k == 0), stop=(k == KC - 1))
            P.act(RSTD[:, 0:n], ss, AF.Ln, bias=EPSC[:, 0:1], scale=1.0)
            P.act(RSTD[:, 0:n], RSTD[:, 0:n], AF.Exp, scale=-0.5)
            for k in range(KC):
                tmp = TMP[k % 2]
                P.stt(tmp[:, 0:n], XT[:, k, t0:t0 + n], Af(i, v, k), RSTD[:, 0:n], ALU.mult, ALU.mult)
                P.ts("dve", Hk(k), tmp[:, 0:n], Bf(i, v, k), ALU.add)

        def resid_update(i, t0, n, v, which, Y, ss):
            Gf = G1 if which == 1 else G2
            P.act(RSTD[:, 0:n], ss, AF.Ln, bias=EPSC[:, 0:1], scale=1.0)
            P.act(RSTD[:, 0:n], RSTD[:, 0:n], AF.Exp, scale=-0.5)
            for k in range(KC):
                tmp = TMP[k % 2]
                P.tt("dve", tmp[:, 0:n], Y[:, k, 0:n], RSTD[:, 0:n], ALU.mult)
                P.stt(XT[:, k, t0:t0 + n], tmp[:, 0:n], Gf(i, v, k), XT[:, k, t0:t0 + n], ALU.mult, ALU.add)

        def proj_out_stats(Wfn, nk, rhsfn, n, Y):
            ss = P.ps(PSS, cols=n)
            pend = None
            for m in range(KC):
                pb = P.ps_rot()
                o = P.ps(pb, cols=n)
                for k in range(nk):
                    P.mm(o, Wfn(k, m), rhsfn(k), start=(k == 0), stop=(k == nk - 1))
                if pend is not None:
                    P.mm(ss, ONESB[:, :], pend[0], start=(pend[1] == 0), stop=False)
                sq = SQ[m % 2]
                P.act(sq[:, 0:n], o, AF.Square)
                P.copy("dve", Y[:, m, 0:n], o)
                pend = (sq[:, 0:n], m)
            P.mm(ss, ONESB[:, :], pend[0], start=False, stop=True)
            return ss

        out_state = {"OS": None, "done": set()}

        def out_tokens(tb):
            OS = out_state["OS"]
            sl = tb % 2
            for h in range(2):
                pb = P.ps_rot()
                for q in range(4):
                    k = h * 4 + q
                    o = P.ps(pb, cols=128, c0=q * 128)
                    src = XT[:, k, tb * 128:(tb + 1) * 128]
                    P.op("pe", lambda o=o, src=src: nc.tensor.transpose(o.ap, src.ap, IDENT.full[:, :]),
                         reads=[src, IDENT[:, :]], writes=[o])
                P.copy("dve" if h == 0 else "act", OS[sl][:, h * 512:(h + 1) * 512], P.ps(pb))
            dst = out_d[tb * 128:(tb + 1) * 128, :] if tb < 16 else outc_d[(tb - 16) * 128:(tb - 15) * 128, :]
            P.dma("sp", s_st[sl], dst, OS[sl].full[:, :], reads=[OS[sl][:, :]])
            out_state["done"].add(tb)

        def mlp(i, blocks, ada_groups=(), emit_out=False):
            mk = P.mark()
            if emit_out:
                out_state["OS"] = [P.alloc("OS%d" % q, [D], F32) for q in range(2)]
            H2 = P.alloc("H2", [KC, 512], BF16)
            G = P.alloc("G", [32, 512], BF16)
            Y = P.alloc("Ym", [KC, 512], F32)
            R1 = [P.alloc("R1_%d" % q, [512], BF16) for q in range(2)]
            RSTDn = P.alloc("RSTDn", [512], F32)
            loads = []
            ada_left = list(ada_groups)
            ada_sched = []
            for bi in range(len(blocks)):
                loads += [("w1", g) for g in range(4)] + [("w2", g) for g in range(4)]
                mine = ada_left[:2]
                ada_left = ada_left[2:]
                ada_sched.append(mine)
                loads += [("ada", g) for g in mine]
            assert not ada_left
            slots = {}

            def issue(L):
                if L >= len(loads) or L in slots:
                    return
                kind, g = loads[L]
                sl = wslot()
                slots[L] = sl
                W = WS[sl]
                key = Acc(None, {("scr", i, kind)})
                if kind == "ada":
                    P.dma("pool", s_w[sl], W.full[:, :, :],
                          adaw_d[1, :, g * 1024:(g + 1) * 1024].rearrange("(k p) n -> p k n", p=128),
                          writes=[W[:, :, :]])
                elif kind == "w1":
                    P.dma("sp", s_wh[sl], W.full[:, :, :],
                          w1s_d[i, :, g * 1024:(g + 1) * 1024].rearrange("(k p) n -> p k n", p=128),
                          reads=[key], writes=[W[:, :, :]])
                else:
                    P.dma("sp", s_wh[sl], W.full.rearrange("p a (b c) -> p (a b) c", c=256),
                          w2s_d[i, :, g * 256:(g + 1) * 256].rearrange("(k p) n -> p k n", p=128),
                          reads=[key], writes=[W[:, :, :]])
            issue(0)
            issue(1)
            L = 0
            norm_block(i, blocks[0][0], blocks[0][1], blocks[0][2], 2, H2, ssb=6, RSTD=RSTDn)
            for bi, (t0, n, v) in enumerate(blocks):
                for g in range(4):
                    issue(L + 2)
                    W = WS[slots[L]]
                    L += 1
                    for jj in range(8):
                        pb = P.ps_rot()
                        o = P.ps(pb, cols=n)
                        for k in range(KC):
                            P.mm(o, W[:, k, jj * 128:(jj + 1) * 128], H2[:, k, 0:n], start=(k == 0), stop=(k == KC - 1))
                        r1 = R1[jj % 2]
                        P.act(r1[:, 0:n], o, AF.Relu)
                        P.tt("dve", G[:, g * 8 + jj, 0:n], r1[:, 0:n], r1[:, 0:n], ALU.mult)
                if bi + 1 < len(blocks):
                    nb_ = blocks[bi + 1]
                    norm_block(i, nb_[0], nb_[1], nb_[2], 2, H2, ssb=6, RSTD=RSTDn)
                ss = P.ps(PSS, cols=n)
                pend2 = None
                for g in range(4):
                    issue(L + 2)
                    W = WS[slots[L]]
                    L += 1
                    Wv = Acc(W.full.rearrange("p a (b c) -> p (a b) c", c=256), W[:, :, :].keys)
                    for mm_ in range(2):
                        m = g * 2 + mm_
                        pb = P.ps_rot()
                        o = P.ps(pb, cols=n)
                        for k in range(32):
                            lw = Acc(Wv.ap[:, k, mm_ * 128:(mm_ + 1) * 128], Wv.keys)
                            P.mm(o, lw, G[:, k, 0:n], start=(k == 0), stop=(k == 31))
                        if pend2 is not None:
                            P.mm(ss, ONESB[:, :], pend2[0], start=(pend2[1] == 0), stop=False)
                        sq = SQ[m % 2]
                        P.act(sq[:, 0:n], o, AF.Square)
                        P.copy("dve", Y[:, m, 0:n], o)
                        pend2 = (sq[:, 0:n], m)
                P.mm(ss, ONESB[:, :], pend2[0], start=False, stop=True)
                resid_update(i, t0, n, v, 2, Y, ss)
                if emit_out and bi > 0:
                    pt0, pn = blocks[bi - 1][0], blocks[bi - 1][1]
                    for tb in range(pt0 // 128, (pt0 + pn) // 128):
                        out_tokens(tb)
                for g in ada_sched[bi]:
                    issue(L + 2)
                    W = WS[slots[L]]
                    L += 1
                    pb = P.ps_rot()
                    for jj in range(8):
                        o = P.ps(pb, cols=2, c0=2 * jj)
                        for k in range(KC):
                            P.mm(o, W[:, k, jj * 128:(jj + 1) * 128], SCB[:, k:16:8], start=(k == 0), stop=(k == KC - 1))
                    for v_ in range(2):
                        pv = Acc(P.psum[pb][:, v_:16:2], {("ps", pb)})
                        P.tt("dve", MOD[:, 48 + g * 8:48 + g * 8 + 8, v_], pv, COLB[:, 48 + g * 8:48 + g * 8 + 8], ALU.add)
            if emit_out:
                pt0, pn = blocks[-1][0], blocks[-1][1]
                for tb in range(pt0 // 128, (pt0 + pn) // 128):
                    out_tokens(tb)
            P.release(mk)

        if 0 in layers:
            i = 0
            mk0 = P.mark()
            RT = P.alloc("RT", [128], BF16)
            MASK = P.alloc("MASK", [2, 4, 128], BF16)
            COS = P.alloc("COS", [T], BF16)
            SIN = P.alloc("SIN", [T], BF16)
            KTb = P.alloc("KT", [2, TT], BF16)
            Vb = P.alloc("V", [TT // 128, 256], BF16)
            WQ, WO, WKV = WS[0], WS[1], WS[2]
            AB = 256
            H = P.alloc("H", [KC, AB], BF16)
            QTb = [P.alloc("QT%d" % q, [2, AB // 128, 4, 128], BF16) for q in range(2)]
            QC = P.alloc("QC", [2, AB // 128, 4, 128], BF16)
            ATb = P.alloc("ATb", [KC, AB], BF16)
            Y = P.alloc("Y", [KC, AB], F32)
            QB = [P.alloc("QB%d" % q, [AB], BF16) for q in range(2)]
            T1 = [P.alloc("T1_%d" % q, [AB], F32) for q in range(2)]
            T2 = [P.alloc("T2_%d" % q, [AB], F32) for q in range(2)]
            PT = [P.alloc("PT%d" % q, [512], BF16) for q in range(10)]
            ptc = [0]
            ZS = P.alloc("ZS", [256], F32)
            mkc = P.mark()
            CSM2 = Buf(P.arena, "CSM2", T1[0].off, [512], F32)
            CST = Buf(P.arena, "CST", Y.off, [T], F32)
            P.dma("sp", P.stream("csm2"), CSM2.full[:, :], csm_d, writes=[CSM2[:, :]])
            P.copy("dve", RT[:, :], CSM2[:, 128:256])
            for q in range(4):
                P.copy("dve", MASK[:, 0, q, :], CSM2[:, 256:384])
                P.copy("dve", MASK[:, 1, q, :], CSM2[:, 384:512])
            P.dma("sp", P.stream("cst"), CST.full[:, :], cos_d, writes=[CST[:, :]])
            P.copy("dve", COS[:, :], CST[:, :])
            P.dma("sp", P.stream("cst"), CST.full[:, :], sin_d, writes=[CST[:, :]])
            P.copy("dve", SIN[:, :], CST[:, :])
            P.release(mkc)
            WQ6 = WQ.full.rearrange("p k (gp g two d) -> p k gp g two d", gp=2, g=4, two=2)
            items = []
            for gp_ in range(2):
                for two_ in range(2):
                    for g_ in range(4):
                        hb = ((2 * gp_ + two_) * 4 + g_) * 64
                        items.append((WQ6[:, :, gp_, g_, two_, :], wqkv_d[:, hb:hb + 64].rearrange("(k p) d -> p k d", p=128),
                                      [], [WQ[:, :, :]] if not items else []))
            P.dma_group("pool", P.stream("wq"), items)
            P.dma("pool", P.stream("wkv"), WKV.full[:, :, 0:512], wqkv_d[:, 1024:1536].rearrange("(k p) n -> p k n", p=128), writes=[WKV[:, :, :]])
            P.dma("pool", P.stream("wo"), WO.full[:, :, :], wo_d.rearrange("(k p) n -> p k n", p=128), writes=[WO[:, :, :]])

            rr2 = [0]

            rope_pend = [None]

            def rope_flush():
                if rope_pend[0] is not None:
                    f = rope_pend[0]
                    rope_pend[0] = None
                    f()

            def rope_evac(o, n, t0, dst, is_x, split=False):
                def vw(a):
                    if not split:
                        return a
                    return Acc(a.ap.rearrange("p (a b) -> p a b", b=128), a.keys)
                if not is_x:
                    P.copy("act", dst, vw(o))
                    return
                q = rr2[0]
                rr2[0] ^= 1
                P.copy("act", QB[q][:, 0:n], o)
                P.tt("dve", T1[q][:, 0:n], o, COS[:, t0:t0 + n], ALU.mult)

                def fin():
                    pb = P.ps_rot()
                    ro = P.ps(pb, cols=n)
                    P.mm(ro, RT[:, :], QB[q][:, 0:n], start=True, stop=True)
                    P.tt("dve", T2[q][:, 0:n], ro, SIN[:, t0:t0 + n], ALU.mult)
                    P.tt("pool", dst, vw(T1[q][:, 0:n]), vw(T2[q][:, 0:n]), ALU.add)
                prevf = rope_pend[0]
                rope_pend[0] = fin
                if prevf is not None:
                    prevf()

            def kvq(t0, n, v, QT, do_norm=True):
                is_x = (v == 0)
                if do_norm:
                    norm_block(i, t0, n, v, 1, H)
                for kc in range(2):
                    pb = P.ps_rot()
                    o = P.ps(pb, cols=n)
                    for k in range(KC):
                        P.mm(o, WKV[:, k, kc * 128:(kc + 1) * 128], H[:, k, 0:n], start=(k == 0), stop=(k == KC - 1))
                    rope_evac(o, n, t0, KTb[:, kc, t0:t0 + n], is_x)
                for s in range(n // 128):
                    pb = P.ps_rot()
                    o = P.ps(pb, cols=256)
                    for k in range(KC):
                        P.mm(o, H[:, k, s * 128:(s + 1) * 128], WKV[:, k, 256:512], start=(k == 0), stop=(k == KC - 1))
                    P.copy("dve", Vb[:, t0 // 128 + s, :], o)
                    rope_flush()
                for j in range(8):
                    gp, g = j // 4, j % 4
                    sidx = (g % 2) * 2 + g // 2
                    pb = P.ps_rot()
                    o = P.ps(pb, cols=n)
                    for k in range(KC):
                        P.mm(o, WQ[:, k, j * 128:(j + 1) * 128], H[:, k, 0:n], start=(k == 0), stop=(k == KC - 1))
                    rope_evac(o, n, t0, QT[:, gp, 0:n // 128, sidx, :], is_x, split=True)
                rope_flush()

            obank = [0]

            def attention(t0, n, v, QT, mid_hook=None):
                its = []
                for qi in range(n // 128):
                    q0 = t0 + qi * 128
                    if v == 0:
                        qb = q0 // 128
                        kchunks = []
                        if qb > 0:
                            kchunks.append((qb - 1, 0))
                        kchunks.append((qb, None))
                        if qb < 15:
                            kchunks.append((qb + 1, 1))
                        kchunks += [(16, None), (17, None)]
                    else:
                        kchunks = [(16, None), (17, None)]
                    for kvh in range(4):
                        its.append((qi, kvh, kchunks))

                def emit_S(qi, kvh, kchunks, pts):
                    half = (kvh % 2) * 64
                    kc = kvh // 2
                    gp = kvh // 2
                    for ci, (kch, msk) in enumerate(kchunks):
                        pb = P.ps_rot()
                        so = P.ps(pb, cols=512)
                        lk = Acc(KTb.full[half:half + 64, kc, kch * 128:(kch + 1) * 128], KTb[:, kc, kch * 128:(kch + 1) * 128].keys)
                        rq = Acc(QT.full[half:half + 64, gp, qi, :, :].rearrange("p s t -> p (s t)"), QT[:, gp, qi, :, :].keys)
                        P.mm(so, lk, rq, start=True, stop=True)
                        pt = PT[ptc[0] % len(PT)]
                        ptc[0] += 1
                        P.act(pt[:, :], so, AF.Exp, scale=0.125)
                        if msk is not None:
                            mv = Acc(MASK.full[:, msk, :, :].rearrange("p g t -> p (g t)"), MASK[:, msk, :, :].keys)
                            P.tt("pool", pt[:, :], pt[:, :], mv, ALU.mult)
                        pts.append((pt, kch))
                        yield

                def emit_PV(qi, kvh, pts):
                    ob = 3 + obank[0]
                    zb = 5 + obank[0]
                    obank[0] ^= 1
                    nch = len(pts)
                    for ci, (pt, kch) in enumerate(pts):
                        vv = Acc(Vb.full[:, kch, kvh * 64:(kvh + 1) * 64], Vb[:, kch, :].keys)
                        last = (ci == nch - 1)
                        for hf in range(2):
                            rp = pt[:, hf * 256:(hf + 1) * 256]
                            oo = Acc(P.psum[ob][hf * 64:(hf + 1) * 64, 0:256], {("ps", ob)})
                            P.op("pe", lambda oo=oo, vv=vv, rp=rp, ci=ci, last=last, hf=hf: nc.tensor.matmul(
                                oo.ap, vv.ap, rp.ap, start=(ci == 0), stop=last, tile_position=(0, hf * 64)),
                                reads=[vv, rp], writes=[oo], inc=True)
                        for hf in range(2):
                            rp = pt[:, hf * 256:(hf + 1) * 256]
                            zz = Acc(P.psum[zb][hf * 64:(hf + 1) * 64, 0:256], {("ps", zb)})
                            P.op("pe", lambda zz=zz, rp=rp, ci=ci, last=last, hf=hf: nc.tensor.matmul(
                                zz.ap, ONES64.full[:, :], rp.ap, start=(ci == 0), stop=last, tile_position=(0, hf * 64)),
                                reads=[ONES64[:, :], rp], writes=[zz], inc=True)
                        yield
                    for gg in range(2):
                        P.ts("dve", ZS[:, gg * 128:(gg + 1) * 128], P.ps(zb, cols=128, c0=gg * 128),
                             ES[:, kvh * 2 + gg:kvh * 2 + gg + 1], ALU.add)
                    P.act(ZS[:, :], ZS[:, :], AF.Ln)
                    P.act(ZS[:, :], ZS[:, :], AF.Exp, scale=-1.0)
                    dst = ATb[:, 2 * kvh:2 * kvh + 2, qi * 128:(qi + 1) * 128]
                    o3 = Acc(P.psum[ob][:, 0:256].rearrange("p (g t) -> p g t", g=2), {("ps", ob)})
                    z3 = Acc(ZS.full.rearrange("p (g t) -> p g t", g=2), ZS[:, :].keys)
                    P.tt("dve", dst, o3, z3, ALU.mult)

                def run(*gens):
                    gens = [g for g in gens if g is not None]
                    while gens:
                        for g in list(gens):
                            try:
                                next(g)
                            except StopIteration:
                                gens.remove(g)
                prev = None
                for idx, (qi, kvh, kchunks) in enumerate(its):
                    pts = []
                    run(emit_S(qi, kvh, kchunks, pts), emit_PV(*prev) if prev is not None else None)
                    prev = (qi, kvh, pts)
                run(emit_PV(*prev))
                if mid_hook is not None:
                    mid_hook()

            def att_block(t0, n, v, QT, mid_hook=None):
                attention(t0, n, v, QT, mid_hook)
                ss = proj_out_stats(lambda k, m: WO[:, k, m * 128:(m + 1) * 128], KC, lambda k: ATb[:, k, 0:n], n, Y)
                resid_update(i, t0, n, v, 1, Y, ss)

            xb = [(b * AB, AB, 0) for b in range(T // AB)]
            cb = (T, TCX, 1)
            kvq(*cb, QC)
            kvq(*xb[0], QTb[0])
            kvq(*xb[1], QTb[1])
            for b in range(len(xb)):
                nxt = xb[b + 2] if b + 2 < len(xb) else None
                hook = (lambda nxt=nxt: norm_block(i, nxt[0], nxt[1], nxt[2], 1, H)) if nxt is not None else None
                att_block(*xb[b], QTb[b % 2], mid_hook=hook)
                if nxt is not None:
                    kvq(*nxt, QTb[b % 2], do_norm=False)
                if b == 0:
                    precast(0)
                if b == 4 and 1 in layers:
                    precast(1)
            att_block(*cb, QC)
            P.release(mk0)
            blocks = [(b * 512, 512, 0) for b in range(4)]
            if 1 in layers or debug_out:
                blocks.append((T, TCX, 1))
            mlp(0, blocks, ada_groups=(list(range(6)) if defer_ada1 else ()))
            if defer_ada1:
                der(1)

        if 1 in layers:
            i = 1
            mk1 = P.mark()
            PROD = P.alloc("PROD", [10, T], BF16)
            mkh = P.mark()
            HT = P.alloc("HT", [KC, T], BF16)
            def xpiece(k, half, shape, dt):
                return Buf(P.arena, "xp", XT.off + (k * TT + T) * 4 + half * 512, shape, dt)
            HTC = [xpiece(k, 0, [TCX], BF16) for k in range(KC)]
            WGP = [xpiece(k, 1, [256], BF16) for k in range(KC)]

            def hsl(k, t0, n):
                return HT[:, k, t0:t0 + n] if t0 < T else HTC[k][:, 0:n]
            VPW = 2320
            XO, CO = 2, 2058
            base = WS[0].off
            AA = Buf(P.arena, "AA", base, [TT], F32)
            WB = [Buf(P.arena, "WBf", base + 9216, [TT], F32), Buf(P.arena, "WBb", base + 2 * 9216, [TT], F32)]
            VP = Buf(P.arena, "VP", base + 3 * 9216, [2, VPW], BF16)
            S1 = Buf(P.arena, "S1", base + 3 * 9216, [TT], F32)
            U = Buf(P.arena, "U", base + 3 * 9216 + 9472, [2, TT], BF16)
            DG = Buf(P.arena, "DG", base + 3 * 9216 + 9472 + 9216, [8, 128], BF16)
            assert base + 3 * 9216 + 9472 + 9216 + 2048 <= WS[2].off + 16384
            WR = Buf(P.arena, "WR", SQ[0].off, [KC, 256], BF16)
            assert SQ[1].off == SQ[0].off + 1024 and RSTD.off == SQ[0].off + 2048
            TR, TI = TMP[0], TMP[1]
            allb = [(T, TCX, 1)] + [(b * 512, 512, 0) for b in range(4)]
            for (t0, n, v) in allb:
                norm_block(i, t0, n, v, 1, None, Hk=(lambda k, t0=t0, n=n: hsl(k, t0, n)))

            def vcol(t0):
                return (XO + t0) if t0 < T else (CO + t0 - T)

            AAC = P.alloc("AAC", [TCX], F32)
            pending = [None]

            def aa_acc(d, t0, n):
                if d == 0:
                    return AA[:, t0:t0 + n]
                if t0 >= T:
                    return AAC[:, 0:n]
                return P.ps(3 + t0 // 512, cols=n)

            def make_scan(d, ch):
                W_ = WB[d]

                def seg(j):
                    if d == 0:
                        if j == 0:
                            P.op("dve", lambda: nc.vector.tensor_tensor_scan(
                                out=W_.full[:, T:TT], data0=AA.full[:, T:TT], data1=W_.full[:, T:TT], initial=0.0,
                                op0=ALU.mult, op1=ALU.add), reads=[AA[:, T:TT], W_[:, T:TT]], writes=[W_[:, T:TT]])
                        else:
                            b_ = j - 1
                            init = W_[:, TT - 1:TT] if b_ == 0 else W_[:, b_ * 512 - 1:b_ * 512]
                            sl_ = slice(b_ * 512, (b_ + 1) * 512)
                            P.op("dve", lambda: nc.vector.tensor_tensor_scan(
                                out=W_.full[:, sl_], data0=AA.full[:, sl_], data1=W_.full[:, sl_], initial=init.ap,
                                op0=ALU.mult, op1=ALU.add), reads=[AA[:, sl_], W_[:, sl_], init], writes=[W_[:, sl_]])
                    else:
                        if j == 0:
                            P.op("dve", lambda: nc.vector.tensor_tensor_scan(
                                out=W_.full[:, TT - 1:T - 1:-1], data0=AAC.full[:, TCX - 1::-1], data1=W_.full[:, TT - 1:T - 1:-1], initial=0.0,
                                op0=ALU.mult, op1=ALU.add), reads=[AAC[:, :], W_[:, T:TT]], writes=[W_[:, T:TT]])
                        else:
                            b_ = 4 - j
                            init = W_[:, T:T + 1] if b_ == 3 else W_[:, (b_ + 1) * 512:(b_ + 1) * 512 + 1]
                            lo, hi = b_ * 512, (b_ + 1) * 512
                            rs = slice(hi - 1, lo - 1 if lo > 0 else None, -1)
                            aap = P.ps(3 + b_)
                            P.op("dve", lambda: nc.vector.tensor_tensor_scan(
                                out=W_.full[:, rs], data0=P.psum[3 + b_][:, 511::-1], data1=W_.full[:, rs], initial=init.ap,
                                op0=ALU.mult, op1=ALU.add), reads=[aap, W_[:, lo:hi], init], writes=[W_[:, lo:hi]])
                            sl_ = slice(lo, hi)
                            P.tt("dve", WB[0][:, sl_], WB[0][:, sl_], WB[1][:, sl_], ALU.add)
                            P.tt("dve", PROD[:, ch, sl_], PROD[:, ch, sl_], WB[0][:, sl_], ALU.mult)
                return seg

            for nb in range(5):
                items = []
                for dd in range(2):
                    for gt, wsrc in enumerate((wa_d, wi_d)):
                        for kc in range(2):
                            pc = WGP[(dd * 2 + gt) * 2 + kc]
                            items.append((pc.full[:, :], wsrc[dd, nb, kc * 128:(kc + 1) * 128, :], [], [pc[:, :]]))
                P.dma_group("pool", P.stream("wg"), items)
                for c2 in range(2):
                    ch = nb * 2 + c2
                    for tap in range(4):
                        P.ts("dve", DG[:, c2 * 4 + tap, :], IDB[:, :], COLC[:, tap * 10 + ch:tap * 10 + ch + 1], ALU.mult)
                P.dma("pool", P.stream("wr"), WR.full[:, :, :], win_d[:, DR + nb * 256:DR + (nb + 1) * 256].rearrange("(k p) n -> p k n", p=128), writes=[WR[:, :, :]])
                for c2_ in range(2):
                    P.memset("dve", VP[:, c2_, 0:XO], 0.0)
                    P.memset("dve", VP[:, c2_, XO + T:CO], 0.0)
                    P.memset("dve", VP[:, c2_, CO + TCX:VPW], 0.0)
                for (t0, n, v) in allb:
                    for c2 in range(2):
                        pb = P.ps_rot()
                        o = P.ps(pb, cols=n)
                        for k in range(KC):
                            P.mm(o, WR[:, k, c2 * 128:(c2 + 1) * 128], hsl(k, t0, n), start=(k == 0), stop=(k == KC - 1))
                        P.copy("act", VP[:, c2, vcol(t0):vcol(t0) + n], o)
                for (t0, n, v) in allb:
                    for c2 in range(2):
                        pb = P.ps_rot()
                        o = P.ps(pb, cols=n)
                        for tap in range(4):
                            c0 = vcol(t0) + tap - 2
                            P.mm(o, DG[:, c2 * 4 + tap, :], VP[:, c2, c0:c0 + n], start=(tap == 0), stop=(tap == 3))
                        ch = nb * 2 + c2
                        P.act(U[:, c2, t0:t0 + n], o, AF.Identity, bias=COLC[:, 40 + ch:40 + ch + 1], scale=1.0)
                P.dma("pool", P.stream("wr"), WR.full[:, :, :], win_d[:, nb * 256:(nb + 1) * 256].rearrange("(k p) n -> p k n", p=128), writes=[WR[:, :, :]])
                for (t0, n, v) in allb[1:]:
                    for c2 in range(2):
                        pb = P.ps_rot()
                        o = P.ps(pb, cols=n)
                        for k in range(KC):
                            P.mm(o, WR[:, k, c2 * 128:(c2 + 1) * 128], hsl(k, t0, n), start=(k == 0), stop=(k == KC - 1))
                        P.act(PROD[:, nb * 2 + c2, t0:t0 + n], o, AF.Gelu_apprx_tanh)
                for oc in range(2):
                    ch = nb * 2 + oc
                    for d in range(2):
                        col = d * 10 + ch
                        fwd = allb
                        bwd = [allb[0]] + allb[:0:-1]
                        order = fwd if d == 1 else bwd
                        for j, (t0, n, v) in enumerate(order):
                            if pending[0] is not None:
                                pending[0](j)
                            pb = P.ps_rot()
                            o = P.ps(pb, cols=n)
                            for kc in range(2):
                                P.mm(o, WGP[(d * 2 + 0) * 2 + kc][:, oc * 128:(oc + 1) * 128], U[:, kc, t0:t0 + n], start=(kc == 0), stop=(kc == 1))
                            P.act(TR[:, 0:n], o, AF.Tanh, bias=LCOL[:, 0, col:col + 1], scale=0.5)
                            P.act(aa_acc(d, t0, n), TR[:, 0:n], AF.Exp, bias=LCOL[:, 3, col:col + 1], scale=LCOL[:, 3, col:col + 1])
                            a2p = P.ps(P.ps_rot(), cols=n)
                            P.act(a2p, aa_acc(d, t0, n), AF.Square)
                            pb = P.ps_rot()
                            o = P.ps(pb, cols=n)
                            for kc in range(2):
                                P.mm(o, WGP[(d * 2 + 1) * 2 + kc][:, oc * 128:(oc + 1) * 128], U[:, kc, t0:t0 + n], start=(kc == 0), stop=(kc == 1))
                            P.act(TI[:, 0:n], o, AF.Tanh, bias=LCOL[:, 1, col:col + 1], scale=0.5)
                            P.ts("dve", S1[:, t0:t0 + n], a2p, -1.0, ALU.mult, 1.0, ALU.add)
                            P.stt(WB[d][:, t0:t0 + n], TI[:, 0:n], 1.0, U[:, oc, t0:t0 + n], ALU.add, ALU.mult)
                        pending[0] = None
                        P.act(S1[:, :], S1[:, :], AF.Sqrt)
                        P.stt(WB[d][:, :], S1[:, :], 0.5, WB[d][:, :], ALU.mult, ALU.mult)
                        pending[0] = make_scan(d, ch)
            for j in range(5):
                pending[0](j)
            pending[0] = None
            P.release(mkh)
            Y1 = P.alloc("Y1", [KC, 512], F32)
            P.dma("pool", s_w[0], WS[0].full[:, :, :], wout_d[0:1024, :].rearrange("(k p) n -> p k n", p=128), writes=[WS[0][:, :, :]])
            P.dma("pool", s_w[1], WS[1].full[:, 0:2, :], wout_d[1024:1280, :].rearrange("(k p) n -> p k n", p=128), writes=[WS[1][:, 0:2, :]])

            def wout(k, m):
                return WS[0][:, k, m * 128:(m + 1) * 128] if k < 8 else WS[1][:, k - 8, m * 128:(m + 1) * 128]
            for b in range(4):
                t0 = b * 512
                ss = proj_out_stats(wout, 10, lambda k: PROD[:, k, t0:t0 + 512], 512, Y1)
                resid_update(i, t0, 512, 0, 1, Y1, ss)
            P.release(mk1)
            mlp(1, [(b * 512, 512, 0) for b in range(4)], emit_out=True)

        mko = P.mark()
        out_state["OS"] = [P.alloc("OSt%d" % q, [D], F32) for q in range(2)]
        nout = TT // 128 if debug_out else T // 128
        for tb in range(nout):
            if tb not in out_state["done"]:
                out_tokens(tb)
        for sl in range(2):
            nc.sync.wait_ge(P.sem[s_st[sl]], P.cnt[s_st[sl]])
        P.release(mko)
    return nc


_CACHE = {}


def _inputs_for_core(b, inp, consts):
    f = lambda a: np.ascontiguousarray(a, dtype=np.float32)
    csm, cos2, sin2 = consts
    return {
        "x": f(inp["x"][b]),
        "c": f(inp["c"][b].reshape(8, 128)),
        "ctx": f(inp["ctx"][b]),
        "c_ctx": f(inp["c_ctx"].reshape(8, 128)),
        "ada_w": f(inp["ada_w"]),
        "ada_b": f(inp["ada_b"].reshape(96, 128)),
        "norm_g": f(inp["norm_g"].reshape(64, 128)),
        "mlp_w1": f(inp["mlp_w1"]),
        "mlp_w2": f(inp["mlp_w2"]),
        "attn_w_qkv": f(inp["attn_w_qkv"][0]),
        "attn_w_o": f(inp["attn_w_o"][0]),
        "attn_sink": f(inp["attn_sink"]),
        "lru_w_in": f(inp["lru_w_in"][0]),
        "lru_conv_w": f(inp["lru_conv_w"][0].reshape(40, 128)),
        "lru_conv_b": f(inp["lru_conv_b"][0].reshape(10, 128)),
        "lru_w_a": f(inp["lru_w_a"][0]),
        "lru_b_a": f(inp["lru_b_a"][0].reshape(20, 128)),
        "lru_w_i": f(inp["lru_w_i"][0]),
        "lru_b_i": f(inp["lru_b_i"][0].reshape(20, 128)),
        "lru_lam": f(inp["lru_lam"][0].reshape(20, 128)),
        "lru_w_out": f(inp["lru_w_out"][0]),
        "k_small": csm, "k_cos": cos2, "k_sin": sin2,
    }


def kernel(**inputs):
    inp = {k: np.asarray(v) for k, v in inputs.items()}
    consts = host_consts()
    if "nc" not in _CACHE:
        _CACHE["nc"] = build((0, 1))
    nc = _CACHE["nc"]
    in_maps = [_inputs_for_core(b, inp, consts) for b in range(8)]
    res = run_bass_kernel_spmd(nc, in_maps, core_ids=list(range(8)))
    out = np.stack([np.asarray(r["out"], dtype=np.float32) for r in res.results], axis=0)
    return out
```
